# Optimizing a Trainium2 kernel written in Bass

```python
import jax, jax.numpy as jnp
from jax import lax
import numpy as np

D_MODEL = 1024
BATCH = 8
SEQ = 2048
DEPTH = 4
DEC_BATCH = 128
DEC_SEQ = 4
PAST_LEN = 16384
PAGE_SIZE = 128

D_A = D_MODEL
D_B = D_MODEL
D_C = D_MODEL
CONV_A_WIDTH = 31
CONV_C_WIDTH = 3
CHUNK = 128
N_GROUPS_B = 8
D_GROUP_B = D_B // N_GROUPS_B
N_BRANCH = 3
RMS_EPS = 1e-6
LN_EPS = 1e-5
WIDTHS = (D_A, D_A, D_A, D_B, D_B, D_B, D_C, D_C, D_C, D_C, N_BRANCH * D_MODEL)
D_IN = sum(WIDTHS)

kernel_name = "hybrid_conformer_gmlp_shortconv_step"


def rms_norm(x, g):
    xf = x.astype(jnp.float32)
    y = xf * lax.rsqrt(jnp.mean(xf * xf, axis=-1, keepdims=True) + RMS_EPS)
    return (y * g.astype(jnp.float32)).astype(x.dtype)


def layer_norm(x, g, b):
    xf = x.astype(jnp.float32)
    mu = jnp.mean(xf, axis=-1, keepdims=True)
    var = jnp.mean(jnp.square(xf - mu), axis=-1, keepdims=True)
    y = (xf - mu) * lax.rsqrt(var + LN_EPS)
    return (y * g.astype(jnp.float32) + b.astype(jnp.float32)).astype(x.dtype)


def causal_dwconv(x, past, w):
    k = w.shape[0]
    xp = jnp.concatenate([past.astype(x.dtype), x], axis=1)
    y = lax.conv_general_dilated(
        xp, w[:, None, :].astype(x.dtype), window_strides=(1,), padding='VALID',
        dimension_numbers=('NWC', 'WIO', 'NWC'), feature_group_count=x.shape[-1])
    return y, xp[:, -(k - 1):]


def spatial_gate(u, v, w_s, b_s):
    bsz, seq_len, _ = v.shape
    mask = jnp.tril(jnp.ones((CHUNK, CHUNK), dtype=bool))
    w = jnp.where(mask, w_s, jnp.zeros_like(w_s))
    n_full, rem = seq_len // CHUNK, seq_len % CHUNK
    parts = []
    if n_full:
        vf = v[:, :n_full * CHUNK].reshape(bsz, n_full, CHUNK, N_GROUPS_B, D_GROUP_B)
        s = jnp.einsum('gij,bcjgd->bcigd', w, vf) + b_s.T[None, None, :, :, None]
        parts.append(s.reshape(bsz, n_full * CHUNK, D_B))
    if rem:
        vr = v[:, n_full * CHUNK:].reshape(bsz, rem, N_GROUPS_B, D_GROUP_B)
        s = jnp.einsum('gij,bjgd->bigd', w[:, :rem, :rem], vr) + b_s[:, :rem].T[None, :, :, None]
        parts.append(s.reshape(bsz, rem, D_B))
    s = parts[0] if len(parts) == 1 else jnp.concatenate(parts, axis=1)
    return u * s.astype(u.dtype)


def mixer_layer(x, conv_a_past, conv_c_past, lw):
    (norm_g, w_in, b_gate, w_conv_a, b_conv_a, ln_a_g, ln_a_b, w_proj_a,
     ln_b_g, ln_b_b, w_s, b_s, w_proj_b, w_conv_c, w_proj_c, w_out) = lw
    xn = rms_norm(x, norm_g)
    h = jnp.einsum('bld,de->ble', xn, w_in)
    a_val, a_gate, z_a, u, v, z_b, gb, gc, hc, z_c, gates = jnp.split(
        h, [int(i) for i in np.cumsum(WIDTHS)[:-1]], axis=-1)
    a = a_val * jax.nn.sigmoid(a_gate)
    a_conv, new_a = causal_dwconv(a, conv_a_past, w_conv_a)
    a_out = jax.nn.silu(layer_norm(a_conv + b_conv_a, ln_a_g, ln_a_b)) * jax.nn.silu(z_a)
    p_a = jnp.einsum('blc,cd->bld', a_out, w_proj_a)
    vn = layer_norm(v, ln_b_g, ln_b_b)
    s = spatial_gate(u, vn, w_s, b_s)
    p_b = jnp.einsum('blc,cd->bld', s * jax.nn.silu(z_b), w_proj_b)
    ci = gc * hc
    c_conv, new_c = causal_dwconv(ci, conv_c_past, w_conv_c)
    p_c = jnp.einsum('blc,cd->bld', gb * c_conv * jax.nn.silu(z_c), w_proj_c)
    g = jax.nn.sigmoid(gates + b_gate).reshape(gates.shape[:-1] + (N_BRANCH, D_MODEL))
    m = g[..., 0, :] * p_a + g[..., 1, :] * p_b + g[..., 2, :] * p_c
    y = x + jnp.einsum('bld,de->ble', m, w_out)
    return y, new_a, new_c, vn


def setup_inputs(seed: int = 0) -> dict:
    key = jax.random.key(seed)
    ks = jax.random.split(key, 24)
    f32 = jnp.float32
    nrm = lambda k, shape, scale: jax.random.normal(k, shape, f32) * scale
    return {
        "x_prompt": nrm(ks[0], (BATCH, SEQ, D_MODEL), 1.0),
        "x_sample": nrm(ks[1], (DEC_BATCH, DEC_SEQ, D_MODEL), 1.0),
        "state_conv_a": nrm(ks[2], (DEPTH, DEC_BATCH, CONV_A_WIDTH - 1, D_A), 0.5),
        "state_conv_c": nrm(ks[3], (DEPTH, DEC_BATCH, CONV_C_WIDTH - 1, D_C), 0.5),
        "norm_g": 1.0 + nrm(ks[4], (DEPTH, D_MODEL), 0.02),
        "w_in": nrm(ks[5], (DEPTH, D_MODEL, D_IN), D_MODEL ** -0.5),
        "b_gate": nrm(ks[6], (DEPTH, N_BRANCH * D_MODEL), 0.02),
        "w_conv_a": nrm(ks[7], (DEPTH, CONV_A_WIDTH, D_A), CONV_A_WIDTH ** -0.5),
        "b_conv_a": nrm(ks[8], (DEPTH, D_A), 0.02),
        "ln_a_g": 1.0 + nrm(ks[9], (DEPTH, D_A), 0.02),
        "ln_a_b": nrm(ks[10], (DEPTH, D_A), 0.02),
        "w_proj_a": nrm(ks[11], (DEPTH, D_A, D_MODEL), D_A ** -0.5),
        "ln_b_g": 1.0 + nrm(ks[12], (DEPTH, D_B), 0.02),
        "ln_b_b": nrm(ks[13], (DEPTH, D_B), 0.02),
        "w_s": nrm(ks[14], (DEPTH, N_GROUPS_B, CHUNK, CHUNK), CHUNK ** -0.5),
        "b_s": 1.0 + nrm(ks[15], (DEPTH, N_GROUPS_B, CHUNK), 0.1),
        "w_proj_b": nrm(ks[16], (DEPTH, D_B, D_MODEL), D_B ** -0.5),
        "w_conv_c": nrm(ks[17], (DEPTH, CONV_C_WIDTH, D_C), CONV_C_WIDTH ** -0.5),
        "w_proj_c": nrm(ks[18], (DEPTH, D_C, D_MODEL), D_C ** -0.5),
        "w_out": nrm(ks[19], (DEPTH, D_MODEL, D_MODEL), 0.5 * D_MODEL ** -0.5),
        "final_g": 1.0 + nrm(ks[20], (D_MODEL,), 0.02),
    }


def reference(x_prompt, x_sample, state_conv_a, state_conv_c, norm_g, w_in, b_gate,
              w_conv_a, b_conv_a, ln_a_g, ln_a_b, w_proj_a, ln_b_g, ln_b_b, w_s, b_s,
              w_proj_b, w_conv_c, w_proj_c, w_out, final_g):
    xp, xs = x_prompt, x_sample
    bp = x_prompt.shape[0]
    a_p, a_s, c_p, c_s, v_s = [], [], [], [], []
    for l in range(DEPTH):
        lw = (norm_g[l], w_in[l], b_gate[l], w_conv_a[l], b_conv_a[l], ln_a_g[l], ln_a_b[l],
              w_proj_a[l], ln_b_g[l], ln_b_b[l], w_s[l], b_s[l], w_proj_b[l], w_conv_c[l],
              w_proj_c[l], w_out[l])
        zero_a = jnp.zeros((bp, CONV_A_WIDTH - 1, D_A), xp.dtype)
        zero_c = jnp.zeros((bp, CONV_C_WIDTH - 1, D_C), xp.dtype)
        xp, na_p, nc_p, _ = mixer_layer(xp, zero_a, zero_c, lw)
        xs, na_s, nc_s, vn_s = mixer_layer(xs, state_conv_a[l], state_conv_c[l], lw)
        a_p.append(na_p); a_s.append(na_s); c_p.append(nc_p); c_s.append(nc_s); v_s.append(vn_s)
    y_prompt = rms_norm(xp, final_g)
    y_sample = rms_norm(xs, final_g)
    return (y_prompt, y_sample, jnp.stack(a_p), jnp.stack(a_s), jnp.stack(c_p), jnp.stack(c_s), jnp.stack(v_s))
```

```python
import numpy as np
from contextlib import ExitStack
import concourse.bass as bass
import concourse.mybir as mybir
from concourse.bass_utils import run_bass_kernel_spmd

F32 = mybir.dt.float32
BF16 = mybir.dt.bfloat16
AF = mybir.ActivationFunctionType
ALU = mybir.AluOpType

D = 1024
NCH = 8
DEPTH = 4
SEQ = 2048
NS = 16
ST = 4
NSTOK = NS * ST
KA = 31
KC = 3
D_IN = 13 * D
RMS_EPS = 1e-6
LN_EPS = 1e-5

R_NORMG = 0
R_BG = 1
R_WCA = 4
R_BCA = 35
R_LNAG = 36
R_LNAB = 37
R_LNBG = 38
R_LNBB = 39
R_WCC = 40
R_FING = 43
NR = 44

C_AVAL, C_AGATE, C_ZA, C_U, C_V, C_ZB, C_GB, C_GC, C_HC, C_ZC, C_G0, C_G1, C_G2 = [i * D for i in range(13)]

NRING = 4
ZA_MUL_ENG = "dve"


class Eng:
    def __init__(self, name, sem):
        self.name = name
        self.sem = sem
        self.count = 0
        self.ops = []
        self.known = {}


class Prog:
    def __init__(self, nc, es):
        self.nc = nc
        self.es = es
        self.sems = {}
        self.eng = {}
        for n in ("pe", "act", "dve", "pool", "sp"):
            self.eng[n] = Eng(n, self.sem("e_" + n))
        self.res = {}
        self.dma_cnt = {}
        self.nbank = 0
        self.rot = {}

    def sem(self, name):
        if name not in self.sems:
            self.sems[name] = self.es.enter_context(self.nc.semaphore(name))
        return name

    def _deps(self, reads, writes):
        deps = []
        for k in reads:
            st = self.res.get(k)
            if st is not None and st[0] is not None:
                deps.append(st[0])
            if st is not None and k[0] == "bk":
                deps.extend(st[1])
        for k in writes:
            st = self.res.get(k)
            if st is not None:
                if st[0] is not None:
                    deps.append(st[0])
                deps.extend(st[1])
        return deps

    def _commit(self, ev, reads, writes):
        for k in reads:
            st = self.res.setdefault(k, [None, []])
            st[1].append(ev)
            if len(st[1]) > 64:
                best = {}
                for (s, v) in st[1]:
                    if best.get(s, 0) < v:
                        best[s] = v
                st[1] = list(best.items())
        for k in writes:
            self.res[k] = [ev, []]

    def _waits(self, E, deps):
        need = {}
        for (s, v) in deps:
            if E.name == "pe" and s == E.sem:
                continue
            if E.known.get(s, 0) >= v:
                continue
            if need.get(s, 0) < v:
                need[s] = v
        for s, v in need.items():
            E.known[s] = v
        return list(need.items())

    def op(self, eng, fns, reads=(), writes=()):
        if not isinstance(fns, (list, tuple)):
            fns = [fns]
        E = self.eng[eng]
        waits = self._waits(E, self._deps(reads, writes))
        E.count += 1
        ev = (E.sem, E.count)
        E.ops.append((waits, list(fns), (E.sem, 1)))
        self._commit(ev, reads, writes)
        return ev

    def dma(self, q, semkey, out, in_, reads=(), writes=(), **kw):
        E = self.eng[q]
        s = self.sem("d_" + semkey)
        waits = self._waits(E, self._deps(reads, writes))
        self.dma_cnt[s] = self.dma_cnt.get(s, 0) + 1
        ev = (s, 16 * self.dma_cnt[s])
        E.ops.append((waits, [lambda e, out=out, in_=in_, kw=kw: e.dma_start(out=out, in_=in_, **kw)], (s, 16)))
        self._commit(ev, reads, writes)
        return ev

    def wait_all(self, eng, evs):
        E = self.eng[eng]
        waits = self._waits(E, evs)
        E.ops.append((waits, [], None))

    def bank(self):
        i = self.nbank % 8
        self.nbank += 1
        return i

    def rotate(self, name, n):
        i = self.rot.get(name, 0)
        self.rot[name] = i + 1
        return i % n

    def emit(self):
        nc = self.nc
        block = self.es.enter_context(nc.Block())
        sems = self.sems

        def run(E):
            def body(e):
                for waits, fns, inc in E.ops:
                    for (s, v) in waits:
                        e.wait_ge(sems[s], v)
                    last = None
                    for f in fns:
                        last = f(e)
                    if inc is not None and last is not None:
                        last.then_inc(sems[inc[0]], inc[1])
            return body

        block.tensor(run(self.eng["pe"]))
        block.scalar(run(self.eng["act"]))
        block.vector(run(self.eng["dve"]))
        block.gpsimd(run(self.eng["pool"]))
        block.sync(run(self.eng["sp"]))


def _consts():
    ident = np.eye(128, dtype=np.float32)
    tril = np.tril(np.ones((128, 128), dtype=np.float32))
    rep = np.zeros((4, 64), dtype=np.float32)
    for q in range(64):
        rep[q % 4, q] = 1.0
    bmask = np.zeros((64, 64), dtype=np.float32)
    for p in range(64):
        for q in range(64):
            if p // 4 == q // 4:
                bmask[p, q] = 1.0
    return ident, tril, rep, bmask


def build(depth=DEPTH, ngroups=2):
    nc = bass.Bass("TRN2", target_bir_lowering=False)
    dt = nc.dram_tensor
    xp_d = dt("xp", [SEQ, D], F32, kind="ExternalInput")
    xs_d = dt("xs", [NSTOK, D], F32, kind="ExternalInput")
    sca_d = dt("sca", [DEPTH, NS, KA - 1, D], F32, kind="ExternalInput")
    scc_d = dt("scc", [DEPTH, NS, KC - 1, D], F32, kind="ExternalInput")
    win_d = dt("w_in", [DEPTH, D, D_IN], F32, kind="ExternalInput")
    wpa_d = dt("w_proj_a", [DEPTH, D, D], F32, kind="ExternalInput")
    wpb_d = dt("w_proj_b", [DEPTH, D, D], F32, kind="ExternalInput")
    wpc_d = dt("w_proj_c", [DEPTH, D, D], F32, kind="ExternalInput")
    wo_d = dt("w_out", [DEPTH, D, D], F32, kind="ExternalInput")
    vecs_d = dt("vecs", [DEPTH, NR, D], F32, kind="ExternalInput")
    ws_d = dt("w_s", [DEPTH, NCH, 128, 128], F32, kind="ExternalInput")
    bs_d = dt("b_s", [DEPTH, NCH * 128], F32, kind="ExternalInput")
    wst_d = dt("wst", [DEPTH, 128, 256], F32, kind="ExternalInput")
    yp_d = dt("yp", [SEQ, D], F32, kind="ExternalOutput")
    ys_d = dt("ys", [NSTOK, D], F32, kind="ExternalOutput")
    nap_d = dt("nap", [DEPTH, KA - 1, D], F32, kind="ExternalOutput")
    nas_d = dt("nas", [DEPTH, NS, KA - 1, D], F32, kind="ExternalOutput")
    ncp_d = dt("ncp", [DEPTH, KC - 1, D], F32, kind="ExternalOutput")
    ncs_d = dt("ncs", [DEPTH, NS * (KC - 1), D], F32, kind="ExternalOutput")
    nvs_d = dt("nvs", [DEPTH, NSTOK, D], F32, kind="ExternalOutput")
    c_ident, c_tril, c_rep, c_bmask = _consts()
    ident_d = nc.inline_tensor(c_ident, "c_ident")
    tril_d = nc.inline_tensor(c_tril, "c_tril")
    rep_d = nc.inline_tensor(c_rep, "c_rep")
    bmask_d = nc.inline_tensor(c_bmask, "c_bmask")
    c_identst = np.tile(np.eye(32, dtype=np.float32), (4, 1))
    identst_d = nc.inline_tensor(c_identst, "c_identst")

    es = ExitStack()
    P = Prog(nc, es)

    def sb(name, shape, dty):
        return es.enter_context(nc.sbuf_tensor(name, shape, dty))

    TG = 1088
    xT = sb("xT", [128, NCH, TG], F32)
    xn = sb("xn", [128, NCH, TG], BF16)
    mbuf = sb("mbuf", [128, NCH * TG], BF16)
    S2 = sb("S2", [128, NCH, TG], BF16)
    NAB = 2
    ab = sb("ab", [128, NAB, 1056], BF16)
    stk = sb("stk", [128, 2, 4, 1056], BF16)
    sdiag = sb("sdiag", [128, 2, 32, 32], BF16)
    wst = sb("wst_sb", [128, 256], BF16)
    ones32 = sb("ones32", [128, 128], F32)
    identst = sb("identst", [128, 32], BF16)
    identstf = sb("identstf", [128, 32], F32)
    abs_ = sb("abs", [128, NCH, NS * 34], BF16)
    cis_ = sb("cis", [128, NCH, NS * 6], BF16)
    stats = sb("stats", [128, 2, TG], F32)
    tmpF = [sb("tmpF%d" % i, [128, 512], F32) for i in range(5)]
    tmpB = [sb("tmpB%d" % i, [128, 512], BF16) for i in range(3)]
    wring = [sb("wring%d" % i, [128, NCH, 512], BF16) for i in range(NRING)]
    colv = sb("colv", [128, NCH, NR], F32)
    diagC = sb("diagC", [128, KC, 128], BF16)
    WmTb = sb("WmTb", [128, NCH, 128], BF16)
    BiasB = sb("BiasB", [128, NCH, 128], F32)
    BD = sb("BD", [64, NCH, 64], BF16)
    ident = sb("ident", [128, 128], F32)
    identb = sb("identb", [128, 128], BF16)
    onesb = sb("onesb", [128, 128], BF16)
    tril = sb("tril", [128, 128], F32)
    rep = sb("rep", [4, 64], F32)
    bmask = sb("bmask", [64, 64], F32)
    epsc = sb("epsc", [128, 2], F32)
    stg = [sb("stg%d" % i, [128, D], F32) for i in range(3)]
    vn = [sb("vn%d" % i, [128, D], BF16) for i in range(2)]
    ahalo = sb("ahalo", [128, DEPTH, NCH, KA - 1], BF16)
    chalo = sb("chalo", [128, DEPTH, NCH, KC - 1], BF16)
    aslab = sb("aslab", [128, NCH, 64], F32)
    cslab = sb("cslab", [128, NCH, 32], F32)
    aslabp = sb("aslabp", [128, NCH, KA - 1], F32)
    cslabp = sb("cslabp", [128, NCH, KC - 1], F32)
    bstr = sb("bstr", [128, 3, 2, 6], F32)
    mvr = sb("mvr", [128, 3, 4], F32)
    banks = [es.enter_context(nc.psum_tensor("bank%d" % i, [128, 512], F32)) for i in range(8)]

    mb_f32 = mbuf[:, 0:4 * D * 2].bitcast(F32).rearrange("p (a c) -> p a c", a=4)
    mview = mbuf[:, :].rearrange("p (c t) -> p c t", c=NCH)
    WmTf = mb_f32[:, 0, :].rearrange("p (g i) -> p g i", g=NCH)
    MB_ALL = [("m", j, ti) for j in range(NCH) for ti in range(3)]

    def BK(i):
        return ("bk", i)

    def act(fn, reads, writes):
        return P.op("act", fn, reads, writes)

    def dve(fn, reads, writes):
        return P.op("dve", fn, reads, writes)

    def pool(fn, reads, writes):
        return P.op("pool", fn, reads, writes)

    def pe(fns, reads, writes):
        return P.op("pe", fns, reads, writes)

    def mm(out, lhsT, rhs, start, stop):
        return lambda e: e.matmul(out, lhsT=lhsT, rhs=rhs, start=start, stop=stop)

    def tp(out, in_, idn):
        return lambda e: e.transpose(out, in_, idn)

    def tf():
        i = P.rotate("tf", 5)
        return tmpF[i], ("tf", i)

    def tb_():
        i = P.rotate("tb", 3)
        return tmpB[i], ("tb", i)

    def sg():
        i = P.rotate("stg", 3)
        return stg[i], ("stg", i)

    out_evs = []

    P.dma("sp", "c0", ident[:], ident_d.ap(), writes=[("ident",)])
    P.dma("sp", "c1", tril[:], tril_d.ap(), writes=[("tril",)])
    P.dma("sp", "c2", rep[:], rep_d.ap(), writes=[("rep",)])
    P.dma("sp", "c3", bmask[:], bmask_d.ap(), writes=[("bmask",)])
    dve(lambda e: e.tensor_copy(out=identb[:], in_=ident[:]), [("ident",)], [("identb",)])
    P.dma("sp", "c4", identstf[:], identst_d.ap(), writes=[("identstf",)])
    dve(lambda e: e.tensor_copy(out=identst[:], in_=identstf[:]), [("identstf",)], [("identst",)])
    for sl in range(NAB):
        pool(lambda e, sl=sl: e.memset(ab[:, sl, 1054:1056], 0.0), [], [("ab", sl)])
    pool(lambda e: e.memset(onesb[:], 1.0), [], [("onesb",)])
    pool(lambda e: e.memset(ones32[:], 1.0), [], [("ones32",)])
    pool(lambda e: e.memset(epsc[:, 0:1], RMS_EPS), [], [("epsc",)])
    pool(lambda e: e.memset(epsc[:, 1:2], LN_EPS), [], [("epsc",)])

    groups = []
    g0 = [dict(off=0, n=512, kind="p", seq0=0), dict(off=512, n=512, kind="p", seq0=512),
          dict(off=1024, n=64, kind="s", seq0=0)]
    g1 = [dict(off=0, n=512, kind="p", seq0=1024), dict(off=512, n=512, kind="p", seq0=1536)]
    groups = [g0, g1][:ngroups]

    def wsrc(tname, l, segs):
        tens = {"in": win_d, "pa": wpa_d, "pb": wpb_d, "pc": wpc_d, "o": wo_d}[tname]
        outs = []
        pos = 0
        for (c0, ncol) in segs:
            src = tens.ap()[l, :, c0:c0 + ncol].rearrange("(k p) c -> p k c", p=128)
            outs.append((pos, ncol, src))
            pos += ncol
        return outs

    def pass_blocks(l):
        bl = []
        for c0 in (0, 2, 4, 6):
            bl.append(("in", l, [(C_AVAL + c0 * 128, 256), (C_AGATE + c0 * 128, 256)]))
        for jb in range(2):
            bl.append(("in", l, [(C_ZA + jb * 512, 512)]))
        for jb in range(2):
            bl.append(("in", l, [(C_G0 + jb * 512, 512)]))
            bl.append(("pa", l, [(jb * 512, 512)]))
        for jb in range(2):
            bl.append(("in", l, [(C_V + jb * 512, 512)]))
        for jb in range(2):
            bl.append(("in", l, [(C_ZB + jb * 512, 512)]))
            bl.append(("in", l, [(C_U + jb * 512, 512)]))
        for jb in range(2):
            bl.append(("in", l, [(C_G1 + jb * 512, 512)]))
            bl.append(("pb", l, [(jb * 512, 512)]))
        for c0 in (0, 2, 4, 6):
            bl.append(("in", l, [(C_GC + c0 * 128, 256), (C_HC + c0 * 128, 256)]))
        for jb in range(2):
            bl.append(("in", l, [(C_ZC + jb * 512, 512)]))
            bl.append(("in", l, [(C_GB + jb * 512, 512)]))
        for jb in range(2):
            bl.append(("in", l, [(C_G2 + jb * 512, 512)]))
            bl.append(("pc", l, [(jb * 512, 512)]))
        for jb in range(2):
            bl.append(("o", l, [(jb * 512, 512)]))
        return bl

    all_blocks = []
    for g in range(len(groups)):
        for l in range(depth):
            all_blocks.extend(pass_blocks(l))
    wstate = dict(issued=0, cur=0)

    def issue_weights(upto):
        while wstate["issued"] < min(upto, len(all_blocks)):
            i = wstate["issued"]
            tname, l, segs = all_blocks[i]
            slot = i % NRING
            for (pos, ncol, src) in wsrc(tname, l, segs):
                P.dma("pool", "w%d" % slot, wring[slot][:, :, pos:pos + ncol], src, writes=[("w", slot)] if pos == 0 else [])
                if pos != 0:
                    P.res[("w", slot)][0] = (P.sem("d_w%d" % slot), 16 * P.dma_cnt[P.sem("d_w%d" % slot)])
            wstate["issued"] += 1

    def next_block(first=True):
        if first:
            issue_weights(wstate["cur"] + NRING)
        i = wstate["cur"]
        wstate["cur"] += 1
        slot = i % NRING
        return wring[slot], ("w", slot)

    def load_group(gi, tiles):
        for ti, t in enumerate(tiles):
            if t["kind"] == "p":
                for tb in range(4):
                    P.dma("sp", "xl%d" % tb, mb_f32[:, tb, :], xp_d.ap()[t["seq0"] + tb * 128:t["seq0"] + (tb + 1) * 128, :],
                          writes=[("mbstg", tb)] + (MB_ALL if tb == 0 else []))
                for c in range(NCH):
                    b = P.bank()
                    pe([tp(banks[b][:, tb * 128:(tb + 1) * 128], mb_f32[:, tb, c * 128:(c + 1) * 128], ident[:]) for tb in range(4)],
                       [("mbstg", tb) for tb in range(4)] + [("ident",)], [BK(b)])
                    act(lambda e, b=b, c=c, t=t: e.activation(out=xT[:, c, t["off"]:t["off"] + 512], in_=banks[b][:, :], func=AF.Copy),
                        [BK(b)], [("xT", c, ti)])
            else:
                s_, sk = sg()
                P.dma("sp", sk[0] + str(sk[1]), s_[0:NSTOK, :], xs_d.ap(), writes=[sk])
                b = P.bank()
                pe([tp(banks[b][:, c * 64:(c + 1) * 64], s_[0:NSTOK, c * 128:(c + 1) * 128], ident[0:NSTOK, 0:NSTOK]) for c in range(NCH)],
                   [sk, ("ident",)], [BK(b)])
                act(lambda e, b=b, t=t: e.activation(out=xT[:, :, t["off"]:t["off"] + 64],
                                                      in_=banks[b][:, :].rearrange("p (c t) -> p c t", c=NCH), func=AF.Copy),
                    [BK(b)], [("xT", c, ti) for c in range(NCH)])

    def rms_stats(tiles):
        for ti, t in enumerate(tiles):
            rms_tile(ti, t)

    def rms_tile(ti, t):
        if True:
            off, n = t["off"], t["n"]
            b = P.bank()
            for c in range(NCH):
                q, qk = tb_()
                act(lambda e, q=q, c=c, off=off, n=n: e.activation(out=q[:, 0:n], in_=xT[:, c, off:off + n], func=AF.Square),
                    [("xT", c, ti)], [qk])
                pe(mm(banks[b][:, 0:n], onesb[:, :], q[:, 0:n], c == 0, c == NCH - 1), [qk, ("onesb",)], [BK(b)])
            act(lambda e, b=b, off=off, n=n: e.activation(out=stats[:, 0, off:off + n], in_=banks[b][:, 0:n], func=AF.Sqrt,
                                                        bias=epsc[:, 0:1], scale=1.0 / D),
                [BK(b), ("epsc",)], [("st", 0, ti)])
            dve(lambda e, off=off, n=n: e.reciprocal(out=stats[:, 0, off:off + n], in_=stats[:, 0, off:off + n]),
                [("st", 0, ti)], [("st", 0, ti)])

    def setup_pass(gi, l, tiles):
        has_s = any(t["kind"] == "s" for t in tiles)
        s_, sk = sg()
        P.dma("sp", sk[0] + str(sk[1]), s_[0:NR, :], vecs_d.ap()[l, :, :], writes=[sk])
        b = P.bank()
        pe([tp(banks[b][:, c * NR:(c + 1) * NR], s_[0:NR, c * 128:(c + 1) * 128], ident[0:NR, 0:NR]) for c in range(NCH)],
           [sk, ("ident",)], [BK(b)])
        dve(lambda e, b=b: e.tensor_copy(out=colv[:, :, :], in_=banks[b][:, 0:NCH * NR].rearrange("p (c r) -> p c r", c=NCH)),
            [BK(b)], [("colv",)])
        P.dma("pool", "wst", wst[:, :], wst_d.ap()[l, :, :], writes=[("wst",)])
        phase_mark()
        s_, sk = sg()
        sv = s_[:, :].rearrange("p (g j) -> p g j", g=NCH)
        P.dma("sp", sk[0] + str(sk[1]), sv, ws_d.ap()[l].rearrange("g i j -> i g j"), writes=[sk])
        dve(lambda e, sv=sv: e.tensor_tensor(out=sv, in0=sv, in1=tril[:, :].unsqueeze(1).broadcast_to([128, NCH, 128]), op=ALU.mult),
            [sk, ("tril",)], [sk])
        for h in range(2):
            b = P.bank()
            pe([tp(banks[b][:, gg * 128:(gg + 1) * 128], sv[:, h * 4 + gg, :], ident[:]) for gg in range(4)],
               [sk, ("ident",)], [BK(b)])
            dve(lambda e, b=b, h=h: e.tensor_copy(out=WmTf[:, h * 4:(h + 1) * 4, :], in_=banks[b][:, :].rearrange("p (g i) -> p g i", g=4)),
                [BK(b)], [("WmTf", h)] + ((MB_ALL + [("mbstg", 0)]) if h == 0 else []))
            act(lambda e, b=b, h=h: e.activation(out=WmTb[:, h * 4:(h + 1) * 4, :], in_=banks[b][:, :].rearrange("p (g i) -> p g i", g=4), func=AF.Copy),
                [BK(b)], [("WmTb", h)])
        phase_mark()
        if has_s:
            b = P.bank()
            pe([mm(banks[b][0:4, gg * 64:(gg + 1) * 64], sv[0:4, gg, 0:4], rep[0:4, :], True, True) for gg in range(NCH)],
               [sk, ("rep",)], [BK(b)])
            Zs, zsk = tf()
            dve(lambda e, b=b, Zs=Zs: e.tensor_copy(out=Zs[0:4, :], in_=banks[b][0:4, :]), [BK(b)], [zsk])
            b = P.bank()
            pe(mm(banks[b][0:64, :], rep[0:4, :], Zs[0:4, :], True, True), [zsk, ("rep",)], [BK(b)])
            dve(lambda e, b=b: e.tensor_tensor(out=BD[:, :, :], in0=banks[b][0:64, :].rearrange("p (g q) -> p g q", g=NCH),
                                               in1=bmask[:, :].unsqueeze(1).broadcast_to([64, NCH, 64]), op=ALU.mult),
                [BK(b), ("bmask",)], [("BD",)])
        phase_mark()
        s2_, sk2 = sg()
        P.dma("sp", sk2[0] + str(sk2[1]), s2_[:, :], vecs_d.ap()[l, R_LNBB, :].partition_broadcast(128), writes=[sk2])
        bsrow, bsk = sg()
        P.dma("sp", bsk[0] + str(bsk[1]), bsrow[:, :], bs_d.ap()[l, :].partition_broadcast(128), writes=[bsk])
        for h in range(2):
            b = P.bank()
            fns = []
            for gg in range(4):
                g_ = h * 4 + gg
                fns.append(mm(banks[b][:, gg * 128:(gg + 1) * 128], s2_[:, g_ * 128:(g_ + 1) * 128], WmTf[:, g_, :], True, True))
            pe(fns, [sk2, ("WmTf", h)], [BK(b)])
            dve(lambda e, b=b, h=h, bsrow=bsrow: e.tensor_tensor(out=BiasB[:, h * 4:(h + 1) * 4, :], in0=banks[b][:, :].rearrange("p (g i) -> p g i", g=4),
                                                              in1=bsrow[:, h * 512:(h + 1) * 512].rearrange("p (g i) -> p g i", g=4), op=ALU.add),
                [BK(b), bsk], [("BiasB", h)])
        phase_mark()
        if has_s:
            rows = NS * (KA - 1)
            srcA = sca_d.ap()[l].rearrange("s r c -> (s r) c")
            for blk in range(4):
                r0 = blk * 128
                nr_ = min(128, rows - r0)
                s_, sk = sg()
                P.dma("sp", sk[0] + str(sk[1]), s_[0:nr_, :], srcA[r0:r0 + nr_, :], writes=[sk])
                for s_i in range(r0 // 30, (r0 + nr_ - 1) // 30 + 1):
                    ra = max(r0, s_i * 30 + ST)
                    rb = min(r0 + nr_, (s_i + 1) * 30)
                    if ra < rb:
                        out_evs.append(P.dma("sp", "o_" + sk[0] + str(sk[1]), nas_d.ap()[l, s_i, ra - s_i * 30 - ST:rb - s_i * 30 - ST, :],
                                             s_[ra - r0:rb - r0, :], reads=[sk]))
                for h in range(2):
                    b = P.bank()
                    pe([tp(banks[b][:, cc * 128:cc * 128 + nr_], s_[0:nr_, (h * 4 + cc) * 128:(h * 4 + cc + 1) * 128], ident[0:nr_, 0:nr_]) for cc in range(4)],
                       [sk, ("ident",)], [BK(b)])
                    s_lo = r0 // 30
                    s_hi = (r0 + nr_ - 1) // 30
                    for s_i in range(s_lo, s_hi + 1):
                        ra = max(r0, s_i * 30)
                        rb = min(r0 + nr_, (s_i + 1) * 30)
                        act(lambda e, b=b, h=h, s_i=s_i, ra=ra, rb=rb, r0=r0: e.activation(
                            out=abs_[:, h * 4:(h + 1) * 4, s_i * 34 + (ra - s_i * 30):s_i * 34 + (rb - s_i * 30)],
                            in_=banks[b][:, :].rearrange("p (c t) -> p c t", c=4)[:, :, ra - r0:rb - r0], func=AF.Copy),
                            [BK(b)], [("abs", c) for c in range(h * 4, h * 4 + 4)])
            phase_mark()
            s_, sk = sg()
            P.dma("sp", sk[0] + str(sk[1]), s_[0:NS * 2, :], scc_d.ap()[l].rearrange("s r c -> (s r) c"), writes=[sk])
            b = P.bank()
            pe([tp(banks[b][:, c * 32:(c + 1) * 32], s_[0:32, c * 128:(c + 1) * 128], ident[0:32, 0:32]) for c in range(NCH)],
               [sk, ("ident",)], [BK(b)])
            act(lambda e, b=b: e.activation(out=cis_[:, :, :].rearrange("p c (s r) -> p c s r", s=NS)[:, :, :, 0:2],
                                            in_=banks[b][:, 0:256].rearrange("p (c s r) -> p c s r", c=NCH, s=NS), func=AF.Copy),
                [BK(b)], [("cis", c) for c in range(NCH)])

    def grp(W, wk, col0, ti, t, srcbuf=None, srckey="xn"):
        off, n = t["off"], t["n"]
        b = P.bank()
        src = xn if srcbuf is None else srcbuf
        pe([mm(banks[b][:, 0:n], W[:, k, col0:col0 + 128], src[:, k, off:off + n], k == 0, k == NCH - 1) for k in range(NCH)],
           [wk] + [(srckey, k, ti) for k in range(NCH)], [BK(b)])
        return b

    def run_pass(gi, l, tiles):
        last_group = (gi == 1)
        setup_pass(gi, l, tiles)
        phase_mark()
        phase_mark()
        for ti, t in enumerate(tiles):
            off, n = t["off"], t["n"]
            for c in range(NCH):
                dve(lambda e, c=c, off=off, n=n: e.scalar_tensor_tensor(
                    out=xn[:, c, off:off + n], in0=xT[:, c, off:off + n], scalar=colv[:, c, R_NORMG:R_NORMG + 1],
                    in1=stats[:, 0, off:off + n], op0=ALU.mult, op1=ALU.mult),
                    [("xT", c, ti), ("colv",), ("st", 0, ti)], [("xn", c, ti)])

        def a_glu(c, W, wk, cc):
            slot = P.rotate("ab", NAB)
            abk = ("ab", slot)
            if gi == 0:
                pool(lambda e, slot=slot: e.memset(ab[:, slot, 0:KA - 1], 0.0), [], [abk])
            else:
                pool(lambda e, slot=slot, c=c: e.tensor_copy(out=ab[:, slot, 0:KA - 1], in_=ahalo[:, l, c, :]), [("ahalo", l, c)], [abk])
            for ti, t in enumerate(tiles):
                off, n = t["off"], t["n"]
                bg = grp(W, wk, 256 + cc * 128, ti, t)
                sgm, sgk = tf()
                act(lambda e, bg=bg, sgm=sgm, n=n: e.activation(out=sgm[:, 0:n], in_=banks[bg][:, 0:n], func=AF.Sigmoid), [BK(bg)], [sgk])
                bv = grp(W, wk, cc * 128, ti, t)
                if t["kind"] == "p":
                    dve(lambda e, bv=bv, sgm=sgm, slot=slot, off=off, n=n: e.tensor_tensor(
                        out=ab[:, slot, KA - 1 + off:KA - 1 + off + n], in0=banks[bv][:, 0:n], in1=sgm[:, 0:n], op=ALU.mult),
                        [BK(bv), sgk], [abk])
                    if last_group and ti == len(tiles) - 1:
                        dve(lambda e, bv=bv, sgm=sgm, c=c, n=n: e.tensor_tensor(
                            out=aslabp[:, c, 0:KA - 1], in0=banks[bv][:, n - (KA - 1):n], in1=sgm[:, n - (KA - 1):n], op=ALU.mult),
                            [BK(bv), sgk], [("aslabp", c)])
                else:
                    dve(lambda e, bv=bv, sgm=sgm, c=c: e.tensor_tensor(
                        out=abs_[:, c, :].rearrange("p (s r) -> p s r", s=NS)[:, :, KA - 1:KA - 1 + ST],
                        in0=banks[bv][:, 0:NSTOK].rearrange("p (s t) -> p s t", s=NS),
                        in1=sgm[:, 0:NSTOK].rearrange("p (s t) -> p s t", s=NS), op=ALU.mult),
                        [BK(bv), sgk], [("abs", c)])
                    dve(lambda e, bv=bv, sgm=sgm, c=c: e.tensor_tensor(
                        out=aslab[:, c, 0:NSTOK], in0=banks[bv][:, 0:NSTOK], in1=sgm[:, 0:NSTOK], op=ALU.mult),
                        [BK(bv), sgk], [("aslab", c)])
            if gi == 0 and len(groups) > 1:
                pool(lambda e, slot=slot, c=c: e.tensor_copy(out=ahalo[:, l, c, :], in_=ab[:, slot, 1024:1024 + KA - 1]), [abk], [("ahalo", l, c)])
            return slot

        def a_diag(c, slot):
            sbuf_i = c % 2
            for j in range(4):
                for r in range(4):
                    P.dma("sp", "rs%d" % (sbuf_i * 16 + j * 4 + r), stk[r * 32:(r + 1) * 32, sbuf_i, j, 0:1052], ab[j * 32:(j + 1) * 32, slot, r:r + 1052],
                          reads=[("ab", slot)], writes=[("stk", sbuf_i, j, r)])
            dve(lambda e, c=c: e.tensor_tensor(out=sdiag[:, c % 2, :, :], in0=identst[:, :].unsqueeze(1).broadcast_to([128, 32, 32]),
                                               in1=wst[:, c * 32:(c + 1) * 32].unsqueeze(2).broadcast_to([128, 32, 32]), op=ALU.mult),
                [("identst",), ("wst",)], [("sdiag", c % 2)])

        def ln_acc(c, ti, off, n, sq, sqk):
            if c == 0:
                dve(lambda e, c=c, off=off, n=n: e.tensor_copy(out=stats[:, 0, off:off + n], in_=S2[:, c, off:off + n]), [("s2", c, ti)], [("st", 0, ti)])
                dve(lambda e, sq=sq, off=off, n=n: e.tensor_copy(out=stats[:, 1, off:off + n], in_=sq[:, 0:n]), [sqk], [("st", 1, ti)])
            else:
                dve(lambda e, c=c, off=off, n=n: e.tensor_tensor(out=stats[:, 0, off:off + n], in0=S2[:, c, off:off + n], in1=stats[:, 0, off:off + n], op=ALU.add),
                    [("s2", c, ti), ("st", 0, ti)], [("st", 0, ti)])
                dve(lambda e, sq=sq, off=off, n=n: e.tensor_tensor(out=stats[:, 1, off:off + n], in0=sq[:, 0:n], in1=stats[:, 1, off:off + n], op=ALU.add),
                    [sqk, ("st", 1, ti)], [("st", 1, ti)])

        def a_conv(c, slot):
            for ti, t in enumerate(tiles):
                off, n = t["off"], t["n"]
                if t["kind"] == "p":
                    b = P.bank()
                    pe([(lambda e, b=b, q=q, j=j, off=off, n=n: e.matmul(banks[b][32 * j:32 * j + 32, 0:n], lhsT=sdiag[:, c % 2, q * 4 + j, :],
                                                                         rhs=stk[:, c % 2, j, off + 4 * q:off + 4 * q + n], start=(q == 0), stop=(q == 7),
                                                                         tile_position=(0, 32 * j)))
                        for q in range(8) for j in range(4)],
                       [("sdiag", c % 2)] + [("stk", c % 2, j, r) for j in range(4) for r in range(4)], [BK(b)])
                    act(lambda e, b=b, c=c, off=off, n=n: e.activation(out=S2[:, c, off:off + n], in_=banks[b][:, 0:n], func=AF.Identity,
                                                                     bias=colv[:, c, R_BCA:R_BCA + 1], scale=1.0),
                        [BK(b), ("colv",)], [("s2", c, ti)])
                    sq, sqk = tf()
                    act(lambda e, b=b, c=c, sq=sq, n=n: e.activation(out=sq[:, 0:n], in_=banks[b][:, 0:n], func=AF.Square,
                                                                   bias=colv[:, c, R_BCA:R_BCA + 1], scale=1.0),
                        [BK(b), ("colv",)], [sqk])
                    ln_acc(c, ti, off, n, sq, sqk)
                else:
                    av = abs_[:, c, :].rearrange("p (s r) -> p s r", s=NS)
                    acc, acck = tf()
                    accv = acc[:, 0:NSTOK].rearrange("p (s t) -> p s t", s=NS)
                    fns = [lambda e, accv=accv, av=av, c=c: e.tensor_scalar(out=accv, in0=av[:, :, 0:ST], scalar1=colv[:, c, R_WCA:R_WCA + 1], scalar2=None, op0=ALU.mult)]
                    for k in range(1, KA):
                        fns.append(lambda e, accv=accv, av=av, c=c, k=k: e.scalar_tensor_tensor(
                            out=accv, in0=av[:, :, k:k + ST], scalar=colv[:, c, R_WCA + k:R_WCA + k + 1], in1=accv, op0=ALU.mult, op1=ALU.add))
                    for f_ in fns:
                        dve(f_, [("abs", c), ("colv",), acck], [acck])
                    act(lambda e, acc=acc, c=c, off=off, n=n: e.activation(out=S2[:, c, off:off + n], in_=acc[:, 0:n], func=AF.Identity,
                                                                         bias=colv[:, c, R_BCA:R_BCA + 1], scale=1.0),
                        [acck, ("colv",)], [("s2", c, ti)])
                    sq, sqk = tf()
                    act(lambda e, acc=acc, c=c, sq=sq, n=n: e.activation(out=sq[:, 0:n], in_=acc[:, 0:n], func=AF.Square,
                                                                     bias=colv[:, c, R_BCA:R_BCA + 1], scale=1.0),
                        [acck, ("colv",)], [sqk])
                    ln_acc(c, ti, off, n, sq, sqk)

        hist = []
        for c in range(NCH):
            if c % 2 == 0:
                W, wk = next_block()
            if len(hist) >= 1:
                a_diag(*hist[-1])
            slot = a_glu(c, W, wk, c % 2)
            if len(hist) >= 2:
                a_conv(*hist[-2])
            hist.append((c, slot))
        a_diag(*hist[-1])
        a_conv(*hist[-2])
        a_conv(*hist[-1])
        phase_mark()
        if last_group:
            emit_rows_out(aslabp, KA - 1, [("aslabp", c) for c in range(NCH)], nap_d.ap()[l, :, :], KA - 1)
        if any(t["kind"] == "s" for t in tiles):
            emit_rows_out(aslab, NSTOK, [("aslab", c) for c in range(NCH)], None, NSTOK, sample_a_layer=l)

        phase_mark()
        for ti, t in enumerate(tiles):
            off, n = t["off"], t["n"]
            bs_ = P.bank()
            bq = P.bank()
            pe(mm(banks[bs_][:, 0:n], ones32[:, :], stats[:, 0, off:off + n], True, True), [("st", 0, ti), ("ones32",)], [BK(bs_)])
            pe(mm(banks[bq][:, 0:n], ones32[:, :], stats[:, 1, off:off + n], True, True), [("st", 1, ti), ("ones32",)], [BK(bq)])
            dve(lambda e, bs_=bs_, off=off, n=n: e.tensor_scalar(out=stats[:, 0, off:off + n], in0=banks[bs_][:, 0:n], scalar1=1.0 / D, scalar2=None, op0=ALU.mult),
                [BK(bs_)], [("st", 0, ti)])
            m2, m2k = tf()
            dve(lambda e, m2=m2, off=off, n=n: e.tensor_tensor(out=m2[:, 0:n], in0=stats[:, 0, off:off + n], in1=stats[:, 0, off:off + n], op=ALU.mult),
                [("st", 0, ti)], [m2k])
            dve(lambda e, m2=m2, bq=bq, n=n: e.scalar_tensor_tensor(out=m2[:, 0:n], in0=banks[bq][:, 0:n], scalar=1.0 / D, in1=m2[:, 0:n],
                                                                   op0=ALU.mult, op1=ALU.subtract),
                [BK(bq), m2k], [m2k])
            act(lambda e, m2=m2, off=off, n=n: e.activation(out=stats[:, 1, off:off + n], in_=m2[:, 0:n], func=AF.Sqrt, bias=epsc[:, 1:2], scale=1.0),
                [m2k, ("epsc",)], [("st", 1, ti)])
            dve(lambda e, off=off, n=n: e.reciprocal(out=stats[:, 1, off:off + n], in_=stats[:, 1, off:off + n]), [("st", 1, ti)], [("st", 1, ti)])

        def za_stage1(W, wk, cc, c, ti, t):
            off, n = t["off"], t["n"]
            bz = grp(W, wk, cc * 128, ti, t)
            sz, szk = tf()
            act(lambda e, bz=bz, sz=sz, n=n: e.activation(out=sz[:, 0:n], in_=banks[bz][:, 0:n], func=AF.Silu), [BK(bz)], [szk])
            t1, t1k = tf()
            dve(lambda e, t1=t1, c=c, off=off, n=n: e.tensor_tensor(out=t1[:, 0:n], in0=S2[:, c, off:off + n], in1=stats[:, 0, off:off + n], op=ALU.subtract),
                [("s2", c, ti), ("st", 0, ti)], [t1k])
            P.op(ZA_MUL_ENG, lambda e, t1=t1, off=off, n=n: e.tensor_tensor(out=t1[:, 0:n], in0=t1[:, 0:n], in1=stats[:, 1, off:off + n], op=ALU.mult),
                 [t1k, ("st", 1, ti)], [t1k])
            return (c, ti, t, sz, szk, t1, t1k)

        def za_stage2(c, ti, t, sz, szk, t1, t1k):
            off, n = t["off"], t["n"]
            act(lambda e, t1=t1, c=c, n=n: e.activation(out=t1[:, 0:n], in_=t1[:, 0:n], func=AF.Silu,
                                                      scale=colv[:, c, R_LNAG:R_LNAG + 1], bias=colv[:, c, R_LNAB:R_LNAB + 1]),
                [t1k, ("colv",)], [t1k])
            dve(lambda e, t1=t1, sz=sz, c=c, off=off, n=n: e.tensor_tensor(out=S2[:, c, off:off + n], in0=t1[:, 0:n], in1=sz[:, 0:n], op=ALU.mult),
                [t1k, szk], [("s2", c, ti)])

        pend = None
        for jb in range(2):
            W, wk = next_block()
            for cc in range(4):
                c = jb * 4 + cc
                for ti, t in enumerate(tiles):
                    u = za_stage1(W, wk, cc, c, ti, t)
                    if pend is not None:
                        za_stage2(*pend)
                    pend = u
        za_stage2(*pend)
        phase_mark()
        proj_phase(0, l, tiles)
        phase_mark()

        Wv0, wvk0 = next_block()
        Wv1, wvk1 = next_block(False)
        vblocks = []
        for ti, t in enumerate(tiles):
            for tbi in range(max(1, t["n"] // 128)):
                vblocks.append((ti, t, tbi))

        vcnt = [0]

        def vmm(ti, t, tbi):
            off, n = t["off"], t["n"]
            m_ = min(128, n)
            tok0 = off + tbi * 128
            b0 = 2 * (vcnt[0] % 3)
            b1 = b0 + 1
            vcnt[0] += 1
            pe([mm(banks[b0][0:m_, :], xn[:, k, tok0:tok0 + m_], Wv0[:, k, :], k == 0, k == NCH - 1) for k in range(NCH)],
               [wvk0] + [("xn", k, ti) for k in range(NCH)], [BK(b0)])
            pe([mm(banks[b1][0:m_, :], xn[:, k, tok0:tok0 + m_], Wv1[:, k, :], k == 0, k == NCH - 1) for k in range(NCH)],
               [wvk1] + [("xn", k, ti) for k in range(NCH)], [BK(b1)])
            return b0, b1

        def vrestA(ti, t, tbi, b0, b1):
            off, n = t["off"], t["n"]
            m_ = min(128, n)
            tok0 = off + tbi * 128
            ri = P.rotate("mvr", 3)
            mvv = mvr[:, ri, :]
            bstv = bstr[:, ri, :, :]
            dve([lambda e, b0=b0, m_=m_: e.bn_stats(out=bstv[0:m_, 0, :], in_=banks[b0][0:m_, :]),
                 lambda e, b1=b1, m_=m_: e.bn_stats(out=bstv[0:m_, 1, :], in_=banks[b1][0:m_, :])], [BK(b0), BK(b1)], [("bst", ri)])
            dve(lambda e, m_=m_: e.bn_aggr(out=mvv[0:m_, 0:2], in_=bstv[0:m_, :, :].rearrange("p a b -> p (a b)")), [("bst", ri)], [("mv", ri, 0)])
            act(lambda e, m_=m_: e.activation(out=mvv[0:m_, 2:3], in_=mvv[0:m_, 1:2], func=AF.Sqrt, bias=epsc[0:m_, 1:2], scale=1.0),
                [("mv", ri, 0), ("epsc",)], [("mv", ri, 1)])
            return (ti, t, tbi, b0, b1, ri)

        def vrestB(ti, t, tbi, b0, b1, ri):
            off, n = t["off"], t["n"]
            m_ = min(128, n)
            tok0 = off + tbi * 128
            mvv = mvr[:, ri, :]
            bstv = bstr[:, ri, :, :]
            dve(lambda e, m_=m_: e.reciprocal(out=mvv[0:m_, 2:3], in_=mvv[0:m_, 2:3]), [("mv", ri, 1)], [("mv", ri, 1)])
            dve(lambda e, m_=m_: e.tensor_scalar(out=mvv[0:m_, 3:4], in0=mvv[0:m_, 0:1], scalar1=mvv[0:m_, 2:3], scalar2=-1.0, op0=ALU.mult, op1=ALU.mult),
                [("mv", ri, 0), ("mv", ri, 1)], [("mv", ri, 2)])
            vi = P.rotate("vn", 2)
            vt = vn[vi]
            vk = ("vn", vi)
            for hh, bb in ((0, b0), (1, b1)):
                act(lambda e, vt=vt, hh=hh, bb=bb, m_=m_: e.activation(out=vt[0:m_, hh * 512:(hh + 1) * 512], in_=banks[bb][0:m_, :], func=AF.Identity,
                                                                  scale=mvv[0:m_, 2:3], bias=mvv[0:m_, 3:4]),
                    [BK(bb), ("mv", ri, 1), ("mv", ri, 2)], [vk])
            if t["kind"] == "s":
                so, sok = sg()
                for hh, bb in ((0, b0), (1, b1)):
                    act(lambda e, so=so, hh=hh, bb=bb, m_=m_: e.activation(out=so[0:m_, hh * 512:(hh + 1) * 512], in_=banks[bb][0:m_, :], func=AF.Identity,
                                                                      scale=mvv[0:m_, 2:3], bias=mvv[0:m_, 3:4]),
                        [BK(bb), ("mv", ri, 1), ("mv", ri, 2)], [sok])
                gb_, gbk = sg()
                P.dma("sp", gbk[0] + str(gbk[1]), gb_[0:NSTOK, :], vecs_d.ap()[l, R_LNBG, :].partition_broadcast(NSTOK), writes=[gbk])
                dve(lambda e, so=so, gb_=gb_: e.tensor_tensor(out=so[0:NSTOK, :], in0=so[0:NSTOK, :], in1=gb_[0:NSTOK, :], op=ALU.mult), [sok, gbk], [sok])
                gb2, gbk2 = sg()
                P.dma("sp", gbk2[0] + str(gbk2[1]), gb2[0:NSTOK, :], vecs_d.ap()[l, R_LNBB, :].partition_broadcast(NSTOK), writes=[gbk2])
                dve(lambda e, so=so, gb2=gb2: e.tensor_tensor(out=so[0:NSTOK, :], in0=so[0:NSTOK, :], in1=gb2[0:NSTOK, :], op=ALU.add), [sok, gbk2], [sok])
                out_evs.append(P.dma("sp", "o_" + sok[0] + str(sok[1]), nvs_d.ap()[l, :, :], so[0:NSTOK, :], reads=[sok]))
                b = 6
                pe([mm(banks[b][:, g_ * 64:(g_ + 1) * 64], vt[0:NSTOK, g_ * 128:(g_ + 1) * 128], BD[0:NSTOK, g_, :], True, True) for g_ in range(NCH)],
                   [vk, ("BD",)], [BK(b)])
                act(lambda e, b=b, off=off: e.activation(out=S2[:, :, off:off + NSTOK], in_=banks[b][:, :].rearrange("p (g q) -> p g q", g=NCH), func=AF.Copy),
                    [BK(b)], [("s2", g_, ti) for g_ in range(NCH)])
            else:
                for h in range(2):
                    b = 6 + h
                    pe([mm(banks[b][:, gg * 128:(gg + 1) * 128], vt[:, (h * 4 + gg) * 128:(h * 4 + gg + 1) * 128], WmTb[:, h * 4 + gg, :], True, True) for gg in range(4)],
                       [vk, ("WmTb", h)], [BK(b)])
                    act(lambda e, b=b, h=h, tok0=tok0: e.activation(out=S2[:, h * 4:(h + 1) * 4, tok0:tok0 + 128],
                                                                    in_=banks[b][:, :].rearrange("p (g i) -> p g i", g=4), func=AF.Copy),
                        [BK(b)], [("s2", h * 4 + gg, ti) for gg in range(4)])

        vq = [vmm(*vblocks[0])]
        if len(vblocks) > 1:
            vq.append(vmm(*vblocks[1]))
        vpend = None
        for i_, blk in enumerate(vblocks):
            b0_, b1_ = vq.pop(0)
            if vpend is not None:
                vrestB(*vpend)
            if i_ + 2 < len(vblocks):
                vq.append(vmm(*vblocks[i_ + 2]))
            vpend = vrestA(blk[0], blk[1], blk[2], b0_, b1_)
        vrestB(*vpend)
        for jb in range(2):
            Wz, wzk = next_block()
            Wu, wuk = next_block(False)
            for cc in range(4):
                c = jb * 4 + cc
                for ti, t in enumerate(tiles):
                    off, n = t["off"], t["n"]
                    bz = grp(Wz, wzk, cc * 128, ti, t)
                    sz, szk = tf()
                    act(lambda e, bz=bz, sz=sz, n=n: e.activation(out=sz[:, 0:n], in_=banks[bz][:, 0:n], func=AF.Silu), [BK(bz)], [szk])
                    bu = grp(Wu, wuk, cc * 128, ti, t)
                    dve(lambda e, bu=bu, sz=sz, n=n: e.tensor_tensor(out=sz[:, 0:n], in0=banks[bu][:, 0:n], in1=sz[:, 0:n], op=ALU.mult), [BK(bu), szk], [szk])
                    wv, wvk = tf()
                    if t["kind"] == "p":
                        dve(lambda e, wv=wv, c=c, off=off, n=n: e.scalar_tensor_tensor(
                            out=wv[:, 0:n].rearrange("p (a i) -> p a i", i=128), in0=S2[:, c, off:off + n].rearrange("p (a i) -> p a i", i=128),
                            scalar=colv[:, c, R_LNBG:R_LNBG + 1], in1=BiasB[:, c, :].unsqueeze(1).broadcast_to([128, n // 128, 128]),
                            op0=ALU.mult, op1=ALU.add),
                            [("s2", c, ti), ("colv",), ("BiasB", c // 4)], [wvk])
                    else:
                        dve(lambda e, wv=wv, c=c, off=off: e.scalar_tensor_tensor(
                            out=wv[:, 0:NSTOK].rearrange("p (s t) -> p s t", s=NS), in0=S2[:, c, off:off + NSTOK].rearrange("p (s t) -> p s t", s=NS),
                            scalar=colv[:, c, R_LNBG:R_LNBG + 1], in1=BiasB[:, c, 0:ST].unsqueeze(1).broadcast_to([128, NS, ST]),
                            op0=ALU.mult, op1=ALU.add),
                            [("s2", c, ti), ("colv",), ("BiasB", c // 4)], [wvk])
                    dve(lambda e, sz=sz, wv=wv, c=c, off=off, n=n: e.tensor_tensor(out=S2[:, c, off:off + n], in0=sz[:, 0:n], in1=wv[:, 0:n], op=ALU.mult),
                        [szk, wvk], [("s2", c, ti)])
        proj_phase(1, l, tiles)
        phase_mark()

        def c_mul(c, W, wk, cc):
            slot = P.rotate("ab", NAB)
            abk = ("ab", slot)
            if gi == 0:
                pool(lambda e, slot=slot: e.memset(ab[:, slot, 0:KC - 1], 0.0), [], [abk])
            else:
                pool(lambda e, slot=slot, c=c: e.tensor_copy(out=ab[:, slot, 0:KC - 1], in_=chalo[:, l, c, :]), [("chalo", l, c)], [abk])
            for ti, t in enumerate(tiles):
                off, n = t["off"], t["n"]
                bg = grp(W, wk, cc * 128, ti, t)
                gcm, gck = tf()
                act(lambda e, bg=bg, gcm=gcm, n=n: e.activation(out=gcm[:, 0:n], in_=banks[bg][:, 0:n], func=AF.Copy), [BK(bg)], [gck])
                bh = grp(W, wk, 256 + cc * 128, ti, t)
                if t["kind"] == "p":
                    dve(lambda e, bh=bh, gcm=gcm, slot=slot, off=off, n=n: e.tensor_tensor(
                        out=ab[:, slot, KC - 1 + off:KC - 1 + off + n], in0=banks[bh][:, 0:n], in1=gcm[:, 0:n], op=ALU.mult), [BK(bh), gck], [abk])
                    if last_group and ti == len(tiles) - 1:
                        dve(lambda e, bh=bh, gcm=gcm, c=c, n=n: e.tensor_tensor(
                            out=cslabp[:, c, 0:KC - 1], in0=banks[bh][:, n - (KC - 1):n], in1=gcm[:, n - (KC - 1):n], op=ALU.mult),
                            [BK(bh), gck], [("cslabp", c)])
                else:
                    dve(lambda e, bh=bh, gcm=gcm, c=c: e.tensor_tensor(
                        out=cis_[:, c, :].rearrange("p (s r) -> p s r", s=NS)[:, :, KC - 1:KC - 1 + ST],
                        in0=banks[bh][:, 0:NSTOK].rearrange("p (s t) -> p s t", s=NS),
                        in1=gcm[:, 0:NSTOK].rearrange("p (s t) -> p s t", s=NS), op=ALU.mult), [BK(bh), gck], [("cis", c)])
                    dve(lambda e, bh=bh, gcm=gcm, c=c: e.tensor_tensor(
                        out=cslab[:, c, 0:NS * 2].rearrange("p (s r) -> p s r", s=NS),
                        in0=banks[bh][:, 0:NSTOK].rearrange("p (s t) -> p s t", s=NS)[:, :, ST - 2:ST],
                        in1=gcm[:, 0:NSTOK].rearrange("p (s t) -> p s t", s=NS)[:, :, ST - 2:ST], op=ALU.mult), [BK(bh), gck], [("cslab", c)])
            if gi == 0 and len(groups) > 1:
                pool(lambda e, slot=slot, c=c: e.tensor_copy(out=chalo[:, l, c, :], in_=ab[:, slot, 1024:1024 + KC - 1]), [abk], [("chalo", l, c)])
            return slot

        def c_diag(c):
            act([lambda e, k=k, c=c: e.activation(out=diagC[:, k, :], in_=identb[:, :], func=AF.Identity, scale=colv[:, c, R_WCC + k:R_WCC + k + 1])
                 for k in range(KC)], [("identb",), ("colv",)], [("diagC",)])

        def c_conv(c, slot):
            abk = ("ab", slot)
            for ti, t in enumerate(tiles):
                off, n = t["off"], t["n"]
                b = P.bank()
                if t["kind"] == "p":
                    pe([mm(banks[b][:, 0:n], diagC[:, k, :], ab[:, slot, off + k:off + k + n], k == 0, k == KC - 1) for k in range(KC)],
                       [("diagC",), abk], [BK(b)])
                else:
                    cv = cis_[:, c, :].rearrange("p (s r) -> p s r", s=NS)
                    pe([mm(banks[b][:, 0:NSTOK].rearrange("p (s t) -> p s t", s=NS), diagC[:, k, :], cv[:, :, k:k + ST], k == 0, k == KC - 1) for k in range(KC)],
                       [("diagC",), ("cis", c)], [BK(b)])
                act(lambda e, b=b, c=c, off=off, n=n: e.activation(out=S2[:, c, off:off + n], in_=banks[b][:, 0:n], func=AF.Copy), [BK(b)], [("s2", c, ti)])

        prev = None
        for c in range(NCH):
            if c % 2 == 0:
                W, wk = next_block()
            if prev is not None:
                c_diag(prev[0])
            slot = c_mul(c, W, wk, c % 2)
            if prev is not None:
                c_conv(*prev)
            prev = (c, slot)
        c_diag(prev[0])
        c_conv(*prev)
        if last_group:
            emit_rows_out(cslabp, KC - 1, [("cslabp", c) for c in range(NCH)], ncp_d.ap()[l, :, :], KC - 1)
        if any(t["kind"] == "s" for t in tiles):
            emit_rows_out(cslab, NS * 2, [("cslab", c) for c in range(NCH)], ncs_d.ap()[l, :, :], NS * 2)
        for jb in range(2):
            Wz, wzk = next_block()
            Wg, wgk = next_block(False)
            for cc in range(4):
                c = jb * 4 + cc
                for ti, t in enumerate(tiles):
                    off, n = t["off"], t["n"]
                    bz = grp(Wz, wzk, cc * 128, ti, t)
                    sz, szk = tf()
                    act(lambda e, bz=bz, sz=sz, n=n: e.activation(out=sz[:, 0:n], in_=banks[bz][:, 0:n], func=AF.Silu), [BK(bz)], [szk])
                    bg = grp(Wg, wgk, cc * 128, ti, t)
                    dve(lambda e, bg=bg, sz=sz, n=n: e.tensor_tensor(out=sz[:, 0:n], in0=banks[bg][:, 0:n], in1=sz[:, 0:n], op=ALU.mult), [BK(bg), szk], [szk])
                    dve(lambda e, sz=sz, c=c, off=off, n=n: e.tensor_tensor(out=S2[:, c, off:off + n], in0=sz[:, 0:n], in1=S2[:, c, off:off + n], op=ALU.mult),
                        [szk, ("s2", c, ti)], [("s2", c, ti)])
        proj_phase(2, l, tiles)
        phase_mark()

        Wo0, wok0 = next_block()
        Wo1, wok1 = next_block(False)
        for ti, t in enumerate(tiles):
            off, n = t["off"], t["n"]
            for j in range(NCH):
                W, wk = (Wo0, wok0) if j < 4 else (Wo1, wok1)
                cc = j % 4
                b = P.bank()
                pe([mm(banks[b][:, 0:n], W[:, k, cc * 128:(cc + 1) * 128], mview[:, k, off:off + n], k == 0, k == NCH - 1) for k in range(NCH)],
                   [wk] + [("m", k, ti) for k in range(NCH)], [BK(b)])
                dve(lambda e, b=b, j=j, off=off, n=n: e.tensor_tensor(out=xT[:, j, off:off + n], in0=banks[b][:, 0:n], in1=xT[:, j, off:off + n], op=ALU.add),
                    [BK(b), ("xT", j, ti)], [("xT", j, ti)])
            rms_tile(ti, t)

    def proj_phase(br, l, tiles):
        for jb in range(2):
            Wg, wgk = next_block()
            Wp, wpk = next_block(False)
            for cc in range(4):
                j = jb * 4 + cc
                for ti, t in enumerate(tiles):
                    off, n = t["off"], t["n"]
                    bg = grp(Wg, wgk, cc * 128, ti, t)
                    gt, gtk = tf()
                    act(lambda e, bg=bg, gt=gt, j=j, n=n: e.activation(out=gt[:, 0:n], in_=banks[bg][:, 0:n], func=AF.Sigmoid,
                                                                     bias=colv[:, j, R_BG + br:R_BG + br + 1], scale=1.0),
                        [BK(bg), ("colv",)], [gtk])
                    bp = grp(Wp, wpk, cc * 128, ti, t, srcbuf=S2, srckey="s2")
                    if br == 0:
                        dve(lambda e, bp=bp, gt=gt, j=j, off=off, n=n: e.tensor_tensor(out=mview[:, j, off:off + n], in0=banks[bp][:, 0:n], in1=gt[:, 0:n], op=ALU.mult),
                            [BK(bp), gtk], [("m", j, ti)])
                    else:
                        dve(lambda e, bp=bp, gt=gt, n=n: e.tensor_tensor(out=gt[:, 0:n], in0=banks[bp][:, 0:n], in1=gt[:, 0:n], op=ALU.mult), [BK(bp), gtk], [gtk])
                        dve(lambda e, gt=gt, j=j, off=off, n=n: e.tensor_tensor(out=mview[:, j, off:off + n], in0=gt[:, 0:n], in1=mview[:, j, off:off + n], op=ALU.add),
                            [gtk, ("m", j, ti)], [("m", j, ti)])

    def emit_rows_out(slab, nrows, keys, dst, nrows_dst, sample_a_layer=None):
        so, sok = sg()
        for h in range(2):
            b = P.bank()
            pe([tp(banks[b][0:nrows, cc * 128:(cc + 1) * 128], slab[:, h * 4 + cc, 0:nrows], ident[:, :]) for cc in range(4)],
               keys + [("ident",)], [BK(b)])
            act(lambda e, b=b, h=h, so=so: e.activation(out=so[0:nrows, h * 512:(h + 1) * 512], in_=banks[b][0:nrows, :], func=AF.Copy), [BK(b)], [sok])
        if sample_a_layer is None:
            out_evs.append(P.dma("sp", "o_" + sok[0] + str(sok[1]), dst, so[0:nrows, :], reads=[sok]))
        else:
            l = sample_a_layer
            for s_i in range(NS):
                out_evs.append(P.dma("sp", "o_" + sok[0] + str(sok[1]), nas_d.ap()[l, s_i, KA - 1 - ST:KA - 1, :], so[s_i * ST:(s_i + 1) * ST, :], reads=[sok]))

    def final_out(gi, tiles):
        for ti, t in enumerate(tiles):
            off, n = t["off"], t["n"]
            for c in range(NCH):
                dve(lambda e, c=c, off=off, n=n: e.scalar_tensor_tensor(
                    out=xT[:, c, off:off + n], in0=xT[:, c, off:off + n], scalar=colv[:, c, R_FING:R_FING + 1],
                    in1=stats[:, 0, off:off + n], op0=ALU.mult, op1=ALU.mult),
                    [("xT", c, ti), ("colv",), ("st", 0, ti)], [("xT", c, ti)])
            nblk = max(1, n // 128)
            for tbi in range(nblk):
                m_ = min(128, n)
                tok0 = off + tbi * 128
                oi = P.rotate("ost", 4)
                so = mb_f32[:, oi, :]
                sok = ("mbstg", oi)
                for h in range(2):
                    b = P.bank()
                    pe([tp(banks[b][0:m_, cc * 128:(cc + 1) * 128], xT[:, h * 4 + cc, tok0:tok0 + m_], ident[:, :]) for cc in range(4)],
                       [("xT", h * 4 + cc, ti) for cc in range(4)] + [("ident",)], [BK(b)])
                    act(lambda e, b=b, h=h, so=so, m_=m_: e.activation(out=so[0:m_, h * 512:(h + 1) * 512], in_=banks[b][0:m_, :], func=AF.Copy),
                        [BK(b)], [sok] + (MB_ALL if (h == 0 and tbi == 0 and ti == 0) else []))
                if t["kind"] == "p":
                    dst = yp_d.ap()[t["seq0"] + tbi * 128:t["seq0"] + (tbi + 1) * 128, :]
                else:
                    dst = ys_d.ap()[:, :]
                out_evs.append(P.dma("sp", "o_mb%d" % oi, dst, so[0:m_, :], reads=[sok]))

    class _Stop(Exception):
        pass

    def phase_mark():
        return None

    issue_weights(NRING)
    try:
        for gi, tiles in enumerate(groups):
            load_group(gi, tiles)
            rms_stats(tiles)
            phase_mark()
            for l in range(depth):
                run_pass(gi, l, tiles)
            final_out(gi, tiles)
    except _Stop:
        pass
    P.wait_all("sp", out_evs)
    P.emit()
    es.close()
    return nc


_NC_CACHE = {}


def kernel(x_prompt, x_sample, state_conv_a, state_conv_c, norm_g, w_in, b_gate,
           w_conv_a, b_conv_a, ln_a_g, ln_a_b, w_proj_a, ln_b_g, ln_b_b, w_s, b_s,
           w_proj_b, w_conv_c, w_proj_c, w_out, final_g):
    f = lambda a: np.ascontiguousarray(np.asarray(a, dtype=np.float32))
    x_prompt, x_sample, state_conv_a, state_conv_c = f(x_prompt), f(x_sample), f(state_conv_a), f(state_conv_c)
    w_in, w_proj_a, w_proj_b, w_proj_c, w_out = f(w_in), f(w_proj_a), f(w_proj_b), f(w_proj_c), f(w_out)
    vecs = np.zeros((DEPTH, NR, D), dtype=np.float32)
    vecs[:, R_NORMG] = f(norm_g)
    vecs[:, R_BG:R_BG + 3] = f(b_gate).reshape(DEPTH, 3, D)
    vecs[:, R_WCA:R_WCA + KA] = f(w_conv_a)
    vecs[:, R_BCA] = f(b_conv_a)
    vecs[:, R_LNAG] = f(ln_a_g)
    vecs[:, R_LNAB] = f(ln_a_b)
    vecs[:, R_LNBG] = f(ln_b_g)
    vecs[:, R_LNBB] = f(ln_b_b)
    vecs[:, R_WCC:R_WCC + KC] = f(w_conv_c)
    vecs[:, R_FING] = f(final_g)[None, :]
    w_s_ = f(w_s)
    wpad = np.zeros((DEPTH, 32, D), dtype=np.float32)
    wpad[:, :KA] = f(w_conv_a)
    wst = np.ascontiguousarray(wpad.reshape(DEPTH, 8, 4, NCH, 4, 32).transpose(0, 2, 5, 3, 1, 4).reshape(DEPTH, 128, 256))
    b_s_ = f(b_s).reshape(DEPTH, NCH * 128)

    if "nc" not in _NC_CACHE:
        _NC_CACHE["nc"] = build()
    nc = _NC_CACHE["nc"]
    in_maps = []
    for b in range(8):
        in_maps.append({
            "xp": x_prompt[b],
            "xs": np.ascontiguousarray(x_sample[b * NS:(b + 1) * NS].reshape(NSTOK, D)),
            "sca": np.ascontiguousarray(state_conv_a[:, b * NS:(b + 1) * NS]),
            "scc": np.ascontiguousarray(state_conv_c[:, b * NS:(b + 1) * NS]),
            "w_in": w_in, "w_proj_a": w_proj_a, "w_proj_b": w_proj_b, "w_proj_c": w_proj_c, "w_out": w_out,
            "vecs": vecs, "w_s": w_s_, "b_s": b_s_, "wst": wst,
        })
    res = run_bass_kernel_spmd(nc, in_maps, core_ids=list(range(8)))
    R = res.results
    y_prompt = np.stack([R[b]["yp"] for b in range(8)], axis=0)
    y_sample = np.concatenate([R[b]["ys"].reshape(NS, ST, D) for b in range(8)], axis=0)
    nap = np.stack([R[b]["nap"] for b in range(8)], axis=1)
    nas = np.concatenate([R[b]["nas"] for b in range(8)], axis=1)
    ncp = np.stack([R[b]["ncp"] for b in range(8)], axis=1)
    ncs = np.concatenate([R[b]["ncs"].reshape(DEPTH, NS, KC - 1, D) for b in range(8)], axis=1)
    nvs = np.concatenate([R[b]["nvs"].reshape(DEPTH, NS, ST, D) for b in range(8)], axis=1)
    return (y_prompt.astype(np.float32), y_sample.astype(np.float32), nap.astype(np.float32), nas.astype(np.float32),
            ncp.astype(np.float32), ncs.astype(np.float32), nvs.astype(np.float32))
```

```python
import numpy as np
from contextlib import ExitStack
import concourse.bass as bass
import concourse.mybir as mybir
from concourse.bass_utils import run_bass_kernel_spmd

F32 = mybir.dt.float32
BF16 = mybir.dt.bfloat16
AF = mybir.ActivationFunctionType
ALU = mybir.AluOpType

D = 1024
NCH = 8
DEPTH = 4
SEQ = 2048
NS = 16
ST = 4
NSTOK = NS * ST
KA = 31
KC = 3
D_IN = 13 * D
RMS_EPS = 1e-6
LN_EPS = 1e-5

R_NORMG = 0
R_BG = 1
R_WCA = 4
R_BCA = 35
R_LNAG = 36
R_LNAB = 37
R_LNBG = 38
R_LNBB = 39
R_WCC = 40
R_FING = 43
NR = 44

C_AVAL, C_AGATE, C_ZA, C_U, C_V, C_ZB, C_GB, C_GC, C_HC, C_ZC, C_G0, C_G1, C_G2 = [i * D for i in range(13)]

NRING = 4
ZA_MUL_ENG = "dve"


class Eng:
    def __init__(self, name, sem):
        self.name = name
        self.sem = sem
        self.count = 0
        self.ops = []
        self.known = {}


class Prog:
    def __init__(self, nc, es):
        self.nc = nc
        self.es = es
        self.sems = {}
        self.eng = {}
        for n in ("pe", "act", "dve", "pool", "sp"):
            self.eng[n] = Eng(n, self.sem("e_" + n))
        self.res = {}
        self.dma_cnt = {}
        self.nbank = 0
        self.rot = {}

    def sem(self, name):
        if name not in self.sems:
            self.sems[name] = self.es.enter_context(self.nc.semaphore(name))
        return name

    def _deps(self, reads, writes):
        deps = []
        for k in reads:
            st = self.res.get(k)
            if st is not None and st[0] is not None:
                deps.append(st[0])
            if st is not None and k[0] == "bk":
                deps.extend(st[1])
        for k in writes:
            st = self.res.get(k)
            if st is not None:
                if st[0] is not None:
                    deps.append(st[0])
                deps.extend(st[1])
        return deps

    def _commit(self, ev, reads, writes):
        for k in reads:
            st = self.res.setdefault(k, [None, []])
            st[1].append(ev)
            if len(st[1]) > 64:
                best = {}
                for (s, v) in st[1]:
                    if best.get(s, 0) < v:
                        best[s] = v
                st[1] = list(best.items())
        for k in writes:
            self.res[k] = [ev, []]

    def _waits(self, E, deps):
        need = {}
        for (s, v) in deps:
            if E.name == "pe" and s == E.sem:
                continue
            if E.known.get(s, 0) >= v:
                continue
            if need.get(s, 0) < v:
                need[s] = v
        for s, v in need.items():
            E.known[s] = v
        return list(need.items())

    def op(self, eng, fns, reads=(), writes=()):
        if not isinstance(fns, (list, tuple)):
            fns = [fns]
        E = self.eng[eng]
        waits = self._waits(E, self._deps(reads, writes))
        E.count += 1
        ev = (E.sem, E.count)
        E.ops.append((waits, list(fns), (E.sem, 1)))
        self._commit(ev, reads, writes)
        return ev

    def dma(self, q, semkey, out, in_, reads=(), writes=(), **kw):
        E = self.eng[q]
        s = self.sem("d_" + semkey)
        waits = self._waits(E, self._deps(reads, writes))
        self.dma_cnt[s] = self.dma_cnt.get(s, 0) + 1
        ev = (s, 16 * self.dma_cnt[s])
        E.ops.append((waits, [lambda e, out=out, in_=in_, kw=kw: e.dma_start(out=out, in_=in_, **kw)], (s, 16)))
        self._commit(ev, reads, writes)
        return ev

    def wait_all(self, eng, evs):
        E = self.eng[eng]
        waits = self._waits(E, evs)
        E.ops.append((waits, [], None))

    def bank(self):
        i = self.nbank % 8
        self.nbank += 1
        return i

    def rotate(self, name, n):
        i = self.rot.get(name, 0)
        self.rot[name] = i + 1
        return i % n

    def emit(self):
        nc = self.nc
        block = self.es.enter_context(nc.Block())
        sems = self.sems

        def run(E):
            def body(e):
                for waits, fns, inc in E.ops:
                    for (s, v) in waits:
                        e.wait_ge(sems[s], v)
                    last = None
                    for f in fns:
                        last = f(e)
                    if inc is not None and last is not None:
                        last.then_inc(sems[inc[0]], inc[1])
            return body

        block.tensor(run(self.eng["pe"]))
        block.scalar(run(self.eng["act"]))
        block.vector(run(self.eng["dve"]))
        block.gpsimd(run(self.eng["pool"]))
        block.sync(run(self.eng["sp"]))


def _consts():
    ident = np.eye(128, dtype=np.float32)
    tril = np.tril(np.ones((128, 128), dtype=np.float32))
    rep = np.zeros((4, 64), dtype=np.float32)
    for q in range(64):
        rep[q % 4, q] = 1.0
    bmask = np.zeros((64, 64), dtype=np.float32)
    for p in range(64):
        for q in range(64):
            if p // 4 == q // 4:
                bmask[p, q] = 1.0
    return ident, tril, rep, bmask


def build(depth=DEPTH, ngroups=2):
    nc = bass.Bass("TRN2", target_bir_lowering=False)
    dt = nc.dram_tensor
    xp_d = dt("xp", [SEQ, D], F32, kind="ExternalInput")
    xs_d = dt("xs", [NSTOK, D], F32, kind="ExternalInput")
    sca_d = dt("sca", [DEPTH, NS, KA - 1, D], F32, kind="ExternalInput")
    scc_d = dt("scc", [DEPTH, NS, KC - 1, D], F32, kind="ExternalInput")
    win_d = dt("w_in", [DEPTH, D, D_IN], F32, kind="ExternalInput")
    wpa_d = dt("w_proj_a", [DEPTH, D, D], F32, kind="ExternalInput")
    wpb_d = dt("w_proj_b", [DEPTH, D, D], F32, kind="ExternalInput")
    wpc_d = dt("w_proj_c", [DEPTH, D, D], F32, kind="ExternalInput")
    wo_d = dt("w_out", [DEPTH, D, D], F32, kind="ExternalInput")
    vecs_d = dt("vecs", [DEPTH, NR, D], F32, kind="ExternalInput")
    ws_d = dt("w_s", [DEPTH, NCH, 128, 128], F32, kind="ExternalInput")
    bs_d = dt("b_s", [DEPTH, NCH * 128], F32, kind="ExternalInput")
    wst_d = dt("wst", [DEPTH, 128, 256], F32, kind="ExternalInput")
    yp_d = dt("yp", [SEQ, D], F32, kind="ExternalOutput")
    ys_d = dt("ys", [NSTOK, D], F32, kind="ExternalOutput")
    nap_d = dt("nap", [DEPTH, KA - 1, D], F32, kind="ExternalOutput")
    nas_d = dt("nas", [DEPTH, NS, KA - 1, D], F32, kind="ExternalOutput")
    ncp_d = dt("ncp", [DEPTH, KC - 1, D], F32, kind="ExternalOutput")
    ncs_d = dt("ncs", [DEPTH, NS * (KC - 1), D], F32, kind="ExternalOutput")
    nvs_d = dt("nvs", [DEPTH, NSTOK, D], F32, kind="ExternalOutput")
    c_ident, c_tril, c_rep, c_bmask = _consts()
    ident_d = nc.inline_tensor(c_ident, "c_ident")
    tril_d = nc.inline_tensor(c_tril, "c_tril")
    rep_d = nc.inline_tensor(c_rep, "c_rep")
    bmask_d = nc.inline_tensor(c_bmask, "c_bmask")
    c_identst = np.tile(np.eye(32, dtype=np.float32), (4, 1))
    identst_d = nc.inline_tensor(c_identst, "c_identst")

    es = ExitStack()
    P = Prog(nc, es)

    def sb(name, shape, dty):
        return es.enter_context(nc.sbuf_tensor(name, shape, dty))

    TG = 1088
    xT = sb("xT", [128, NCH, TG], F32)
    xn = sb("xn", [128, NCH, TG], BF16)
    mbuf = sb("mbuf", [128, NCH * TG], BF16)
    S2 = sb("S2", [128, NCH, TG], BF16)
    NAB = 2
    ab = sb("ab", [128, NAB, 1056], BF16)
    stk = sb("stk", [128, 2, 4, 1056], BF16)
    sdiag = sb("sdiag", [128, 2, 32, 32], BF16)
    wst = sb("wst_sb", [128, 256], BF16)
    ones32 = sb("ones32", [128, 128], F32)
    identst = sb("identst", [128, 32], BF16)
    identstf = sb("identstf", [128, 32], F32)
    abs_ = sb("abs", [128, NCH, NS * 34], BF16)
    cis_ = sb("cis", [128, NCH, NS * 6], BF16)
    stats = sb("stats", [128, 2, TG], F32)
    tmpF = [sb("tmpF%d" % i, [128, 512], F32) for i in range(5)]
    tmpB = [sb("tmpB%d" % i, [128, 512], BF16) for i in range(3)]
    wring = [sb("wring%d" % i, [128, NCH, 512], BF16) for i in range(NRING)]
    colv = sb("colv", [128, NCH, NR], F32)
    diagC = sb("diagC", [128, KC, 128], BF16)
    WmTb = sb("WmTb", [128, NCH, 128], BF16)
    BiasB = sb("BiasB", [128, NCH, 128], F32)
    BD = sb("BD", [64, NCH, 64], BF16)
    ident = sb("ident", [128, 128], F32)
    identb = sb("identb", [128, 128], BF16)
    onesb = sb("onesb", [128, 128], BF16)
    tril = sb("tril", [128, 128], F32)
    rep = sb("rep", [4, 64], F32)
    bmask = sb("bmask", [64, 64], F32)
    epsc = sb("epsc", [128, 2], F32)
    stg = [sb("stg%d" % i, [128, D], F32) for i in range(3)]
    vn = [sb("vn%d" % i, [128, D], BF16) for i in range(2)]
    ahalo = sb("ahalo", [128, DEPTH, NCH, KA - 1], BF16)
    chalo = sb("chalo", [128, DEPTH, NCH, KC - 1], BF16)
    aslab = sb("aslab", [128, NCH, 64], F32)
    cslab = sb("cslab", [128, NCH, 32], F32)
    aslabp = sb("aslabp", [128, NCH, KA - 1], F32)
    cslabp = sb("cslabp", [128, NCH, KC - 1], F32)
    bstr = sb("bstr", [128, 3, 2, 6], F32)
    mvr = sb("mvr", [128, 3, 4], F32)
    banks = [es.enter_context(nc.psum_tensor("bank%d" % i, [128, 512], F32)) for i in range(8)]

    mb_f32 = mbuf[:, 0:4 * D * 2].bitcast(F32).rearrange("p (a c) -> p a c", a=4)
    mview = mbuf[:, :].rearrange("p (c t) -> p c t", c=NCH)
    WmTf = mb_f32[:, 0, :].rearrange("p (g i) -> p g i", g=NCH)
    MB_ALL = [("m", j, ti) for j in range(NCH) for ti in range(3)]

    def BK(i):
        return ("bk", i)

    def act(fn, reads, writes):
        return P.op("act", fn, reads, writes)

    def dve(fn, reads, writes):
        return P.op("dve", fn, reads, writes)

    def pool(fn, reads, writes):
        return P.op("pool", fn, reads, writes)

    def pe(fns, reads, writes):
        return P.op("pe", fns, reads, writes)

    def mm(out, lhsT, rhs, start, stop):
        return lambda e: e.matmul(out, lhsT=lhsT, rhs=rhs, start=start, stop=stop)

    def tp(out, in_, idn):
        return lambda e: e.transpose(out, in_, idn)

    def tf():
        i = P.rotate("tf", 5)
        return tmpF[i], ("tf", i)

    def tb_():
        i = P.rotate("tb", 3)
        return tmpB[i], ("tb", i)

    def sg():
        i = P.rotate("stg", 3)
        return stg[i], ("stg", i)

    out_evs = []

    P.dma("sp", "c0", ident[:], ident_d.ap(), writes=[("ident",)])
    P.dma("sp", "c1", tril[:], tril_d.ap(), writes=[("tril",)])
    P.dma("sp", "c2", rep[:], rep_d.ap(), writes=[("rep",)])
    P.dma("sp", "c3", bmask[:], bmask_d.ap(), writes=[("bmask",)])
    dve(lambda e: e.tensor_copy(out=identb[:], in_=ident[:]), [("ident",)], [("identb",)])
    P.dma("sp", "c4", identstf[:], identst_d.ap(), writes=[("identstf",)])
    dve(lambda e: e.tensor_copy(out=identst[:], in_=identstf[:]), [("identstf",)], [("identst",)])
    for sl in range(NAB):
        pool(lambda e, sl=sl: e.memset(ab[:, sl, 1054:1056], 0.0), [], [("ab", sl)])
    pool(lambda e: e.memset(onesb[:], 1.0), [], [("onesb",)])
    pool(lambda e: e.memset(ones32[:], 1.0), [], [("ones32",)])
    pool(lambda e: e.memset(epsc[:, 0:1], RMS_EPS), [], [("epsc",)])
    pool(lambda e: e.memset(epsc[:, 1:2], LN_EPS), [], [("epsc",)])

    groups = []
    g0 = [dict(off=0, n=512, kind="p", seq0=0), dict(off=512, n=512, kind="p", seq0=512),
          dict(off=1024, n=64, kind="s", seq0=0)]
    g1 = [dict(off=0, n=512, kind="p", seq0=1024), dict(off=512, n=512, kind="p", seq0=1536)]
    groups = [g0, g1][:ngroups]

    def wsrc(tname, l, segs):
        tens = {"in": win_d, "pa": wpa_d, "pb": wpb_d, "pc": wpc_d, "o": wo_d}[tname]
        outs = []
        pos = 0
        for (c0, ncol) in segs:
            src = tens.ap()[l, :, c0:c0 + ncol].rearrange("(k p) c -> p k c", p=128)
            outs.append((pos, ncol, src))
            pos += ncol
        return outs

    def pass_blocks(l):
        bl = []
        for c0 in (0, 2, 4, 6):
            bl.append(("in", l, [(C_AVAL + c0 * 128, 256), (C_AGATE + c0 * 128, 256)]))
        for jb in range(2):
            bl.append(("in", l, [(C_ZA + jb * 512, 512)]))
        bl.append(("in", l, [(C_GC, 256), (C_HC, 256)]))
        for jb in range(2):
            bl.append(("in", l, [(C_G0 + jb * 512, 512)]))
            bl.append(("pa", l, [(jb * 512, 512)]))
        for jb in range(2):
            bl.append(("in", l, [(C_V + jb * 512, 512)]))
        for jb in range(2):
            bl.append(("in", l, [(C_ZB + jb * 512, 512)]))
            bl.append(("in", l, [(C_U + jb * 512, 512)]))
        for jb in range(2):
            bl.append(("in", l, [(C_G1 + jb * 512, 512)]))
            bl.append(("pb", l, [(jb * 512, 512)]))
        for c0 in (2, 4, 6):
            bl.append(("in", l, [(C_GC + c0 * 128, 256), (C_HC + c0 * 128, 256)]))
        for jb in range(2):
            bl.append(("in", l, [(C_ZC + jb * 512, 512)]))
            bl.append(("in", l, [(C_GB + jb * 512, 512)]))
        for jb in range(2):
            bl.append(("in", l, [(C_G2 + jb * 512, 512)]))
            bl.append(("pc", l, [(jb * 512, 512)]))
        for jb in range(2):
            bl.append(("o", l, [(jb * 512, 512)]))
        return bl

    all_blocks = []
    for g in range(len(groups)):
        for l in range(depth):
            all_blocks.extend(pass_blocks(l))
    wstate = dict(issued=0, cur=0)

    def issue_weights(upto):
        while wstate["issued"] < min(upto, len(all_blocks)):
            i = wstate["issued"]
            tname, l, segs = all_blocks[i]
            slot = i % NRING
            for (pos, ncol, src) in wsrc(tname, l, segs):
                P.dma("pool", "w%d" % slot, wring[slot][:, :, pos:pos + ncol], src, writes=[("w", slot)] if pos == 0 else [])
                if pos != 0:
                    P.res[("w", slot)][0] = (P.sem("d_w%d" % slot), 16 * P.dma_cnt[P.sem("d_w%d" % slot)])
            wstate["issued"] += 1

    def next_block(first=True):
        if first:
            issue_weights(wstate["cur"] + NRING)
        i = wstate["cur"]
        wstate["cur"] += 1
        slot = i % NRING
        return wring[slot], ("w", slot)

    def load_group(gi, tiles):
        for ti, t in enumerate(tiles):
            if t["kind"] == "p":
                for tb in range(4):
                    P.dma("sp", "xl%d" % tb, mb_f32[:, tb, :], xp_d.ap()[t["seq0"] + tb * 128:t["seq0"] + (tb + 1) * 128, :],
                          writes=[("mbstg", tb)] + (MB_ALL if tb == 0 else []))
                for c in range(NCH):
                    b = P.bank()
                    pe([tp(banks[b][:, tb * 128:(tb + 1) * 128], mb_f32[:, tb, c * 128:(c + 1) * 128], ident[:]) for tb in range(4)],
                       [("mbstg", tb) for tb in range(4)] + [("ident",)], [BK(b)])
                    act(lambda e, b=b, c=c, t=t: e.activation(out=xT[:, c, t["off"]:t["off"] + 512], in_=banks[b][:, :], func=AF.Copy),
                        [BK(b)], [("xT", c, ti)])
            else:
                s_, sk = sg()
                P.dma("sp", sk[0] + str(sk[1]), s_[0:NSTOK, :], xs_d.ap(), writes=[sk])
                b = P.bank()
                pe([tp(banks[b][:, c * 64:(c + 1) * 64], s_[0:NSTOK, c * 128:(c + 1) * 128], ident[0:NSTOK, 0:NSTOK]) for c in range(NCH)],
                   [sk, ("ident",)], [BK(b)])
                act(lambda e, b=b, t=t: e.activation(out=xT[:, :, t["off"]:t["off"] + 64],
                                                      in_=banks[b][:, :].rearrange("p (c t) -> p c t", c=NCH), func=AF.Copy),
                    [BK(b)], [("xT", c, ti) for c in range(NCH)])

    def rms_stats(tiles):
        for ti, t in enumerate(tiles):
            rms_tile(ti, t)

    def rms_tile(ti, t):
        if True:
            off, n = t["off"], t["n"]
            b = P.bank()
            for c in range(NCH):
                q, qk = tb_()
                act(lambda e, q=q, c=c, off=off, n=n: e.activation(out=q[:, 0:n], in_=xT[:, c, off:off + n], func=AF.Square),
                    [("xT", c, ti)], [qk])
                pe(mm(banks[b][:, 0:n], onesb[:, :], q[:, 0:n], c == 0, c == NCH - 1), [qk, ("onesb",)], [BK(b)])
            act(lambda e, b=b, off=off, n=n: e.activation(out=stats[:, 0, off:off + n], in_=banks[b][:, 0:n], func=AF.Sqrt,
                                                        bias=epsc[:, 0:1], scale=1.0 / D),
                [BK(b), ("epsc",)], [("st", 0, ti)])
            dve(lambda e, off=off, n=n: e.reciprocal(out=stats[:, 0, off:off + n], in_=stats[:, 0, off:off + n]),
                [("st", 0, ti)], [("st", 0, ti)])

    def setup_pass(gi, l, tiles, part):
        has_s = any(t["kind"] == "s" for t in tiles)
        if part == "a":
            s_, sk = sg()
            P.dma("sp", sk[0] + str(sk[1]), s_[0:NR, :], vecs_d.ap()[l, :, :], writes=[sk])
            b = P.bank()
            pe([tp(banks[b][:, c * NR:(c + 1) * NR], s_[0:NR, c * 128:(c + 1) * 128], ident[0:NR, 0:NR]) for c in range(NCH)],
               [sk, ("ident",)], [BK(b)])
            dve(lambda e, b=b: e.tensor_copy(out=colv[:, :, :], in_=banks[b][:, 0:NCH * NR].rearrange("p (c r) -> p c r", c=NCH)),
                [BK(b)], [("colv",)])
            P.dma("pool", "wst", wst[:, :], wst_d.ap()[l, :, :], writes=[("wst",)])
            phase_mark()
        if part == "b":
            s_, sk = sg()
            sv = s_[:, :].rearrange("p (g j) -> p g j", g=NCH)
            P.dma("sp", sk[0] + str(sk[1]), sv, ws_d.ap()[l].rearrange("g i j -> i g j"), writes=[sk])
            dve(lambda e, sv=sv: e.tensor_tensor(out=sv, in0=sv, in1=tril[:, :].unsqueeze(1).broadcast_to([128, NCH, 128]), op=ALU.mult),
                [sk, ("tril",)], [sk])
            for h in range(2):
                b = P.bank()
                pe([tp(banks[b][:, gg * 128:(gg + 1) * 128], sv[:, h * 4 + gg, :], ident[:]) for gg in range(4)],
                   [sk, ("ident",)], [BK(b)])
                dve(lambda e, b=b, h=h: e.tensor_copy(out=WmTf[:, h * 4:(h + 1) * 4, :], in_=banks[b][:, :].rearrange("p (g i) -> p g i", g=4)),
                    [BK(b)], [("WmTf", h)] + ((MB_ALL + [("mbstg", 0)]) if h == 0 else []))
                act(lambda e, b=b, h=h: e.activation(out=WmTb[:, h * 4:(h + 1) * 4, :], in_=banks[b][:, :].rearrange("p (g i) -> p g i", g=4), func=AF.Copy),
                    [BK(b)], [("WmTb", h)])
            phase_mark()
            if has_s:
                b = P.bank()
                pe([mm(banks[b][0:4, gg * 64:(gg + 1) * 64], sv[0:4, gg, 0:4], rep[0:4, :], True, True) for gg in range(NCH)],
                   [sk, ("rep",)], [BK(b)])
                Zs, zsk = tf()
                dve(lambda e, b=b, Zs=Zs: e.tensor_copy(out=Zs[0:4, :], in_=banks[b][0:4, :]), [BK(b)], [zsk])
                b = P.bank()
                pe(mm(banks[b][0:64, :], rep[0:4, :], Zs[0:4, :], True, True), [zsk, ("rep",)], [BK(b)])
                dve(lambda e, b=b: e.tensor_tensor(out=BD[:, :, :], in0=banks[b][0:64, :].rearrange("p (g q) -> p g q", g=NCH),
                                                   in1=bmask[:, :].unsqueeze(1).broadcast_to([64, NCH, 64]), op=ALU.mult),
                    [BK(b), ("bmask",)], [("BD",)])
            phase_mark()
            s2_, sk2 = sg()
            P.dma("sp", sk2[0] + str(sk2[1]), s2_[:, :], vecs_d.ap()[l, R_LNBB, :].partition_broadcast(128), writes=[sk2])
            bsrow, bsk = sg()
            P.dma("sp", bsk[0] + str(bsk[1]), bsrow[:, :], bs_d.ap()[l, :].partition_broadcast(128), writes=[bsk])
            for h in range(2):
                b = P.bank()
                fns = []
                for gg in range(4):
                    g_ = h * 4 + gg
                    fns.append(mm(banks[b][:, gg * 128:(gg + 1) * 128], s2_[:, g_ * 128:(g_ + 1) * 128], WmTf[:, g_, :], True, True))
                pe(fns, [sk2, ("WmTf", h)], [BK(b)])
                dve(lambda e, b=b, h=h, bsrow=bsrow: e.tensor_tensor(out=BiasB[:, h * 4:(h + 1) * 4, :], in0=banks[b][:, :].rearrange("p (g i) -> p g i", g=4),
                                                                  in1=bsrow[:, h * 512:(h + 1) * 512].rearrange("p (g i) -> p g i", g=4), op=ALU.add),
                    [BK(b), bsk], [("BiasB", h)])
            phase_mark()
        if part == "c" and has_s:
            rows = NS * (KA - 1)
            srcA = sca_d.ap()[l].rearrange("s r c -> (s r) c")
            for blk in range(4):
                r0 = blk * 128
                nr_ = min(128, rows - r0)
                s_, sk = sg()
                P.dma("sp", sk[0] + str(sk[1]), s_[0:nr_, :], srcA[r0:r0 + nr_, :], writes=[sk])
                for s_i in range(r0 // 30, (r0 + nr_ - 1) // 30 + 1):
                    ra = max(r0, s_i * 30 + ST)
                    rb = min(r0 + nr_, (s_i + 1) * 30)
                    if ra < rb:
                        out_evs.append(P.dma("sp", "o_" + sk[0] + str(sk[1]), nas_d.ap()[l, s_i, ra - s_i * 30 - ST:rb - s_i * 30 - ST, :],
                                             s_[ra - r0:rb - r0, :], reads=[sk]))
                for h in range(2):
                    b = P.bank()
                    pe([tp(banks[b][:, cc * 128:cc * 128 + nr_], s_[0:nr_, (h * 4 + cc) * 128:(h * 4 + cc + 1) * 128], ident[0:nr_, 0:nr_]) for cc in range(4)],
                       [sk, ("ident",)], [BK(b)])
                    s_lo = r0 // 30
                    s_hi = (r0 + nr_ - 1) // 30
                    for s_i in range(s_lo, s_hi + 1):
                        ra = max(r0, s_i * 30)
                        rb = min(r0 + nr_, (s_i + 1) * 30)
                        act(lambda e, b=b, h=h, s_i=s_i, ra=ra, rb=rb, r0=r0: e.activation(
                            out=abs_[:, h * 4:(h + 1) * 4, s_i * 34 + (ra - s_i * 30):s_i * 34 + (rb - s_i * 30)],
                            in_=banks[b][:, :].rearrange("p (c t) -> p c t", c=4)[:, :, ra - r0:rb - r0], func=AF.Copy),
                            [BK(b)], [("abs", c) for c in range(h * 4, h * 4 + 4)])
            phase_mark()
            s_, sk = sg()
            P.dma("sp", sk[0] + str(sk[1]), s_[0:NS * 2, :], scc_d.ap()[l].rearrange("s r c -> (s r) c"), writes=[sk])
            b = P.bank()
            pe([tp(banks[b][:, c * 32:(c + 1) * 32], s_[0:32, c * 128:(c + 1) * 128], ident[0:32, 0:32]) for c in range(NCH)],
               [sk, ("ident",)], [BK(b)])
            act(lambda e, b=b: e.activation(out=cis_[:, :, :].rearrange("p c (s r) -> p c s r", s=NS)[:, :, :, 0:2],
                                            in_=banks[b][:, 0:256].rearrange("p (c s r) -> p c s r", c=NCH, s=NS), func=AF.Copy),
                [BK(b)], [("cis", c) for c in range(NCH)])

    def grp(W, wk, col0, ti, t, srcbuf=None, srckey="xn"):
        off, n = t["off"], t["n"]
        b = P.bank()
        src = xn if srcbuf is None else srcbuf
        pe([mm(banks[b][:, 0:n], W[:, k, col0:col0 + 128], src[:, k, off:off + n], k == 0, k == NCH - 1) for k in range(NCH)],
           [wk] + [(srckey, k, ti) for k in range(NCH)], [BK(b)])
        return b

    def run_pass(gi, l, tiles):
        last_group = (gi == 1)
        setup_pass(gi, l, tiles, "a")
        phase_mark()
        phase_mark()
        for ti, t in enumerate(tiles):
            off, n = t["off"], t["n"]
            for c in range(NCH):
                dve(lambda e, c=c, off=off, n=n: e.scalar_tensor_tensor(
                    out=xn[:, c, off:off + n], in0=xT[:, c, off:off + n], scalar=colv[:, c, R_NORMG:R_NORMG + 1],
                    in1=stats[:, 0, off:off + n], op0=ALU.mult, op1=ALU.mult),
                    [("xT", c, ti), ("colv",), ("st", 0, ti)], [("xn", c, ti)])

        def a_glu(c, W, wk, cc):
            slot = P.rotate("ab", NAB)
            abk = ("ab", slot)
            if gi == 0:
                pool(lambda e, slot=slot: e.memset(ab[:, slot, 0:KA - 1], 0.0), [], [abk])
            else:
                pool(lambda e, slot=slot, c=c: e.tensor_copy(out=ab[:, slot, 0:KA - 1], in_=ahalo[:, l, c, :]), [("ahalo", l, c)], [abk])
            for ti, t in enumerate(tiles):
                off, n = t["off"], t["n"]
                bg = grp(W, wk, 256 + cc * 128, ti, t)
                sgm, sgk = tf()
                act(lambda e, bg=bg, sgm=sgm, n=n: e.activation(out=sgm[:, 0:n], in_=banks[bg][:, 0:n], func=AF.Sigmoid), [BK(bg)], [sgk])
                bv = grp(W, wk, cc * 128, ti, t)
                if t["kind"] == "p":
                    dve(lambda e, bv=bv, sgm=sgm, slot=slot, off=off, n=n: e.tensor_tensor(
                        out=ab[:, slot, KA - 1 + off:KA - 1 + off + n], in0=banks[bv][:, 0:n], in1=sgm[:, 0:n], op=ALU.mult),
                        [BK(bv), sgk], [abk])
                    if last_group and ti == len(tiles) - 1:
                        dve(lambda e, bv=bv, sgm=sgm, c=c, n=n: e.tensor_tensor(
                            out=aslabp[:, c, 0:KA - 1], in0=banks[bv][:, n - (KA - 1):n], in1=sgm[:, n - (KA - 1):n], op=ALU.mult),
                            [BK(bv), sgk], [("aslabp", c)])
                else:
                    dve(lambda e, bv=bv, sgm=sgm, c=c: e.tensor_tensor(
                        out=abs_[:, c, :].rearrange("p (s r) -> p s r", s=NS)[:, :, KA - 1:KA - 1 + ST],
                        in0=banks[bv][:, 0:NSTOK].rearrange("p (s t) -> p s t", s=NS),
                        in1=sgm[:, 0:NSTOK].rearrange("p (s t) -> p s t", s=NS), op=ALU.mult),
                        [BK(bv), sgk], [("abs", c)])
                    dve(lambda e, bv=bv, sgm=sgm, c=c: e.tensor_tensor(
                        out=aslab[:, c, 0:NSTOK], in0=banks[bv][:, 0:NSTOK], in1=sgm[:, 0:NSTOK], op=ALU.mult),
                        [BK(bv), sgk], [("aslab", c)])
            if gi == 0 and len(groups) > 1:
                pool(lambda e, slot=slot, c=c: e.tensor_copy(out=ahalo[:, l, c, :], in_=ab[:, slot, 1024:1024 + KA - 1]), [abk], [("ahalo", l, c)])
            return slot

        def a_diag(c, slot):
            sbuf_i = c % 2
            for j in range(4):
                for r in range(4):
                    P.dma("sp", "rs%d" % (sbuf_i * 16 + j * 4 + r), stk[r * 32:(r + 1) * 32, sbuf_i, j, 0:1052], ab[j * 32:(j + 1) * 32, slot, r:r + 1052],
                          reads=[("ab", slot)], writes=[("stk", sbuf_i, j, r)])
            dve(lambda e, c=c: e.tensor_tensor(out=sdiag[:, c % 2, :, :], in0=identst[:, :].unsqueeze(1).broadcast_to([128, 32, 32]),
                                               in1=wst[:, c * 32:(c + 1) * 32].unsqueeze(2).broadcast_to([128, 32, 32]), op=ALU.mult),
                [("identst",), ("wst",)], [("sdiag", c % 2)])

        def ln_acc(c, ti, off, n, sq, sqk):
            if c == 0:
                dve(lambda e, sq=sq, off=off, n=n: e.tensor_copy(out=stats[:, 1, off:off + n], in_=sq[:, 0:n]), [sqk], [("st", 1, ti)])
            else:
                dve(lambda e, sq=sq, off=off, n=n: e.tensor_tensor(out=stats[:, 1, off:off + n], in0=sq[:, 0:n], in1=stats[:, 1, off:off + n], op=ALU.add),
                    [sqk, ("st", 1, ti)], [("st", 1, ti)])

        def a_conv(c, slot):
            for ti, t in enumerate(tiles):
                off, n = t["off"], t["n"]
                if t["kind"] == "p":
                    b = P.bank()
                    pe([(lambda e, b=b, q=q, j=j, off=off, n=n: e.matmul(banks[b][32 * j:32 * j + 32, 0:n], lhsT=sdiag[:, c % 2, q * 4 + j, :],
                                                                         rhs=stk[:, c % 2, j, off + 4 * q:off + 4 * q + n], start=(q == 0), stop=(q == 7),
                                                                         tile_position=(0, 32 * j)))
                        for q in range(8) for j in range(4)],
                       [("sdiag", c % 2)] + [("stk", c % 2, j, r) for j in range(4) for r in range(4)], [BK(b)])
                    act(lambda e, b=b, c=c, off=off, n=n: e.activation(out=S2[:, c, off:off + n], in_=banks[b][:, 0:n], func=AF.Identity,
                                                                     bias=colv[:, c, R_BCA:R_BCA + 1], scale=1.0),
                        [BK(b), ("colv",)], [("s2", c, ti)])
                    sq, sqk = tf()
                    act(lambda e, b=b, c=c, sq=sq, n=n: e.activation(out=sq[:, 0:n], in_=banks[b][:, 0:n], func=AF.Square,
                                                                   bias=colv[:, c, R_BCA:R_BCA + 1], scale=1.0),
                        [BK(b), ("colv",)], [sqk])
                    ln_acc(c, ti, off, n, sq, sqk)
                else:
                    av = abs_[:, c, :].rearrange("p (s r) -> p s r", s=NS)
                    acc, acck = tf()
                    accs = [acc[:, 0:NSTOK].rearrange("p (s t) -> p s t", s=NS), acc[:, NSTOK:2 * NSTOK].rearrange("p (s t) -> p s t", s=NS)]
                    hk = [("acch", acck[1], 0), ("acch", acck[1], 1)]
                    for k in range(KA):
                        h_ = k % 2
                        if k < 2:
                            dve(lambda e, h_=h_, av=av, c=c, k=k: e.tensor_scalar(out=accs[h_], in0=av[:, :, k:k + ST], scalar1=colv[:, c, R_WCA + k:R_WCA + k + 1],
                                                                                 scalar2=None, op0=ALU.mult),
                                [("abs", c), ("colv",)], [hk[h_], acck])
                        else:
                            dve(lambda e, h_=h_, av=av, c=c, k=k: e.scalar_tensor_tensor(
                                out=accs[h_], in0=av[:, :, k:k + ST], scalar=colv[:, c, R_WCA + k:R_WCA + k + 1], in1=accs[h_], op0=ALU.mult, op1=ALU.add),
                                [("abs", c), ("colv",), hk[h_]], [hk[h_]])
                    dve(lambda e: e.tensor_tensor(out=accs[0], in0=accs[0], in1=accs[1], op=ALU.add), [hk[0], hk[1]], [hk[0], acck])
                    act(lambda e, acc=acc, c=c, off=off, n=n: e.activation(out=S2[:, c, off:off + n], in_=acc[:, 0:n], func=AF.Identity,
                                                                         bias=colv[:, c, R_BCA:R_BCA + 1], scale=1.0),
                        [acck, ("colv",)], [("s2", c, ti)])
                    sq, sqk = tf()
                    act(lambda e, acc=acc, c=c, sq=sq, n=n: e.activation(out=sq[:, 0:n], in_=acc[:, 0:n], func=AF.Square,
                                                                     bias=colv[:, c, R_BCA:R_BCA + 1], scale=1.0),
                        [acck, ("colv",)], [sqk])
                    ln_acc(c, ti, off, n, sq, sqk)

        hist = []
        for c in range(NCH):
            if c % 2 == 0:
                W, wk = next_block()
            if len(hist) >= 1:
                a_diag(*hist[-1])
            slot = a_glu(c, W, wk, c % 2)
            if c == 0:
                setup_pass(gi, l, tiles, "c")
            if len(hist) >= 2:
                a_conv(*hist[-2])
            hist.append((c, slot))
        a_diag(*hist[-1])
        a_conv(*hist[-2])
        setup_pass(gi, l, tiles, "b")
        a_conv(*hist[-1])
        phase_mark()
        if last_group:
            emit_rows_out(aslabp, KA - 1, [("aslabp", c) for c in range(NCH)], nap_d.ap()[l, :, :], KA - 1)
        if any(t["kind"] == "s" for t in tiles):
            emit_rows_out(aslab, NSTOK, [("aslab", c) for c in range(NCH)], None, NSTOK, sample_a_layer=l)

        phase_mark()
        for ti, t in enumerate(tiles):
            off, n = t["off"], t["n"]
            bs_ = P.bank()
            bq = P.bank()
            pe([mm(banks[bs_][:, 0:n], onesb[:, :], S2[:, c, off:off + n], c == 0, c == NCH - 1) for c in range(NCH)],
               [("s2", c, ti) for c in range(NCH)] + [("onesb",)], [BK(bs_)])
            pe(mm(banks[bq][:, 0:n], ones32[:, :], stats[:, 1, off:off + n], True, True), [("st", 1, ti), ("ones32",)], [BK(bq)])
            dve(lambda e, bs_=bs_, off=off, n=n: e.tensor_scalar(out=stats[:, 0, off:off + n], in0=banks[bs_][:, 0:n], scalar1=1.0 / D, scalar2=None, op0=ALU.mult),
                [BK(bs_)], [("st", 0, ti)])
            m2, m2k = tf()
            dve(lambda e, m2=m2, off=off, n=n: e.tensor_tensor(out=m2[:, 0:n], in0=stats[:, 0, off:off + n], in1=stats[:, 0, off:off + n], op=ALU.mult),
                [("st", 0, ti)], [m2k])
            dve(lambda e, m2=m2, bq=bq, n=n: e.scalar_tensor_tensor(out=m2[:, 0:n], in0=banks[bq][:, 0:n], scalar=1.0 / D, in1=m2[:, 0:n],
                                                                   op0=ALU.mult, op1=ALU.subtract),
                [BK(bq), m2k], [m2k])
            act(lambda e, m2=m2, off=off, n=n: e.activation(out=stats[:, 1, off:off + n], in_=m2[:, 0:n], func=AF.Sqrt, bias=epsc[:, 1:2], scale=1.0),
                [m2k, ("epsc",)], [("st", 1, ti)])
            dve(lambda e, off=off, n=n: e.reciprocal(out=stats[:, 1, off:off + n], in_=stats[:, 1, off:off + n]), [("st", 1, ti)], [("st", 1, ti)])

        def c_mul(c, W, wk, cc):
            slot = P.rotate("ab", NAB)
            abk = ("ab", slot)
            if gi == 0:
                pool(lambda e, slot=slot: e.memset(ab[:, slot, 0:KC - 1], 0.0), [], [abk])
            else:
                pool(lambda e, slot=slot, c=c: e.tensor_copy(out=ab[:, slot, 0:KC - 1], in_=chalo[:, l, c, :]), [("chalo", l, c)], [abk])
            for ti, t in enumerate(tiles):
                off, n = t["off"], t["n"]
                bg = grp(W, wk, cc * 128, ti, t)
                gcm, gck = tf()
                act(lambda e, bg=bg, gcm=gcm, n=n: e.activation(out=gcm[:, 0:n], in_=banks[bg][:, 0:n], func=AF.Copy), [BK(bg)], [gck])
                bh = grp(W, wk, 256 + cc * 128, ti, t)
                if t["kind"] == "p":
                    dve(lambda e, bh=bh, gcm=gcm, slot=slot, off=off, n=n: e.tensor_tensor(
                        out=ab[:, slot, KC - 1 + off:KC - 1 + off + n], in0=banks[bh][:, 0:n], in1=gcm[:, 0:n], op=ALU.mult), [BK(bh), gck], [abk])
                    if last_group and ti == len(tiles) - 1:
                        dve(lambda e, bh=bh, gcm=gcm, c=c, n=n: e.tensor_tensor(
                            out=cslabp[:, c, 0:KC - 1], in0=banks[bh][:, n - (KC - 1):n], in1=gcm[:, n - (KC - 1):n], op=ALU.mult),
                            [BK(bh), gck], [("cslabp", c)])
                else:
                    dve(lambda e, bh=bh, gcm=gcm, c=c: e.tensor_tensor(
                        out=cis_[:, c, :].rearrange("p (s r) -> p s r", s=NS)[:, :, KC - 1:KC - 1 + ST],
                        in0=banks[bh][:, 0:NSTOK].rearrange("p (s t) -> p s t", s=NS),
                        in1=gcm[:, 0:NSTOK].rearrange("p (s t) -> p s t", s=NS), op=ALU.mult), [BK(bh), gck], [("cis", c)])
                    dve(lambda e, bh=bh, gcm=gcm, c=c: e.tensor_tensor(
                        out=cslab[:, c, 0:NS * 2].rearrange("p (s r) -> p s r", s=NS),
                        in0=banks[bh][:, 0:NSTOK].rearrange("p (s t) -> p s t", s=NS)[:, :, ST - 2:ST],
                        in1=gcm[:, 0:NSTOK].rearrange("p (s t) -> p s t", s=NS)[:, :, ST - 2:ST], op=ALU.mult), [BK(bh), gck], [("cslab", c)])
            if gi == 0 and len(groups) > 1:
                pool(lambda e, slot=slot, c=c: e.tensor_copy(out=chalo[:, l, c, :], in_=ab[:, slot, 1024:1024 + KC - 1]), [abk], [("chalo", l, c)])
            return slot

        def c_diag(c):
            act([lambda e, k=k, c=c: e.activation(out=diagC[:, k, :], in_=identb[:, :], func=AF.Identity, scale=colv[:, c, R_WCC + k:R_WCC + k + 1])
                 for k in range(KC)], [("identb",), ("colv",)], [("diagC",)])

        def c_conv(c, slot):
            abk = ("ab", slot)
            for ti, t in enumerate(tiles):
                off, n = t["off"], t["n"]
                b = P.bank()
                if t["kind"] == "p":
                    pe([mm(banks[b][:, 0:n], diagC[:, k, :], ab[:, slot, off + k:off + k + n], k == 0, k == KC - 1) for k in range(KC)],
                       [("diagC",), abk], [BK(b)])
                else:
                    cv = cis_[:, c, :].rearrange("p (s r) -> p s r", s=NS)
                    pe([mm(banks[b][:, 0:NSTOK].rearrange("p (s t) -> p s t", s=NS), diagC[:, k, :], cv[:, :, k:k + ST], k == 0, k == KC - 1) for k in range(KC)],
                       [("diagC",), ("cis", c)], [BK(b)])
                act(lambda e, b=b, c=c, off=off, n=n: e.activation(out=S2[:, c, off:off + n], in_=banks[b][:, 0:n], func=AF.Copy), [BK(b)], [("s2", c, ti)])

        def za_stage1(W, wk, cc, c, ti, t):
            off, n = t["off"], t["n"]
            bz = grp(W, wk, cc * 128, ti, t)
            sz, szk = tf()
            act(lambda e, bz=bz, sz=sz, n=n: e.activation(out=sz[:, 0:n], in_=banks[bz][:, 0:n], func=AF.Silu), [BK(bz)], [szk])
            t1, t1k = tf()
            dve(lambda e, t1=t1, c=c, off=off, n=n: e.tensor_tensor(out=t1[:, 0:n], in0=S2[:, c, off:off + n], in1=stats[:, 0, off:off + n], op=ALU.subtract),
                [("s2", c, ti), ("st", 0, ti)], [t1k])
            P.op(ZA_MUL_ENG, lambda e, t1=t1, off=off, n=n: e.tensor_tensor(out=t1[:, 0:n], in0=t1[:, 0:n], in1=stats[:, 1, off:off + n], op=ALU.mult),
                 [t1k, ("st", 1, ti)], [t1k])
            return (c, ti, t, sz, szk, t1, t1k)

        def za_stage2(c, ti, t, sz, szk, t1, t1k):
            off, n = t["off"], t["n"]
            act(lambda e, t1=t1, c=c, n=n: e.activation(out=t1[:, 0:n], in_=t1[:, 0:n], func=AF.Silu,
                                                      scale=colv[:, c, R_LNAG:R_LNAG + 1], bias=colv[:, c, R_LNAB:R_LNAB + 1]),
                [t1k, ("colv",)], [t1k])
            dve(lambda e, t1=t1, sz=sz, c=c, off=off, n=n: e.tensor_tensor(out=S2[:, c, off:off + n], in0=t1[:, 0:n], in1=sz[:, 0:n], op=ALU.mult),
                [t1k, szk], [("s2", c, ti)])

        pend = None
        c_early = []
        for jb in range(2):
            W, wk = next_block()
            if jb == 1:
                Wc0, wck0 = next_block(False)
            for cc in range(4):
                c = jb * 4 + cc
                for ti, t in enumerate(tiles):
                    u = za_stage1(W, wk, cc, c, ti, t)
                    if pend is not None:
                        za_stage2(*pend)
                    pend = u
                if jb == 1 and cc in (1, 3):
                    za_stage2(*pend)
                    pend = None
                    c_early.append((cc // 2, c_mul(cc // 2, Wc0, wck0, cc // 2)))
        if pend is not None:
            za_stage2(*pend)
        phase_mark()
        proj_phase(0, l, tiles)
        phase_mark()

        Wv0, wvk0 = next_block()
        Wv1, wvk1 = next_block(False)
        vblocks = []
        for ti, t in enumerate(tiles):
            for tbi in range(max(1, t["n"] // 128)):
                vblocks.append((ti, t, tbi))

        vcnt = [0]

        def vmm(ti, t, tbi):
            off, n = t["off"], t["n"]
            m_ = min(128, n)
            tok0 = off + tbi * 128
            b0 = 2 * (vcnt[0] % 3)
            b1 = b0 + 1
            vcnt[0] += 1
            pe([mm(banks[b0][0:m_, :], xn[:, k, tok0:tok0 + m_], Wv0[:, k, :], k == 0, k == NCH - 1) for k in range(NCH)],
               [wvk0] + [("xn", k, ti) for k in range(NCH)], [BK(b0)])
            pe([mm(banks[b1][0:m_, :], xn[:, k, tok0:tok0 + m_], Wv1[:, k, :], k == 0, k == NCH - 1) for k in range(NCH)],
               [wvk1] + [("xn", k, ti) for k in range(NCH)], [BK(b1)])
            return b0, b1

        def vrestA(ti, t, tbi, b0, b1):
            off, n = t["off"], t["n"]
            m_ = min(128, n)
            tok0 = off + tbi * 128
            ri = P.rotate("mvr", 3)
            mvv = mvr[:, ri, :]
            bstv = bstr[:, ri, :, :]
            dve([lambda e, b0=b0, m_=m_: e.bn_stats(out=bstv[0:m_, 0, :], in_=banks[b0][0:m_, :]),
                 lambda e, b1=b1, m_=m_: e.bn_stats(out=bstv[0:m_, 1, :], in_=banks[b1][0:m_, :])], [BK(b0), BK(b1)], [("bst", ri)])
            dve(lambda e, m_=m_: e.bn_aggr(out=mvv[0:m_, 0:2], in_=bstv[0:m_, :, :].rearrange("p a b -> p (a b)")), [("bst", ri)], [("mv", ri, 0)])
            act(lambda e, m_=m_: e.activation(out=mvv[0:m_, 2:3], in_=mvv[0:m_, 1:2], func=AF.Sqrt, bias=epsc[0:m_, 1:2], scale=1.0),
                [("mv", ri, 0), ("epsc",)], [("mv", ri, 1)])
            return (ti, t, tbi, b0, b1, ri)

        def vrestB(ti, t, tbi, b0, b1, ri):
            off, n = t["off"], t["n"]
            m_ = min(128, n)
            tok0 = off + tbi * 128
            mvv = mvr[:, ri, :]
            bstv = bstr[:, ri, :, :]
            dve(lambda e, m_=m_: e.reciprocal(out=mvv[0:m_, 2:3], in_=mvv[0:m_, 2:3]), [("mv", ri, 1)], [("mv", ri, 1)])
            dve(lambda e, m_=m_: e.tensor_scalar(out=mvv[0:m_, 3:4], in0=mvv[0:m_, 0:1], scalar1=mvv[0:m_, 2:3], scalar2=-1.0, op0=ALU.mult, op1=ALU.mult),
                [("mv", ri, 0), ("mv", ri, 1)], [("mv", ri, 2)])
            vi = P.rotate("vn", 2)
            vt = vn[vi]
            vk = ("vn", vi)
            for hh, bb in ((0, b0), (1, b1)):
                act(lambda e, vt=vt, hh=hh, bb=bb, m_=m_: e.activation(out=vt[0:m_, hh * 512:(hh + 1) * 512], in_=banks[bb][0:m_, :], func=AF.Identity,
                                                                  scale=mvv[0:m_, 2:3], bias=mvv[0:m_, 3:4]),
                    [BK(bb), ("mv", ri, 1), ("mv", ri, 2)], [vk])
            if t["kind"] == "s":
                so, sok = sg()
                for hh, bb in ((0, b0), (1, b1)):
                    act(lambda e, so=so, hh=hh, bb=bb, m_=m_: e.activation(out=so[0:m_, hh * 512:(hh + 1) * 512], in_=banks[bb][0:m_, :], func=AF.Identity,
                                                                      scale=mvv[0:m_, 2:3], bias=mvv[0:m_, 3:4]),
                        [BK(bb), ("mv", ri, 1), ("mv", ri, 2)], [sok])
                gb_, gbk = sg()
                P.dma("sp", gbk[0] + str(gbk[1]), gb_[0:NSTOK, :], vecs_d.ap()[l, R_LNBG, :].partition_broadcast(NSTOK), writes=[gbk])
                dve(lambda e, so=so, gb_=gb_: e.tensor_tensor(out=so[0:NSTOK, :], in0=so[0:NSTOK, :], in1=gb_[0:NSTOK, :], op=ALU.mult), [sok, gbk], [sok])
                gb2, gbk2 = sg()
                P.dma("sp", gbk2[0] + str(gbk2[1]), gb2[0:NSTOK, :], vecs_d.ap()[l, R_LNBB, :].partition_broadcast(NSTOK), writes=[gbk2])
                dve(lambda e, so=so, gb2=gb2: e.tensor_tensor(out=so[0:NSTOK, :], in0=so[0:NSTOK, :], in1=gb2[0:NSTOK, :], op=ALU.add), [sok, gbk2], [sok])
                out_evs.append(P.dma("sp", "o_" + sok[0] + str(sok[1]), nvs_d.ap()[l, :, :], so[0:NSTOK, :], reads=[sok]))
                b = 6
                pe([mm(banks[b][:, g_ * 64:(g_ + 1) * 64], vt[0:NSTOK, g_ * 128:(g_ + 1) * 128], BD[0:NSTOK, g_, :], True, True) for g_ in range(NCH)],
                   [vk, ("BD",)], [BK(b)])
                act(lambda e, b=b, off=off: e.activation(out=S2[:, :, off:off + NSTOK], in_=banks[b][:, :].rearrange("p (g q) -> p g q", g=NCH), func=AF.Copy),
                    [BK(b)], [("s2", g_, ti) for g_ in range(NCH)])
            else:
                for h in range(2):
                    b = 6 + h
                    pe([mm(banks[b][:, gg * 128:(gg + 1) * 128], vt[:, (h * 4 + gg) * 128:(h * 4 + gg + 1) * 128], WmTb[:, h * 4 + gg, :], True, True) for gg in range(4)],
                       [vk, ("WmTb", h)], [BK(b)])
                    act(lambda e, b=b, h=h, tok0=tok0: e.activation(out=S2[:, h * 4:(h + 1) * 4, tok0:tok0 + 128],
                                                                    in_=banks[b][:, :].rearrange("p (g i) -> p g i", g=4), func=AF.Copy),
                        [BK(b)], [("s2", h * 4 + gg, ti) for gg in range(4)])

        vq = [vmm(*vblocks[0])]
        if len(vblocks) > 1:
            vq.append(vmm(*vblocks[1]))
        vpend = None
        for i_, blk in enumerate(vblocks):
            b0_, b1_ = vq.pop(0)
            if vpend is not None:
                vrestB(*vpend)
            if i_ + 2 < len(vblocks):
                vq.append(vmm(*vblocks[i_ + 2]))
            vpend = vrestA(blk[0], blk[1], blk[2], b0_, b1_)
        vrestB(*vpend)
        for jb in range(2):
            Wz, wzk = next_block()
            Wu, wuk = next_block(False)
            for cc in range(4):
                c = jb * 4 + cc
                for ti, t in enumerate(tiles):
                    off, n = t["off"], t["n"]
                    bz = grp(Wz, wzk, cc * 128, ti, t)
                    sz, szk = tf()
                    act(lambda e, bz=bz, sz=sz, n=n: e.activation(out=sz[:, 0:n], in_=banks[bz][:, 0:n], func=AF.Silu), [BK(bz)], [szk])
                    bu = grp(Wu, wuk, cc * 128, ti, t)
                    dve(lambda e, bu=bu, sz=sz, n=n: e.tensor_tensor(out=sz[:, 0:n], in0=banks[bu][:, 0:n], in1=sz[:, 0:n], op=ALU.mult), [BK(bu), szk], [szk])
                    wv, wvk = tf()
                    if t["kind"] == "p":
                        dve(lambda e, wv=wv, c=c, off=off, n=n: e.scalar_tensor_tensor(
                            out=wv[:, 0:n].rearrange("p (a i) -> p a i", i=128), in0=S2[:, c, off:off + n].rearrange("p (a i) -> p a i", i=128),
                            scalar=colv[:, c, R_LNBG:R_LNBG + 1], in1=BiasB[:, c, :].unsqueeze(1).broadcast_to([128, n // 128, 128]),
                            op0=ALU.mult, op1=ALU.add),
                            [("s2", c, ti), ("colv",), ("BiasB", c // 4)], [wvk])
                    else:
                        dve(lambda e, wv=wv, c=c, off=off: e.scalar_tensor_tensor(
                            out=wv[:, 0:NSTOK].rearrange("p (s t) -> p s t", s=NS), in0=S2[:, c, off:off + NSTOK].rearrange("p (s t) -> p s t", s=NS),
                            scalar=colv[:, c, R_LNBG:R_LNBG + 1], in1=BiasB[:, c, 0:ST].unsqueeze(1).broadcast_to([128, NS, ST]),
                            op0=ALU.mult, op1=ALU.add),
                            [("s2", c, ti), ("colv",), ("BiasB", c // 4)], [wvk])
                    dve(lambda e, sz=sz, wv=wv, c=c, off=off, n=n: e.tensor_tensor(out=S2[:, c, off:off + n], in0=sz[:, 0:n], in1=wv[:, 0:n], op=ALU.mult),
                        [szk, wvk], [("s2", c, ti)])
        proj_phase(1, l, tiles)
        phase_mark()

        c_diag(c_early[0][0])
        c_conv(*c_early[0])
        prev = c_early[1]
        for c in range(2, NCH):
            if c % 2 == 0:
                W, wk = next_block()
            c_diag(prev[0])
            slot = c_mul(c, W, wk, c % 2)
            c_conv(*prev)
            prev = (c, slot)
        c_diag(prev[0])
        c_conv(*prev)
        if last_group:
            emit_rows_out(cslabp, KC - 1, [("cslabp", c) for c in range(NCH)], ncp_d.ap()[l, :, :], KC - 1)
        if any(t["kind"] == "s" for t in tiles):
            emit_rows_out(cslab, NS * 2, [("cslab", c) for c in range(NCH)], ncs_d.ap()[l, :, :], NS * 2)
        for jb in range(2):
            Wz, wzk = next_block()
            Wg, wgk = next_block(False)
            for cc in range(4):
                c = jb * 4 + cc
                for ti, t in enumerate(tiles):
                    off, n = t["off"], t["n"]
                    bz = grp(Wz, wzk, cc * 128, ti, t)
                    sz, szk = tf()
                    act(lambda e, bz=bz, sz=sz, n=n: e.activation(out=sz[:, 0:n], in_=banks[bz][:, 0:n], func=AF.Silu), [BK(bz)], [szk])
                    bg = grp(Wg, wgk, cc * 128, ti, t)
                    dve(lambda e, bg=bg, sz=sz, n=n: e.tensor_tensor(out=sz[:, 0:n], in0=banks[bg][:, 0:n], in1=sz[:, 0:n], op=ALU.mult), [BK(bg), szk], [szk])
                    dve(lambda e, sz=sz, c=c, off=off, n=n: e.tensor_tensor(out=S2[:, c, off:off + n], in0=sz[:, 0:n], in1=S2[:, c, off:off + n], op=ALU.mult),
                        [szk, ("s2", c, ti)], [("s2", c, ti)])
        proj_phase(2, l, tiles)
        phase_mark()

        Wo0, wok0 = next_block()
        Wo1, wok1 = next_block(False)
        for ti, t in enumerate(tiles):
            off, n = t["off"], t["n"]
            for j in range(NCH):
                W, wk = (Wo0, wok0) if j < 4 else (Wo1, wok1)
                cc = j % 4
                b = P.bank()
                pe([mm(banks[b][:, 0:n], W[:, k, cc * 128:(cc + 1) * 128], mview[:, k, off:off + n], k == 0, k == NCH - 1) for k in range(NCH)],
                   [wk] + [("m", k, ti) for k in range(NCH)], [BK(b)])
                dve(lambda e, b=b, j=j, off=off, n=n: e.tensor_tensor(out=xT[:, j, off:off + n], in0=banks[b][:, 0:n], in1=xT[:, j, off:off + n], op=ALU.add),
                    [BK(b), ("xT", j, ti)], [("xT", j, ti)])
            rms_tile(ti, t)

    def proj_phase(br, l, tiles):
        for jb in range(2):
            Wg, wgk = next_block()
            Wp, wpk = next_block(False)
            for cc in range(4):
                j = jb * 4 + cc
                for ti, t in enumerate(tiles):
                    off, n = t["off"], t["n"]
                    bg = grp(Wg, wgk, cc * 128, ti, t)
                    gt, gtk = tf()
                    act(lambda e, bg=bg, gt=gt, j=j, n=n: e.activation(out=gt[:, 0:n], in_=banks[bg][:, 0:n], func=AF.Sigmoid,
                                                                     bias=colv[:, j, R_BG + br:R_BG + br + 1], scale=1.0),
                        [BK(bg), ("colv",)], [gtk])
                    bp = grp(Wp, wpk, cc * 128, ti, t, srcbuf=S2, srckey="s2")
                    if br == 0:
                        dve(lambda e, bp=bp, gt=gt, j=j, off=off, n=n: e.tensor_tensor(out=mview[:, j, off:off + n], in0=banks[bp][:, 0:n], in1=gt[:, 0:n], op=ALU.mult),
                            [BK(bp), gtk], [("m", j, ti)])
                    else:
                        dve(lambda e, bp=bp, gt=gt, n=n: e.tensor_tensor(out=gt[:, 0:n], in0=banks[bp][:, 0:n], in1=gt[:, 0:n], op=ALU.mult), [BK(bp), gtk], [gtk])
                        dve(lambda e, gt=gt, j=j, off=off, n=n: e.tensor_tensor(out=mview[:, j, off:off + n], in0=gt[:, 0:n], in1=mview[:, j, off:off + n], op=ALU.add),
                            [gtk, ("m", j, ti)], [("m", j, ti)])

    def emit_rows_out(slab, nrows, keys, dst, nrows_dst, sample_a_layer=None):
        so, sok = sg()
        for h in range(2):
            b = P.bank()
            pe([tp(banks[b][0:nrows, cc * 128:(cc + 1) * 128], slab[:, h * 4 + cc, 0:nrows], ident[:, :]) for cc in range(4)],
               keys + [("ident",)], [BK(b)])
            act(lambda e, b=b, h=h, so=so: e.activation(out=so[0:nrows, h * 512:(h + 1) * 512], in_=banks[b][0:nrows, :], func=AF.Copy), [BK(b)], [sok])
        if sample_a_layer is None:
            out_evs.append(P.dma("sp", "o_" + sok[0] + str(sok[1]), dst, so[0:nrows, :], reads=[sok]))
        else:
            l = sample_a_layer
            for s_i in range(NS):
                out_evs.append(P.dma("sp", "o_" + sok[0] + str(sok[1]), nas_d.ap()[l, s_i, KA - 1 - ST:KA - 1, :], so[s_i * ST:(s_i + 1) * ST, :], reads=[sok]))

    def final_out(gi, tiles):
        for ti, t in enumerate(tiles):
            off, n = t["off"], t["n"]
            for c in range(NCH):
                dve(lambda e, c=c, off=off, n=n: e.scalar_tensor_tensor(
                    out=xT[:, c, off:off + n], in0=xT[:, c, off:off + n], scalar=colv[:, c, R_FING:R_FING + 1],
                    in1=stats[:, 0, off:off + n], op0=ALU.mult, op1=ALU.mult),
                    [("xT", c, ti), ("colv",), ("st", 0, ti)], [("xT", c, ti)])
            nblk = max(1, n // 128)
            for tbi in range(nblk):
                m_ = min(128, n)
                tok0 = off + tbi * 128
                oi = P.rotate("ost", 4)
                so = mb_f32[:, oi, :]
                sok = ("mbstg", oi)
                for h in range(2):
                    b = P.bank()
                    pe([tp(banks[b][0:m_, cc * 128:(cc + 1) * 128], xT[:, h * 4 + cc, tok0:tok0 + m_], ident[:, :]) for cc in range(4)],
                       [("xT", h * 4 + cc, ti) for cc in range(4)] + [("ident",)], [BK(b)])
                    act(lambda e, b=b, h=h, so=so, m_=m_: e.activation(out=so[0:m_, h * 512:(h + 1) * 512], in_=banks[b][0:m_, :], func=AF.Copy),
                        [BK(b)], [sok] + (MB_ALL if (h == 0 and tbi == 0 and ti == 0) else []))
                if t["kind"] == "p":
                    dst = yp_d.ap()[t["seq0"] + tbi * 128:t["seq0"] + (tbi + 1) * 128, :]
                else:
                    dst = ys_d.ap()[:, :]
                out_evs.append(P.dma("sp", "o_mb%d" % oi, dst, so[0:m_, :], reads=[sok]))

    class _Stop(Exception):
        pass

    def phase_mark():
        return None

    issue_weights(NRING)
    try:
        for gi, tiles in enumerate(groups):
            load_group(gi, tiles)
            rms_stats(tiles)
            phase_mark()
            for l in range(depth):
                run_pass(gi, l, tiles)
            final_out(gi, tiles)
    except _Stop:
        pass
    P.wait_all("sp", out_evs)
    P.emit()
    es.close()
    return nc


_NC_CACHE = {}


def kernel(x_prompt, x_sample, state_conv_a, state_conv_c, norm_g, w_in, b_gate,
           w_conv_a, b_conv_a, ln_a_g, ln_a_b, w_proj_a, ln_b_g, ln_b_b, w_s, b_s,
           w_proj_b, w_conv_c, w_proj_c, w_out, final_g):
    f = lambda a: np.ascontiguousarray(np.asarray(a, dtype=np.float32))
    x_prompt, x_sample, state_conv_a, state_conv_c = f(x_prompt), f(x_sample), f(state_conv_a), f(state_conv_c)
    w_in, w_proj_a, w_proj_b, w_proj_c, w_out = f(w_in), f(w_proj_a), f(w_proj_b), f(w_proj_c), f(w_out)
    vecs = np.zeros((DEPTH, NR, D), dtype=np.float32)
    vecs[:, R_NORMG] = f(norm_g)
    vecs[:, R_BG:R_BG + 3] = f(b_gate).reshape(DEPTH, 3, D)
    vecs[:, R_WCA:R_WCA + KA] = f(w_conv_a)
    vecs[:, R_BCA] = f(b_conv_a)
    vecs[:, R_LNAG] = f(ln_a_g)
    vecs[:, R_LNAB] = f(ln_a_b)
    vecs[:, R_LNBG] = f(ln_b_g)
    vecs[:, R_LNBB] = f(ln_b_b)
    vecs[:, R_WCC:R_WCC + KC] = f(w_conv_c)
    vecs[:, R_FING] = f(final_g)[None, :]
    w_s_ = f(w_s)
    wpad = np.zeros((DEPTH, 32, D), dtype=np.float32)
    wpad[:, :KA] = f(w_conv_a)
    wst = np.ascontiguousarray(wpad.reshape(DEPTH, 8, 4, NCH, 4, 32).transpose(0, 2, 5, 3, 1, 4).reshape(DEPTH, 128, 256))
    b_s_ = f(b_s).reshape(DEPTH, NCH * 128)

    if "nc" not in _NC_CACHE:
        _NC_CACHE["nc"] = build()
    nc = _NC_CACHE["nc"]
    in_maps = []
    for b in range(8):
        in_maps.append({
            "xp": x_prompt[b],
            "xs": np.ascontiguousarray(x_sample[b * NS:(b + 1) * NS].reshape(NSTOK, D)),
            "sca": np.ascontiguousarray(state_conv_a[:, b * NS:(b + 1) * NS]),
            "scc": np.ascontiguousarray(state_conv_c[:, b * NS:(b + 1) * NS]),
            "w_in": w_in, "w_proj_a": w_proj_a, "w_proj_b": w_proj_b, "w_proj_c": w_proj_c, "w_out": w_out,
            "vecs": vecs, "w_s": w_s_, "b_s": b_s_, "wst": wst,
        })
    res = run_bass_kernel_spmd(nc, in_maps, core_ids=list(range(8)))
    R = res.results
    y_prompt = np.stack([R[b]["yp"] for b in range(8)], axis=0)
    y_sample = np.concatenate([R[b]["ys"].reshape(NS, ST, D) for b in range(8)], axis=0)
    nap = np.stack([R[b]["nap"] for b in range(8)], axis=1)
    nas = np.concatenate([R[b]["nas"] for b in range(8)], axis=1)
    ncp = np.stack([R[b]["ncp"] for b in range(8)], axis=1)
    ncs = np.concatenate([R[b]["ncs"].reshape(DEPTH, NS, KC - 1, D) for b in range(8)], axis=1)
    nvs = np.concatenate([R[b]["nvs"].reshape(DEPTH, NS, ST, D) for b in range(8)], axis=1)
    return (y_prompt.astype(np.float32), y_sample.astype(np.float32), nap.astype(np.float32), nas.astype(np.float32),
            ncp.astype(np.float32), ncs.astype(np.float32), nvs.astype(np.float32))
```

```python
import numpy as np
from contextlib import ExitStack
import concourse.bass as bass
import concourse.mybir as mybir
from concourse.bass_utils import run_bass_kernel_spmd

F32 = mybir.dt.float32
BF16 = mybir.dt.bfloat16
AF = mybir.ActivationFunctionType
ALU = mybir.AluOpType

D = 1024
NCH = 8
DEPTH = 4
SEQ = 2048
NS = 16
ST = 4
NSTOK = NS * ST
KA = 31
KC = 3
D_IN = 13 * D
RMS_EPS = 1e-6
LN_EPS = 1e-5

R_NORMG = 0
R_BG = 1
R_WCA = 4
R_BCA = 35
R_LNAG = 36
R_LNAB = 37
R_LNBG = 38
R_LNBB = 39
R_WCC = 40
R_FING = 43
NR = 44

C_AVAL, C_AGATE, C_ZA, C_U, C_V, C_ZB, C_GB, C_GC, C_HC, C_ZC, C_G0, C_G1, C_G2 = [i * D for i in range(13)]

NRING = 4
ZA_MUL_ENG = "dve"


class Eng:
    def __init__(self, name, sem):
        self.name = name
        self.sem = sem
        self.count = 0
        self.ops = []
        self.known = {}


class Prog:
    def __init__(self, nc, es):
        self.nc = nc
        self.es = es
        self.sems = {}
        self.eng = {}
        for n in ("pe", "act", "dve", "pool", "sp"):
            self.eng[n] = Eng(n, self.sem("e_" + n))
        self.res = {}
        self.dma_cnt = {}
        self.nbank = 0
        self.rot = {}

    def sem(self, name):
        if name not in self.sems:
            self.sems[name] = self.es.enter_context(self.nc.semaphore(name))
        return name

    def _deps(self, reads, writes):
        deps = []
        for k in reads:
            st = self.res.get(k)
            if st is not None and st[0] is not None:
                deps.append(st[0])
            if st is not None and k[0] == "bk":
                deps.extend(st[1])
        for k in writes:
            st = self.res.get(k)
            if st is not None:
                if st[0] is not None:
                    deps.append(st[0])
                deps.extend(st[1])
        return deps

    def _commit(self, ev, reads, writes):
        for k in reads:
            st = self.res.setdefault(k, [None, []])
            st[1].append(ev)
            if len(st[1]) > 64:
                best = {}
                for (s, v) in st[1]:
                    if best.get(s, 0) < v:
                        best[s] = v
                st[1] = list(best.items())
        for k in writes:
            self.res[k] = [ev, []]

    def _waits(self, E, deps):
        need = {}
        for (s, v) in deps:
            if E.name == "pe" and s == E.sem:
                continue
            if E.known.get(s, 0) >= v:
                continue
            if need.get(s, 0) < v:
                need[s] = v
        for s, v in need.items():
            E.known[s] = v
        return list(need.items())

    def op(self, eng, fns, reads=(), writes=()):
        if not isinstance(fns, (list, tuple)):
            fns = [fns]
        E = self.eng[eng]
        waits = self._waits(E, self._deps(reads, writes))
        E.count += 1
        ev = (E.sem, E.count)
        E.ops.append((waits, list(fns), (E.sem, 1)))
        self._commit(ev, reads, writes)
        return ev

    def dma(self, q, semkey, out, in_, reads=(), writes=(), **kw):
        E = self.eng[q]
        s = self.sem("d_" + semkey)
        waits = self._waits(E, self._deps(reads, writes))
        self.dma_cnt[s] = self.dma_cnt.get(s, 0) + 1
        ev = (s, 16 * self.dma_cnt[s])
        E.ops.append((waits, [lambda e, out=out, in_=in_, kw=kw: e.dma_start(out=out, in_=in_, **kw)], (s, 16)))
        self._commit(ev, reads, writes)
        return ev

    def wait_all(self, eng, evs):
        E = self.eng[eng]
        waits = self._waits(E, evs)
        E.ops.append((waits, [], None))

    def bank(self):
        i = self.nbank % 8
        self.nbank += 1
        return i

    def rotate(self, name, n):
        i = self.rot.get(name, 0)
        self.rot[name] = i + 1
        return i % n

    def emit(self):
        nc = self.nc
        block = self.es.enter_context(nc.Block())
        sems = self.sems

        def run(E):
            def body(e):
                for waits, fns, inc in E.ops:
                    for (s, v) in waits:
                        e.wait_ge(sems[s], v)
                    last = None
                    for f in fns:
                        last = f(e)
                    if inc is not None and last is not None:
                        last.then_inc(sems[inc[0]], inc[1])
            return body

        block.tensor(run(self.eng["pe"]))
        block.scalar(run(self.eng["act"]))
        block.vector(run(self.eng["dve"]))
        block.gpsimd(run(self.eng["pool"]))
        block.sync(run(self.eng["sp"]))


def _consts():
    ident = np.eye(128, dtype=np.float32)
    tril = np.tril(np.ones((128, 128), dtype=np.float32))
    rep = np.zeros((4, 64), dtype=np.float32)
    for q in range(64):
        rep[q % 4, q] = 1.0
    bmask = np.zeros((64, 64), dtype=np.float32)
    for p in range(64):
        for q in range(64):
            if p // 4 == q // 4:
                bmask[p, q] = 1.0
    return ident, tril, rep, bmask


def build(depth=DEPTH, ngroups=2):
    nc = bass.Bass("TRN2", target_bir_lowering=False)
    dt = nc.dram_tensor
    xp_d = dt("xp", [SEQ, D], F32, kind="ExternalInput")
    xs_d = dt("xs", [NSTOK, D], F32, kind="ExternalInput")
    sca_d = dt("sca", [DEPTH, NS, KA - 1, D], F32, kind="ExternalInput")
    scc_d = dt("scc", [DEPTH, NS, KC - 1, D], F32, kind="ExternalInput")
    win_d = dt("w_in", [DEPTH, D, D_IN], F32, kind="ExternalInput")
    wpa_d = dt("w_proj_a", [DEPTH, D, D], F32, kind="ExternalInput")
    wpb_d = dt("w_proj_b", [DEPTH, D, D], F32, kind="ExternalInput")
    wpc_d = dt("w_proj_c", [DEPTH, D, D], F32, kind="ExternalInput")
    wo_d = dt("w_out", [DEPTH, D, D], F32, kind="ExternalInput")
    vecs_d = dt("vecs", [DEPTH, NR, D], F32, kind="ExternalInput")
    ws_d = dt("w_s", [DEPTH, NCH, 128, 128], F32, kind="ExternalInput")
    bs_d = dt("b_s", [DEPTH, NCH * 128], F32, kind="ExternalInput")
    wst_d = dt("wst", [DEPTH, 128, 256], F32, kind="ExternalInput")
    yp_d = dt("yp", [SEQ, D], F32, kind="ExternalOutput")
    ys_d = dt("ys", [NSTOK, D], F32, kind="ExternalOutput")
    nap_d = dt("nap", [DEPTH, KA - 1, D], F32, kind="ExternalOutput")
    nas_d = dt("nas", [DEPTH, NS, KA - 1, D], F32, kind="ExternalOutput")
    ncp_d = dt("ncp", [DEPTH, KC - 1, D], F32, kind="ExternalOutput")
    ncs_d = dt("ncs", [DEPTH, NS * (KC - 1), D], F32, kind="ExternalOutput")
    nvs_d = dt("nvs", [DEPTH, NSTOK, D], F32, kind="ExternalOutput")
    c_ident, c_tril, c_rep, c_bmask = _consts()
    ident_d = nc.inline_tensor(c_ident, "c_ident")
    tril_d = nc.inline_tensor(c_tril, "c_tril")
    rep_d = nc.inline_tensor(c_rep, "c_rep")
    bmask_d = nc.inline_tensor(c_bmask, "c_bmask")
    c_identst = np.tile(np.eye(32, dtype=np.float32), (4, 1))
    identst_d = nc.inline_tensor(c_identst, "c_identst")

    es = ExitStack()
    P = Prog(nc, es)

    def sb(name, shape, dty):
        return es.enter_context(nc.sbuf_tensor(name, shape, dty))

    TG = 1088
    xT = sb("xT", [128, NCH, TG], F32)
    xn = sb("xn", [128, NCH, TG], BF16)
    mbuf = sb("mbuf", [128, NCH * TG], BF16)
    S2 = sb("S2", [128, NCH, TG], BF16)
    NAB = 2
    ab = sb("ab", [128, NAB, 1056], BF16)
    stk = sb("stk", [128, 2, 4, 1056], BF16)
    sdiag = sb("sdiag", [128, 2, 32, 32], BF16)
    wst = sb("wst_sb", [128, 256], BF16)
    ones32 = sb("ones32", [128, 128], F32)
    identst = sb("identst", [128, 32], BF16)
    identstf = sb("identstf", [128, 32], F32)
    abs_ = sb("abs", [128, NCH, NS * 34], BF16)
    cis_ = sb("cis", [128, NCH, NS * 6], BF16)
    stats = sb("stats", [128, 2, TG], F32)
    tmpF = [sb("tmpF%d" % i, [128, 512], F32) for i in range(5)]
    tmpB = [sb("tmpB%d" % i, [128, 512], BF16) for i in range(3)]
    wring = [sb("wring%d" % i, [128, NCH, 512], BF16) for i in range(NRING)]
    colv = sb("colv", [128, NCH, NR], F32)
    diagC = sb("diagC", [128, KC, 128], BF16)
    WmTb = sb("WmTb", [128, NCH, 128], BF16)
    BiasB = sb("BiasB", [128, NCH, 128], F32)
    BD = sb("BD", [64, NCH, 64], BF16)
    ident = sb("ident", [128, 128], F32)
    identb = sb("identb", [128, 128], BF16)
    onesb = sb("onesb", [128, 128], BF16)
    tril = sb("tril", [128, 128], F32)
    rep = sb("rep", [4, 64], F32)
    bmask = sb("bmask", [64, 64], F32)
    epsc = sb("epsc", [128, 2], F32)
    stg = [sb("stg%d" % i, [128, D], F32) for i in range(3)]
    vn = [sb("vn%d" % i, [128, D], BF16) for i in range(2)]
    ahalo = sb("ahalo", [128, DEPTH, NCH, KA - 1], BF16)
    chalo = sb("chalo", [128, DEPTH, NCH, KC - 1], BF16)
    aslab = sb("aslab", [128, NCH, 64], F32)
    cslab = sb("cslab", [128, NCH, 32], F32)
    aslabp = sb("aslabp", [128, NCH, KA - 1], F32)
    cslabp = sb("cslabp", [128, NCH, KC - 1], F32)
    bstr = sb("bstr", [128, 3, 2, 6], F32)
    mvr = sb("mvr", [128, 3, 4], F32)
    banks = [es.enter_context(nc.psum_tensor("bank%d" % i, [128, 512], F32)) for i in range(8)]

    mb_f32 = mbuf[:, 0:4 * D * 2].bitcast(F32).rearrange("p (a c) -> p a c", a=4)
    mview = mbuf[:, :].rearrange("p (c t) -> p c t", c=NCH)
    WmTf = mb_f32[:, 0, :].rearrange("p (g i) -> p g i", g=NCH)
    MB_ALL = [("m", j, ti) for j in range(NCH) for ti in range(3)]

    def BK(i):
        return ("bk", i)

    def act(fn, reads, writes):
        return P.op("act", fn, reads, writes)

    def dve(fn, reads, writes):
        return P.op("dve", fn, reads, writes)

    def pool(fn, reads, writes):
        return P.op("pool", fn, reads, writes)

    def pe(fns, reads, writes):
        return P.op("pe", fns, reads, writes)

    def mm(out, lhsT, rhs, start, stop):
        return lambda e: e.matmul(out, lhsT=lhsT, rhs=rhs, start=start, stop=stop)

    def tp(out, in_, idn):
        return lambda e: e.transpose(out, in_, idn)

    def tf():
        i = P.rotate("tf", 5)
        return tmpF[i], ("tf", i)

    def tb_():
        i = P.rotate("tb", 3)
        return tmpB[i], ("tb", i)

    def sg():
        i = P.rotate("stg", 3)
        return stg[i], ("stg", i)

    out_evs = []

    P.dma("sp", "c0", ident[:], ident_d.ap(), writes=[("ident",)])
    P.dma("sp", "c1", tril[:], tril_d.ap(), writes=[("tril",)])
    P.dma("sp", "c2", rep[:], rep_d.ap(), writes=[("rep",)])
    P.dma("sp", "c3", bmask[:], bmask_d.ap(), writes=[("bmask",)])
    dve(lambda e: e.tensor_copy(out=identb[:], in_=ident[:]), [("ident",)], [("identb",)])
    P.dma("sp", "c4", identstf[:], identst_d.ap(), writes=[("identstf",)])
    dve(lambda e: e.tensor_copy(out=identst[:], in_=identstf[:]), [("identstf",)], [("identst",)])
    for sl in range(NAB):
        pool(lambda e, sl=sl: e.memset(ab[:, sl, 1054:1056], 0.0), [], [("ab", sl)])
    pool(lambda e: e.memset(onesb[:], 1.0), [], [("onesb",)])
    pool(lambda e: e.memset(ones32[:], 1.0), [], [("ones32",)])
    pool(lambda e: e.memset(epsc[:, 0:1], RMS_EPS), [], [("epsc",)])
    pool(lambda e: e.memset(epsc[:, 1:2], LN_EPS), [], [("epsc",)])

    groups = []
    g0 = [dict(off=0, n=512, kind="p", seq0=0), dict(off=512, n=512, kind="p", seq0=512),
          dict(off=1024, n=64, kind="s", seq0=0)]
    g1 = [dict(off=0, n=512, kind="p", seq0=1024), dict(off=512, n=512, kind="p", seq0=1536)]
    groups = [g0, g1][:ngroups]

    def wsrc(tname, l, segs):
        tens = {"in": win_d, "pa": wpa_d, "pb": wpb_d, "pc": wpc_d, "o": wo_d}[tname]
        outs = []
        pos = 0
        for (c0, ncol) in segs:
            src = tens.ap()[l, :, c0:c0 + ncol].rearrange("(k p) c -> p k c", p=128)
            outs.append((pos, ncol, src))
            pos += ncol
        return outs

    def pass_blocks(l):
        bl = []
        for c0 in (0, 2, 4, 6):
            bl.append(("in", l, [(C_AVAL + c0 * 128, 256), (C_AGATE + c0 * 128, 256)]))
        for jb in range(2):
            bl.append(("in", l, [(C_ZA + jb * 512, 512)]))
        bl.append(("in", l, [(C_GC, 256), (C_HC, 256)]))
        for jb in range(2):
            bl.append(("in", l, [(C_G0 + jb * 512, 512)]))
            bl.append(("pa", l, [(jb * 512, 512)]))
        for jb in range(2):
            bl.append(("in", l, [(C_V + jb * 512, 512)]))
        for jb in range(2):
            bl.append(("in", l, [(C_ZB + jb * 512, 512)]))
            bl.append(("in", l, [(C_U + jb * 512, 512)]))
        for jb in range(2):
            bl.append(("in", l, [(C_G1 + jb * 512, 512)]))
            bl.append(("pb", l, [(jb * 512, 512)]))
        for c0 in (2, 4, 6):
            bl.append(("in", l, [(C_GC + c0 * 128, 256), (C_HC + c0 * 128, 256)]))
        for jb in range(2):
            bl.append(("in", l, [(C_ZC + jb * 512, 512)]))
            bl.append(("in", l, [(C_GB + jb * 512, 512)]))
        for jb in range(2):
            bl.append(("in", l, [(C_G2 + jb * 512, 512)]))
            bl.append(("pc", l, [(jb * 512, 512)]))
        for jb in range(2):
            bl.append(("o", l, [(jb * 512, 512)]))
        return bl

    all_blocks = []
    for g in range(len(groups)):
        for l in range(depth):
            all_blocks.extend(pass_blocks(l))
    wstate = dict(issued=0, cur=0)

    def issue_weights(upto):
        while wstate["issued"] < min(upto, len(all_blocks)):
            i = wstate["issued"]
            tname, l, segs = all_blocks[i]
            slot = i % NRING
            for (pos, ncol, src) in wsrc(tname, l, segs):
                P.dma("pool", "w%d" % slot, wring[slot][:, :, pos:pos + ncol], src, writes=[("w", slot)] if pos == 0 else [])
                if pos != 0:
                    P.res[("w", slot)][0] = (P.sem("d_w%d" % slot), 16 * P.dma_cnt[P.sem("d_w%d" % slot)])
            wstate["issued"] += 1

    def next_block(first=True):
        if first:
            issue_weights(wstate["cur"] + NRING)
        i = wstate["cur"]
        wstate["cur"] += 1
        slot = i % NRING
        return wring[slot], ("w", slot)

    def load_group(gi, tiles):
        for ti, t in enumerate(tiles):
            if t["kind"] == "p":
                for tb in range(4):
                    P.dma("sp", "xl%d" % tb, mb_f32[:, tb, :], xp_d.ap()[t["seq0"] + tb * 128:t["seq0"] + (tb + 1) * 128, :],
                          writes=[("mbstg", tb)] + (MB_ALL if tb == 0 else []))
                for c in range(NCH):
                    b = P.bank()
                    pe([tp(banks[b][:, tb * 128:(tb + 1) * 128], mb_f32[:, tb, c * 128:(c + 1) * 128], ident[:]) for tb in range(4)],
                       [("mbstg", tb) for tb in range(4)] + [("ident",)], [BK(b)])
                    act(lambda e, b=b, c=c, t=t: e.activation(out=xT[:, c, t["off"]:t["off"] + 512], in_=banks[b][:, :], func=AF.Copy),
                        [BK(b)], [("xT", c, ti)])
            else:
                s_, sk = sg()
                P.dma("sp", sk[0] + str(sk[1]), s_[0:NSTOK, :], xs_d.ap(), writes=[sk])
                b = P.bank()
                pe([tp(banks[b][:, c * 64:(c + 1) * 64], s_[0:NSTOK, c * 128:(c + 1) * 128], ident[0:NSTOK, 0:NSTOK]) for c in range(NCH)],
                   [sk, ("ident",)], [BK(b)])
                act(lambda e, b=b, t=t: e.activation(out=xT[:, :, t["off"]:t["off"] + 64],
                                                      in_=banks[b][:, :].rearrange("p (c t) -> p c t", c=NCH), func=AF.Copy),
                    [BK(b)], [("xT", c, ti) for c in range(NCH)])

    def rms_stats(tiles):
        for ti, t in enumerate(tiles):
            rms_tile(ti, t)

    def rms_tile(ti, t):
        if True:
            off, n = t["off"], t["n"]
            b = P.bank()
            for c in range(NCH):
                q, qk = tb_()
                act(lambda e, q=q, c=c, off=off, n=n: e.activation(out=q[:, 0:n], in_=xT[:, c, off:off + n], func=AF.Square),
                    [("xT", c, ti)], [qk])
                pe(mm(banks[b][:, 0:n], onesb[:, :], q[:, 0:n], c == 0, c == NCH - 1), [qk, ("onesb",)], [BK(b)])
            act(lambda e, b=b, off=off, n=n: e.activation(out=stats[:, 0, off:off + n], in_=banks[b][:, 0:n], func=AF.Sqrt,
                                                        bias=epsc[:, 0:1], scale=1.0 / D),
                [BK(b), ("epsc",)], [("st", 0, ti)])
            dve(lambda e, off=off, n=n: e.reciprocal(out=stats[:, 0, off:off + n], in_=stats[:, 0, off:off + n]),
                [("st", 0, ti)], [("st", 0, ti)])

    def setup_pass(gi, l, tiles, part):
        has_s = any(t["kind"] == "s" for t in tiles)
        if part == "a":
            s_, sk = sg()
            P.dma("sp", sk[0] + str(sk[1]), s_[0:NR, :], vecs_d.ap()[l, :, :], writes=[sk])
            b = P.bank()
            pe([tp(banks[b][:, c * NR:(c + 1) * NR], s_[0:NR, c * 128:(c + 1) * 128], ident[0:NR, 0:NR]) for c in range(NCH)],
               [sk, ("ident",)], [BK(b)])
            dve(lambda e, b=b: e.tensor_copy(out=colv[:, :, :], in_=banks[b][:, 0:NCH * NR].rearrange("p (c r) -> p c r", c=NCH)),
                [BK(b)], [("colv",)])
            P.dma("pool", "wst", wst[:, :], wst_d.ap()[l, :, :], writes=[("wst",)])
            phase_mark()
        if part == "b":
            s_, sk = sg()
            sv = s_[:, :].rearrange("p (g j) -> p g j", g=NCH)
            P.dma("sp", sk[0] + str(sk[1]), sv, ws_d.ap()[l].rearrange("g i j -> i g j"), writes=[sk])
            dve(lambda e, sv=sv: e.tensor_tensor(out=sv, in0=sv, in1=tril[:, :].unsqueeze(1).broadcast_to([128, NCH, 128]), op=ALU.mult),
                [sk, ("tril",)], [sk])
            for h in range(2):
                b = P.bank()
                pe([tp(banks[b][:, gg * 128:(gg + 1) * 128], sv[:, h * 4 + gg, :], ident[:]) for gg in range(4)],
                   [sk, ("ident",)], [BK(b)])
                dve(lambda e, b=b, h=h: e.tensor_copy(out=WmTf[:, h * 4:(h + 1) * 4, :], in_=banks[b][:, :].rearrange("p (g i) -> p g i", g=4)),
                    [BK(b)], [("WmTf", h)] + ((MB_ALL + [("mbstg", 0)]) if h == 0 else []))
                act(lambda e, b=b, h=h: e.activation(out=WmTb[:, h * 4:(h + 1) * 4, :], in_=banks[b][:, :].rearrange("p (g i) -> p g i", g=4), func=AF.Copy),
                    [BK(b)], [("WmTb", h)])
            phase_mark()
            if has_s:
                b = P.bank()
                pe([mm(banks[b][0:4, gg * 64:(gg + 1) * 64], sv[0:4, gg, 0:4], rep[0:4, :], True, True) for gg in range(NCH)],
                   [sk, ("rep",)], [BK(b)])
                Zs, zsk = tf()
                dve(lambda e, b=b, Zs=Zs: e.tensor_copy(out=Zs[0:4, :], in_=banks[b][0:4, :]), [BK(b)], [zsk])
                b = P.bank()
                pe(mm(banks[b][0:64, :], rep[0:4, :], Zs[0:4, :], True, True), [zsk, ("rep",)], [BK(b)])
                dve(lambda e, b=b: e.tensor_tensor(out=BD[:, :, :], in0=banks[b][0:64, :].rearrange("p (g q) -> p g q", g=NCH),
                                                   in1=bmask[:, :].unsqueeze(1).broadcast_to([64, NCH, 64]), op=ALU.mult),
                    [BK(b), ("bmask",)], [("BD",)])
            phase_mark()
            s2_, sk2 = sg()
            P.dma("sp", sk2[0] + str(sk2[1]), s2_[:, :], vecs_d.ap()[l, R_LNBB, :].partition_broadcast(128), writes=[sk2])
            bsrow, bsk = sg()
            P.dma("sp", bsk[0] + str(bsk[1]), bsrow[:, :], bs_d.ap()[l, :].partition_broadcast(128), writes=[bsk])
            for h in range(2):
                b = P.bank()
                fns = []
                for gg in range(4):
                    g_ = h * 4 + gg
                    fns.append(mm(banks[b][:, gg * 128:(gg + 1) * 128], s2_[:, g_ * 128:(g_ + 1) * 128], WmTf[:, g_, :], True, True))
                pe(fns, [sk2, ("WmTf", h)], [BK(b)])
                dve(lambda e, b=b, h=h, bsrow=bsrow: e.tensor_tensor(out=BiasB[:, h * 4:(h + 1) * 4, :], in0=banks[b][:, :].rearrange("p (g i) -> p g i", g=4),
                                                                  in1=bsrow[:, h * 512:(h + 1) * 512].rearrange("p (g i) -> p g i", g=4), op=ALU.add),
                    [BK(b), bsk], [("BiasB", h)])
            phase_mark()
        if part == "c" and has_s:
            rows = NS * (KA - 1)
            srcA = sca_d.ap()[l].rearrange("s r c -> (s r) c")
            for blk in range(4):
                r0 = blk * 128
                nr_ = min(128, rows - r0)
                s_, sk = sg()
                P.dma("sp", sk[0] + str(sk[1]), s_[0:nr_, :], srcA[r0:r0 + nr_, :], writes=[sk])
                for s_i in range(r0 // 30, (r0 + nr_ - 1) // 30 + 1):
                    ra = max(r0, s_i * 30 + ST)
                    rb = min(r0 + nr_, (s_i + 1) * 30)
                    if ra < rb:
                        out_evs.append(P.dma("sp", "o_" + sk[0] + str(sk[1]), nas_d.ap()[l, s_i, ra - s_i * 30 - ST:rb - s_i * 30 - ST, :],
                                             s_[ra - r0:rb - r0, :], reads=[sk]))
                for h in range(2):
                    b = P.bank()
                    pe([tp(banks[b][:, cc * 128:cc * 128 + nr_], s_[0:nr_, (h * 4 + cc) * 128:(h * 4 + cc + 1) * 128], ident[0:nr_, 0:nr_]) for cc in range(4)],
                       [sk, ("ident",)], [BK(b)])
                    s_lo = r0 // 30
                    s_hi = (r0 + nr_ - 1) // 30
                    for s_i in range(s_lo, s_hi + 1):
                        ra = max(r0, s_i * 30)
                        rb = min(r0 + nr_, (s_i + 1) * 30)
                        act(lambda e, b=b, h=h, s_i=s_i, ra=ra, rb=rb, r0=r0: e.activation(
                            out=abs_[:, h * 4:(h + 1) * 4, s_i * 34 + (ra - s_i * 30):s_i * 34 + (rb - s_i * 30)],
                            in_=banks[b][:, :].rearrange("p (c t) -> p c t", c=4)[:, :, ra - r0:rb - r0], func=AF.Copy),
                            [BK(b)], [("abs", c) for c in range(h * 4, h * 4 + 4)])
            phase_mark()
            s_, sk = sg()
            P.dma("sp", sk[0] + str(sk[1]), s_[0:NS * 2, :], scc_d.ap()[l].rearrange("s r c -> (s r) c"), writes=[sk])
            b = P.bank()
            pe([tp(banks[b][:, c * 32:(c + 1) * 32], s_[0:32, c * 128:(c + 1) * 128], ident[0:32, 0:32]) for c in range(NCH)],
               [sk, ("ident",)], [BK(b)])
            act(lambda e, b=b: e.activation(out=cis_[:, :, :].rearrange("p c (s r) -> p c s r", s=NS)[:, :, :, 0:2],
                                            in_=banks[b][:, 0:256].rearrange("p (c s r) -> p c s r", c=NCH, s=NS), func=AF.Copy),
                [BK(b)], [("cis", c) for c in range(NCH)])

    def grp(W, wk, col0, ti, t, srcbuf=None, srckey="xn"):
        off, n = t["off"], t["n"]
        b = P.bank()
        src = xn if srcbuf is None else srcbuf
        pe([mm(banks[b][:, 0:n], W[:, k, col0:col0 + 128], src[:, k, off:off + n], k == 0, k == NCH - 1) for k in range(NCH)],
           [wk] + [(srckey, k, ti) for k in range(NCH)], [BK(b)])
        return b

    def run_pass(gi, l, tiles):
        last_group = (gi == 1)
        setup_pass(gi, l, tiles, "a")
        setup_pass(gi, l, tiles, "b")
        phase_mark()
        phase_mark()
        for ti, t in enumerate(tiles):
            off, n = t["off"], t["n"]
            for c in range(NCH):
                dve(lambda e, c=c, off=off, n=n: e.scalar_tensor_tensor(
                    out=xn[:, c, off:off + n], in0=xT[:, c, off:off + n], scalar=colv[:, c, R_NORMG:R_NORMG + 1],
                    in1=stats[:, 0, off:off + n], op0=ALU.mult, op1=ALU.mult),
                    [("xT", c, ti), ("colv",), ("st", 0, ti)], [("xn", c, ti)])

        def a_glu(c, W, wk, cc):
            slot = P.rotate("ab", NAB)
            abk = ("ab", slot)
            if gi == 0:
                pool(lambda e, slot=slot: e.memset(ab[:, slot, 0:KA - 1], 0.0), [], [abk])
            else:
                pool(lambda e, slot=slot, c=c: e.tensor_copy(out=ab[:, slot, 0:KA - 1], in_=ahalo[:, l, c, :]), [("ahalo", l, c)], [abk])
            for ti, t in enumerate(tiles):
                off, n = t["off"], t["n"]
                bg = grp(W, wk, 256 + cc * 128, ti, t)
                sgm, sgk = tf()
                act(lambda e, bg=bg, sgm=sgm, n=n: e.activation(out=sgm[:, 0:n], in_=banks[bg][:, 0:n], func=AF.Sigmoid), [BK(bg)], [sgk])
                bv = grp(W, wk, cc * 128, ti, t)
                if t["kind"] == "p":
                    dve(lambda e, bv=bv, sgm=sgm, slot=slot, off=off, n=n: e.tensor_tensor(
                        out=ab[:, slot, KA - 1 + off:KA - 1 + off + n], in0=banks[bv][:, 0:n], in1=sgm[:, 0:n], op=ALU.mult),
                        [BK(bv), sgk], [abk])
                    if last_group and ti == len(tiles) - 1:
                        dve(lambda e, bv=bv, sgm=sgm, c=c, n=n: e.tensor_tensor(
                            out=aslabp[:, c, 0:KA - 1], in0=banks[bv][:, n - (KA - 1):n], in1=sgm[:, n - (KA - 1):n], op=ALU.mult),
                            [BK(bv), sgk], [("aslabp", c)])
                else:
                    dve(lambda e, bv=bv, sgm=sgm, c=c: e.tensor_tensor(
                        out=abs_[:, c, :].rearrange("p (s r) -> p s r", s=NS)[:, :, KA - 1:KA - 1 + ST],
                        in0=banks[bv][:, 0:NSTOK].rearrange("p (s t) -> p s t", s=NS),
                        in1=sgm[:, 0:NSTOK].rearrange("p (s t) -> p s t", s=NS), op=ALU.mult),
                        [BK(bv), sgk], [("abs", c)])
                    dve(lambda e, bv=bv, sgm=sgm, c=c: e.tensor_tensor(
                        out=aslab[:, c, 0:NSTOK], in0=banks[bv][:, 0:NSTOK], in1=sgm[:, 0:NSTOK], op=ALU.mult),
                        [BK(bv), sgk], [("aslab", c)])
            if gi == 0 and len(groups) > 1:
                pool(lambda e, slot=slot, c=c: e.tensor_copy(out=ahalo[:, l, c, :], in_=ab[:, slot, 1024:1024 + KA - 1]), [abk], [("ahalo", l, c)])
            return slot

        def a_diag(c, slot):
            sbuf_i = c % 2
            for j in range(4):
                for r in range(4):
                    P.dma("sp", "rs%d" % (sbuf_i * 16 + j * 4 + r), stk[r * 32:(r + 1) * 32, sbuf_i, j, 0:1052], ab[j * 32:(j + 1) * 32, slot, r:r + 1052],
                          reads=[("ab", slot)], writes=[("stk", sbuf_i, j, r)])
            dve(lambda e, c=c: e.tensor_tensor(out=sdiag[:, c % 2, :, :], in0=identst[:, :].unsqueeze(1).broadcast_to([128, 32, 32]),
                                               in1=wst[:, c * 32:(c + 1) * 32].unsqueeze(2).broadcast_to([128, 32, 32]), op=ALU.mult),
                [("identst",), ("wst",)], [("sdiag", c % 2)])

        def ln_acc(c, ti, off, n, sq, sqk):
            if c == 0:
                dve(lambda e, sq=sq, off=off, n=n: e.tensor_copy(out=stats[:, 1, off:off + n], in_=sq[:, 0:n]), [sqk], [("st", 1, ti)])
            else:
                dve(lambda e, sq=sq, off=off, n=n: e.tensor_tensor(out=stats[:, 1, off:off + n], in0=sq[:, 0:n], in1=stats[:, 1, off:off + n], op=ALU.add),
                    [sqk, ("st", 1, ti)], [("st", 1, ti)])

        def a_conv(c, slot):
            for ti, t in enumerate(tiles):
                off, n = t["off"], t["n"]
                if t["kind"] == "p":
                    b = P.bank()
                    pe([(lambda e, b=b, q=q, j=j, off=off, n=n: e.matmul(banks[b][32 * j:32 * j + 32, 0:n], lhsT=sdiag[:, c % 2, q * 4 + j, :],
                                                                         rhs=stk[:, c % 2, j, off + 4 * q:off + 4 * q + n], start=(q == 0), stop=(q == 7),
                                                                         tile_position=(0, 32 * j)))
                        for q in range(8) for j in range(4)],
                       [("sdiag", c % 2)] + [("stk", c % 2, j, r) for j in range(4) for r in range(4)], [BK(b)])
                    act(lambda e, b=b, c=c, off=off, n=n: e.activation(out=S2[:, c, off:off + n], in_=banks[b][:, 0:n], func=AF.Identity,
                                                                     bias=colv[:, c, R_BCA:R_BCA + 1], scale=1.0),
                        [BK(b), ("colv",)], [("s2", c, ti)])
                    sq, sqk = tf()
                    act(lambda e, b=b, c=c, sq=sq, n=n: e.activation(out=sq[:, 0:n], in_=banks[b][:, 0:n], func=AF.Square,
                                                                   bias=colv[:, c, R_BCA:R_BCA + 1], scale=1.0),
                        [BK(b), ("colv",)], [sqk])
                    ln_acc(c, ti, off, n, sq, sqk)
                else:
                    av = abs_[:, c, :].rearrange("p (s r) -> p s r", s=NS)
                    acc, acck = tf()
                    accs = [acc[:, 0:NSTOK].rearrange("p (s t) -> p s t", s=NS), acc[:, NSTOK:2 * NSTOK].rearrange("p (s t) -> p s t", s=NS)]
                    hk = [("acch", acck[1], 0), ("acch", acck[1], 1)]
                    for k in range(KA):
                        h_ = k % 2
                        if k < 2:
                            dve(lambda e, h_=h_, av=av, c=c, k=k: e.tensor_scalar(out=accs[h_], in0=av[:, :, k:k + ST], scalar1=colv[:, c, R_WCA + k:R_WCA + k + 1],
                                                                                 scalar2=None, op0=ALU.mult),
                                [("abs", c), ("colv",)], [hk[h_], acck])
                        else:
                            dve(lambda e, h_=h_, av=av, c=c, k=k: e.scalar_tensor_tensor(
                                out=accs[h_], in0=av[:, :, k:k + ST], scalar=colv[:, c, R_WCA + k:R_WCA + k + 1], in1=accs[h_], op0=ALU.mult, op1=ALU.add),
                                [("abs", c), ("colv",), hk[h_]], [hk[h_]])
                    dve(lambda e: e.tensor_tensor(out=accs[0], in0=accs[0], in1=accs[1], op=ALU.add), [hk[0], hk[1]], [hk[0], acck])
                    act(lambda e, acc=acc, c=c, off=off, n=n: e.activation(out=S2[:, c, off:off + n], in_=acc[:, 0:n], func=AF.Identity,
                                                                         bias=colv[:, c, R_BCA:R_BCA + 1], scale=1.0),
                        [acck, ("colv",)], [("s2", c, ti)])
                    sq, sqk = tf()
                    act(lambda e, acc=acc, c=c, sq=sq, n=n: e.activation(out=sq[:, 0:n], in_=acc[:, 0:n], func=AF.Square,
                                                                     bias=colv[:, c, R_BCA:R_BCA + 1], scale=1.0),
                        [acck, ("colv",)], [sqk])
                    ln_acc(c, ti, off, n, sq, sqk)

        hist = []
        for c in range(NCH):
            if c % 2 == 0:
                W, wk = next_block()
            if len(hist) >= 1:
                a_diag(*hist[-1])
            slot = a_glu(c, W, wk, c % 2)
            if c == 0:
                setup_pass(gi, l, tiles, "c")
            if len(hist) >= 2:
                a_conv(*hist[-2])
            hist.append((c, slot))
        a_diag(*hist[-1])
        a_conv(*hist[-2])
        a_conv(*hist[-1])
        phase_mark()
        if last_group:
            emit_rows_out(aslabp, KA - 1, [("aslabp", c) for c in range(NCH)], nap_d.ap()[l, :, :], KA - 1)
        if any(t["kind"] == "s" for t in tiles):
            emit_rows_out(aslab, NSTOK, [("aslab", c) for c in range(NCH)], None, NSTOK, sample_a_layer=l)

        phase_mark()
        for ti, t in enumerate(tiles):
            off, n = t["off"], t["n"]
            bs_ = P.bank()
            bq = P.bank()
            pe([mm(banks[bs_][:, 0:n], onesb[:, :], S2[:, c, off:off + n], c == 0, c == NCH - 1) for c in range(NCH)],
               [("s2", c, ti) for c in range(NCH)] + [("onesb",)], [BK(bs_)])
            pe(mm(banks[bq][:, 0:n], ones32[:, :], stats[:, 1, off:off + n], True, True), [("st", 1, ti), ("ones32",)], [BK(bq)])
            dve(lambda e, bs_=bs_, off=off, n=n: e.tensor_scalar(out=stats[:, 0, off:off + n], in0=banks[bs_][:, 0:n], scalar1=1.0 / D, scalar2=None, op0=ALU.mult),
                [BK(bs_)], [("st", 0, ti)])
            m2, m2k = tf()
            dve(lambda e, m2=m2, off=off, n=n: e.tensor_tensor(out=m2[:, 0:n], in0=stats[:, 0, off:off + n], in1=stats[:, 0, off:off + n], op=ALU.mult),
                [("st", 0, ti)], [m2k])
            dve(lambda e, m2=m2, bq=bq, n=n: e.scalar_tensor_tensor(out=m2[:, 0:n], in0=banks[bq][:, 0:n], scalar=1.0 / D, in1=m2[:, 0:n],
                                                                   op0=ALU.mult, op1=ALU.subtract),
                [BK(bq), m2k], [m2k])
            act(lambda e, m2=m2, off=off, n=n: e.activation(out=stats[:, 1, off:off + n], in_=m2[:, 0:n], func=AF.Sqrt, bias=epsc[:, 1:2], scale=1.0),
                [m2k, ("epsc",)], [("st", 1, ti)])
            dve(lambda e, off=off, n=n: e.reciprocal(out=stats[:, 1, off:off + n], in_=stats[:, 1, off:off + n]), [("st", 1, ti)], [("st", 1, ti)])

        def c_mul(c, W, wk, cc):
            slot = P.rotate("ab", NAB)
            abk = ("ab", slot)
            if gi == 0:
                pool(lambda e, slot=slot: e.memset(ab[:, slot, 0:KC - 1], 0.0), [], [abk])
            else:
                pool(lambda e, slot=slot, c=c: e.tensor_copy(out=ab[:, slot, 0:KC - 1], in_=chalo[:, l, c, :]), [("chalo", l, c)], [abk])
            for ti, t in enumerate(tiles):
                off, n = t["off"], t["n"]
                bg = grp(W, wk, cc * 128, ti, t)
                gcm, gck = tf()
                act(lambda e, bg=bg, gcm=gcm, n=n: e.activation(out=gcm[:, 0:n], in_=banks[bg][:, 0:n], func=AF.Copy), [BK(bg)], [gck])
                bh = grp(W, wk, 256 + cc * 128, ti, t)
                if t["kind"] == "p":
                    dve(lambda e, bh=bh, gcm=gcm, slot=slot, off=off, n=n: e.tensor_tensor(
                        out=ab[:, slot, KC - 1 + off:KC - 1 + off + n], in0=banks[bh][:, 0:n], in1=gcm[:, 0:n], op=ALU.mult), [BK(bh), gck], [abk])
                    if last_group and ti == len(tiles) - 1:
                        dve(lambda e, bh=bh, gcm=gcm, c=c, n=n: e.tensor_tensor(
                            out=cslabp[:, c, 0:KC - 1], in0=banks[bh][:, n - (KC - 1):n], in1=gcm[:, n - (KC - 1):n], op=ALU.mult),
                            [BK(bh), gck], [("cslabp", c)])
                else:
                    dve(lambda e, bh=bh, gcm=gcm, c=c: e.tensor_tensor(
                        out=cis_[:, c, :].rearrange("p (s r) -> p s r", s=NS)[:, :, KC - 1:KC - 1 + ST],
                        in0=banks[bh][:, 0:NSTOK].rearrange("p (s t) -> p s t", s=NS),
                        in1=gcm[:, 0:NSTOK].rearrange("p (s t) -> p s t", s=NS), op=ALU.mult), [BK(bh), gck], [("cis", c)])
                    dve(lambda e, bh=bh, gcm=gcm, c=c: e.tensor_tensor(
                        out=cslab[:, c, 0:NS * 2].rearrange("p (s r) -> p s r", s=NS),
                        in0=banks[bh][:, 0:NSTOK].rearrange("p (s t) -> p s t", s=NS)[:, :, ST - 2:ST],
                        in1=gcm[:, 0:NSTOK].rearrange("p (s t) -> p s t", s=NS)[:, :, ST - 2:ST], op=ALU.mult), [BK(bh), gck], [("cslab", c)])
            if gi == 0 and len(groups) > 1:
                pool(lambda e, slot=slot, c=c: e.tensor_copy(out=chalo[:, l, c, :], in_=ab[:, slot, 1024:1024 + KC - 1]), [abk], [("chalo", l, c)])
            return slot

        def c_diag(c):
            act([lambda e, k=k, c=c: e.activation(out=diagC[:, k, :], in_=identb[:, :], func=AF.Identity, scale=colv[:, c, R_WCC + k:R_WCC + k + 1])
                 for k in range(KC)], [("identb",), ("colv",)], [("diagC",)])

        def c_conv(c, slot):
            abk = ("ab", slot)
            for ti, t in enumerate(tiles):
                off, n = t["off"], t["n"]
                b = P.bank()
                if t["kind"] == "p":
                    pe([mm(banks[b][:, 0:n], diagC[:, k, :], ab[:, slot, off + k:off + k + n], k == 0, k == KC - 1) for k in range(KC)],
                       [("diagC",), abk], [BK(b)])
                else:
                    cv = cis_[:, c, :].rearrange("p (s r) -> p s r", s=NS)
                    pe([mm(banks[b][:, 0:NSTOK].rearrange("p (s t) -> p s t", s=NS), diagC[:, k, :], cv[:, :, k:k + ST], k == 0, k == KC - 1) for k in range(KC)],
                       [("diagC",), ("cis", c)], [BK(b)])
                act(lambda e, b=b, c=c, off=off, n=n: e.activation(out=S2[:, c, off:off + n], in_=banks[b][:, 0:n], func=AF.Copy), [BK(b)], [("s2", c, ti)])

        def za_stage1(W, wk, cc, c, ti, t):
            off, n = t["off"], t["n"]
            bz = grp(W, wk, cc * 128, ti, t)
            sz, szk = tf()
            act(lambda e, bz=bz, sz=sz, n=n: e.activation(out=sz[:, 0:n], in_=banks[bz][:, 0:n], func=AF.Silu), [BK(bz)], [szk])
            t1, t1k = tf()
            dve(lambda e, t1=t1, c=c, off=off, n=n: e.tensor_tensor(out=t1[:, 0:n], in0=S2[:, c, off:off + n], in1=stats[:, 0, off:off + n], op=ALU.subtract),
                [("s2", c, ti), ("st", 0, ti)], [t1k])
            P.op(ZA_MUL_ENG, lambda e, t1=t1, off=off, n=n: e.tensor_tensor(out=t1[:, 0:n], in0=t1[:, 0:n], in1=stats[:, 1, off:off + n], op=ALU.mult),
                 [t1k, ("st", 1, ti)], [t1k])
            return (c, ti, t, sz, szk, t1, t1k)

        def za_stage2(c, ti, t, sz, szk, t1, t1k):
            off, n = t["off"], t["n"]
            act(lambda e, t1=t1, c=c, n=n: e.activation(out=t1[:, 0:n], in_=t1[:, 0:n], func=AF.Silu,
                                                      scale=colv[:, c, R_LNAG:R_LNAG + 1], bias=colv[:, c, R_LNAB:R_LNAB + 1]),
                [t1k, ("colv",)], [t1k])
            dve(lambda e, t1=t1, sz=sz, c=c, off=off, n=n: e.tensor_tensor(out=S2[:, c, off:off + n], in0=t1[:, 0:n], in1=sz[:, 0:n], op=ALU.mult),
                [t1k, szk], [("s2", c, ti)])

        pend = None
        c_early = []
        for jb in range(2):
            W, wk = next_block()
            if jb == 1:
                Wc0, wck0 = next_block(False)
            for cc in range(4):
                c = jb * 4 + cc
                for ti, t in enumerate(tiles):
                    u = za_stage1(W, wk, cc, c, ti, t)
                    if pend is not None:
                        za_stage2(*pend)
                    pend = u
                if jb == 1 and cc in (1, 3):
                    za_stage2(*pend)
                    pend = None
                    c_early.append((cc // 2, c_mul(cc // 2, Wc0, wck0, cc // 2)))
        if pend is not None:
            za_stage2(*pend)
        phase_mark()
        proj_phase(0, l, tiles)
        phase_mark()

        Wv0, wvk0 = next_block()
        Wv1, wvk1 = next_block(False)
        vblocks = []
        for ti, t in enumerate(tiles):
            for tbi in range(max(1, t["n"] // 128)):
                vblocks.append((ti, t, tbi))

        vcnt = [0]

        def vmm(ti, t, tbi):
            off, n = t["off"], t["n"]
            m_ = min(128, n)
            tok0 = off + tbi * 128
            b0 = 2 * (vcnt[0] % 3)
            b1 = b0 + 1
            vcnt[0] += 1
            pe([mm(banks[b0][0:m_, :], xn[:, k, tok0:tok0 + m_], Wv0[:, k, :], k == 0, k == NCH - 1) for k in range(NCH)],
               [wvk0] + [("xn", k, ti) for k in range(NCH)], [BK(b0)])
            pe([mm(banks[b1][0:m_, :], xn[:, k, tok0:tok0 + m_], Wv1[:, k, :], k == 0, k == NCH - 1) for k in range(NCH)],
               [wvk1] + [("xn", k, ti) for k in range(NCH)], [BK(b1)])
            return b0, b1

        def vrestA(ti, t, tbi, b0, b1):
            off, n = t["off"], t["n"]
            m_ = min(128, n)
            tok0 = off + tbi * 128
            ri = P.rotate("mvr", 3)
            mvv = mvr[:, ri, :]
            bstv = bstr[:, ri, :, :]
            dve([lambda e, b0=b0, m_=m_: e.bn_stats(out=bstv[0:m_, 0, :], in_=banks[b0][0:m_, :]),
                 lambda e, b1=b1, m_=m_: e.bn_stats(out=bstv[0:m_, 1, :], in_=banks[b1][0:m_, :])], [BK(b0), BK(b1)], [("bst", ri)])
            dve(lambda e, m_=m_: e.bn_aggr(out=mvv[0:m_, 0:2], in_=bstv[0:m_, :, :].rearrange("p a b -> p (a b)")), [("bst", ri)], [("mv", ri, 0)])
            act(lambda e, m_=m_: e.activation(out=mvv[0:m_, 2:3], in_=mvv[0:m_, 1:2], func=AF.Sqrt, bias=epsc[0:m_, 1:2], scale=1.0),
                [("mv", ri, 0), ("epsc",)], [("mv", ri, 1)])
            return (ti, t, tbi, b0, b1, ri)

        def vrestB(ti, t, tbi, b0, b1, ri):
            off, n = t["off"], t["n"]
            m_ = min(128, n)
            tok0 = off + tbi * 128
            mvv = mvr[:, ri, :]
            bstv = bstr[:, ri, :, :]
            dve(lambda e, m_=m_: e.reciprocal(out=mvv[0:m_, 2:3], in_=mvv[0:m_, 2:3]), [("mv", ri, 1)], [("mv", ri, 1)])
            dve(lambda e, m_=m_: e.tensor_scalar(out=mvv[0:m_, 3:4], in0=mvv[0:m_, 0:1], scalar1=mvv[0:m_, 2:3], scalar2=-1.0, op0=ALU.mult, op1=ALU.mult),
                [("mv", ri, 0), ("mv", ri, 1)], [("mv", ri, 2)])
            vi = P.rotate("vn", 2)
            vt = vn[vi]
            vk = ("vn", vi)
            for hh, bb in ((0, b0), (1, b1)):
                act(lambda e, vt=vt, hh=hh, bb=bb, m_=m_: e.activation(out=vt[0:m_, hh * 512:(hh + 1) * 512], in_=banks[bb][0:m_, :], func=AF.Identity,
                                                                  scale=mvv[0:m_, 2:3], bias=mvv[0:m_, 3:4]),
                    [BK(bb), ("mv", ri, 1), ("mv", ri, 2)], [vk])
            if t["kind"] == "s":
                so, sok = sg()
                for hh, bb in ((0, b0), (1, b1)):
                    act(lambda e, so=so, hh=hh, bb=bb, m_=m_: e.activation(out=so[0:m_, hh * 512:(hh + 1) * 512], in_=banks[bb][0:m_, :], func=AF.Identity,
                                                                      scale=mvv[0:m_, 2:3], bias=mvv[0:m_, 3:4]),
                        [BK(bb), ("mv", ri, 1), ("mv", ri, 2)], [sok])
                gb_, gbk = sg()
                P.dma("sp", gbk[0] + str(gbk[1]), gb_[0:NSTOK, :], vecs_d.ap()[l, R_LNBG, :].partition_broadcast(NSTOK), writes=[gbk])
                dve(lambda e, so=so, gb_=gb_: e.tensor_tensor(out=so[0:NSTOK, :], in0=so[0:NSTOK, :], in1=gb_[0:NSTOK, :], op=ALU.mult), [sok, gbk], [sok])
                gb2, gbk2 = sg()
                P.dma("sp", gbk2[0] + str(gbk2[1]), gb2[0:NSTOK, :], vecs_d.ap()[l, R_LNBB, :].partition_broadcast(NSTOK), writes=[gbk2])
                dve(lambda e, so=so, gb2=gb2: e.tensor_tensor(out=so[0:NSTOK, :], in0=so[0:NSTOK, :], in1=gb2[0:NSTOK, :], op=ALU.add), [sok, gbk2], [sok])
                out_evs.append(P.dma("sp", "o_" + sok[0] + str(sok[1]), nvs_d.ap()[l, :, :], so[0:NSTOK, :], reads=[sok]))
                b = 6
                pe([mm(banks[b][:, g_ * 64:(g_ + 1) * 64], vt[0:NSTOK, g_ * 128:(g_ + 1) * 128], BD[0:NSTOK, g_, :], True, True) for g_ in range(NCH)],
                   [vk, ("BD",)], [BK(b)])
                act(lambda e, b=b, off=off: e.activation(out=S2[:, :, off:off + NSTOK], in_=banks[b][:, :].rearrange("p (g q) -> p g q", g=NCH), func=AF.Copy),
                    [BK(b)], [("s2", g_, ti) for g_ in range(NCH)])
            else:
                for h in range(2):
                    b = 6 + h
                    pe([mm(banks[b][:, gg * 128:(gg + 1) * 128], vt[:, (h * 4 + gg) * 128:(h * 4 + gg + 1) * 128], WmTb[:, h * 4 + gg, :], True, True) for gg in range(4)],
                       [vk, ("WmTb", h)], [BK(b)])
                    act(lambda e, b=b, h=h, tok0=tok0: e.activation(out=S2[:, h * 4:(h + 1) * 4, tok0:tok0 + 128],
                                                                    in_=banks[b][:, :].rearrange("p (g i) -> p g i", g=4), func=AF.Copy),
                        [BK(b)], [("s2", h * 4 + gg, ti) for gg in range(4)])

        vq = [vmm(*vblocks[0])]
        if len(vblocks) > 1:
            vq.append(vmm(*vblocks[1]))
        vpend = None
        for i_, blk in enumerate(vblocks):
            b0_, b1_ = vq.pop(0)
            if vpend is not None:
                vrestB(*vpend)
            if i_ + 2 < len(vblocks):
                vq.append(vmm(*vblocks[i_ + 2]))
            vpend = vrestA(blk[0], blk[1], blk[2], b0_, b1_)
        vrestB(*vpend)
        for jb in range(2):
            Wz, wzk = next_block()
            Wu, wuk = next_block(False)
            for cc in range(4):
                c = jb * 4 + cc
                for ti, t in enumerate(tiles):
                    off, n = t["off"], t["n"]
                    bz = grp(Wz, wzk, cc * 128, ti, t)
                    sz, szk = tf()
                    act(lambda e, bz=bz, sz=sz, n=n: e.activation(out=sz[:, 0:n], in_=banks[bz][:, 0:n], func=AF.Silu), [BK(bz)], [szk])
                    bu = grp(Wu, wuk, cc * 128, ti, t)
                    dve(lambda e, bu=bu, sz=sz, n=n: e.tensor_tensor(out=sz[:, 0:n], in0=banks[bu][:, 0:n], in1=sz[:, 0:n], op=ALU.mult), [BK(bu), szk], [szk])
                    wv, wvk = tf()
                    if t["kind"] == "p":
                        dve(lambda e, wv=wv, c=c, off=off, n=n: e.scalar_tensor_tensor(
                            out=wv[:, 0:n].rearrange("p (a i) -> p a i", i=128), in0=S2[:, c, off:off + n].rearrange("p (a i) -> p a i", i=128),
                            scalar=colv[:, c, R_LNBG:R_LNBG + 1], in1=BiasB[:, c, :].unsqueeze(1).broadcast_to([128, n // 128, 128]),
                            op0=ALU.mult, op1=ALU.add),
                            [("s2", c, ti), ("colv",), ("BiasB", c // 4)], [wvk])
                    else:
                        dve(lambda e, wv=wv, c=c, off=off: e.scalar_tensor_tensor(
                            out=wv[:, 0:NSTOK].rearrange("p (s t) -> p s t", s=NS), in0=S2[:, c, off:off + NSTOK].rearrange("p (s t) -> p s t", s=NS),
                            scalar=colv[:, c, R_LNBG:R_LNBG + 1], in1=BiasB[:, c, 0:ST].unsqueeze(1).broadcast_to([128, NS, ST]),
                            op0=ALU.mult, op1=ALU.add),
                            [("s2", c, ti), ("colv",), ("BiasB", c // 4)], [wvk])
                    dve(lambda e, sz=sz, wv=wv, c=c, off=off, n=n: e.tensor_tensor(out=S2[:, c, off:off + n], in0=sz[:, 0:n], in1=wv[:, 0:n], op=ALU.mult),
                        [szk, wvk], [("s2", c, ti)])
        proj_phase(1, l, tiles)
        phase_mark()

        c_diag(c_early[0][0])
        c_conv(*c_early[0])
        prev = c_early[1]
        for c in range(2, NCH):
            if c % 2 == 0:
                W, wk = next_block()
            c_diag(prev[0])
            slot = c_mul(c, W, wk, c % 2)
            c_conv(*prev)
            prev = (c, slot)
        c_diag(prev[0])
        c_conv(*prev)
        if last_group:
            emit_rows_out(cslabp, KC - 1, [("cslabp", c) for c in range(NCH)], ncp_d.ap()[l, :, :], KC - 1)
        if any(t["kind"] == "s" for t in tiles):
            emit_rows_out(cslab, NS * 2, [("cslab", c) for c in range(NCH)], ncs_d.ap()[l, :, :], NS * 2)
        for jb in range(2):
            Wz, wzk = next_block()
            Wg, wgk = next_block(False)
            for cc in range(4):
                c = jb * 4 + cc
                for ti, t in enumerate(tiles):
                    off, n = t["off"], t["n"]
                    bz = grp(Wz, wzk, cc * 128, ti, t)
                    sz, szk = tf()
                    act(lambda e, bz=bz, sz=sz, n=n: e.activation(out=sz[:, 0:n], in_=banks[bz][:, 0:n], func=AF.Silu), [BK(bz)], [szk])
                    bg = grp(Wg, wgk, cc * 128, ti, t)
                    dve(lambda e, bg=bg, sz=sz, n=n: e.tensor_tensor(out=sz[:, 0:n], in0=banks[bg][:, 0:n], in1=sz[:, 0:n], op=ALU.mult), [BK(bg), szk], [szk])
                    dve(lambda e, sz=sz, c=c, off=off, n=n: e.tensor_tensor(out=S2[:, c, off:off + n], in0=sz[:, 0:n], in1=S2[:, c, off:off + n], op=ALU.mult),
                        [szk, ("s2", c, ti)], [("s2", c, ti)])
        proj_phase(2, l, tiles)
        phase_mark()

        Wo0, wok0 = next_block()
        Wo1, wok1 = next_block(False)
        for ti, t in enumerate(tiles):
            off, n = t["off"], t["n"]
            for j in range(NCH):
                W, wk = (Wo0, wok0) if j < 4 else (Wo1, wok1)
                cc = j % 4
                b = P.bank()
                pe([mm(banks[b][:, 0:n], W[:, k, cc * 128:(cc + 1) * 128], mview[:, k, off:off + n], k == 0, k == NCH - 1) for k in range(NCH)],
                   [wk] + [("m", k, ti) for k in range(NCH)], [BK(b)])
                dve(lambda e, b=b, j=j, off=off, n=n: e.tensor_tensor(out=xT[:, j, off:off + n], in0=banks[b][:, 0:n], in1=xT[:, j, off:off + n], op=ALU.add),
                    [BK(b), ("xT", j, ti)], [("xT", j, ti)])
            rms_tile(ti, t)

    def proj_phase(br, l, tiles):
        for jb in range(2):
            Wg, wgk = next_block()
            Wp, wpk = next_block(False)
            for cc in range(4):
                j = jb * 4 + cc
                for ti, t in enumerate(tiles):
                    off, n = t["off"], t["n"]
                    bg = grp(Wg, wgk, cc * 128, ti, t)
                    gt, gtk = tf()
                    act(lambda e, bg=bg, gt=gt, j=j, n=n: e.activation(out=gt[:, 0:n], in_=banks[bg][:, 0:n], func=AF.Sigmoid,
                                                                     bias=colv[:, j, R_BG + br:R_BG + br + 1], scale=1.0),
                        [BK(bg), ("colv",)], [gtk])
                    bp = grp(Wp, wpk, cc * 128, ti, t, srcbuf=S2, srckey="s2")
                    if br == 0:
                        dve(lambda e, bp=bp, gt=gt, j=j, off=off, n=n: e.tensor_tensor(out=mview[:, j, off:off + n], in0=banks[bp][:, 0:n], in1=gt[:, 0:n], op=ALU.mult),
                            [BK(bp), gtk], [("m", j, ti)])
                    else:
                        dve(lambda e, bp=bp, gt=gt, n=n: e.tensor_tensor(out=gt[:, 0:n], in0=banks[bp][:, 0:n], in1=gt[:, 0:n], op=ALU.mult), [BK(bp), gtk], [gtk])
                        dve(lambda e, gt=gt, j=j, off=off, n=n: e.tensor_tensor(out=mview[:, j, off:off + n], in0=gt[:, 0:n], in1=mview[:, j, off:off + n], op=ALU.add),
                            [gtk, ("m", j, ti)], [("m", j, ti)])

    def emit_rows_out(slab, nrows, keys, dst, nrows_dst, sample_a_layer=None):
        so, sok = sg()
        for h in range(2):
            b = P.bank()
            pe([tp(banks[b][0:nrows, cc * 128:(cc + 1) * 128], slab[:, h * 4 + cc, 0:nrows], ident[:, :]) for cc in range(4)],
               keys + [("ident",)], [BK(b)])
            act(lambda e, b=b, h=h, so=so: e.activation(out=so[0:nrows, h * 512:(h + 1) * 512], in_=banks[b][0:nrows, :], func=AF.Copy), [BK(b)], [sok])
        if sample_a_layer is None:
            out_evs.append(P.dma("sp", "o_" + sok[0] + str(sok[1]), dst, so[0:nrows, :], reads=[sok]))
        else:
            l = sample_a_layer
            for s_i in range(NS):
                out_evs.append(P.dma("sp", "o_" + sok[0] + str(sok[1]), nas_d.ap()[l, s_i, KA - 1 - ST:KA - 1, :], so[s_i * ST:(s_i + 1) * ST, :], reads=[sok]))

    def final_out(gi, tiles):
        for ti, t in enumerate(tiles):
            off, n = t["off"], t["n"]
            for c in range(NCH):
                dve(lambda e, c=c, off=off, n=n: e.scalar_tensor_tensor(
                    out=xT[:, c, off:off + n], in0=xT[:, c, off:off + n], scalar=colv[:, c, R_FING:R_FING + 1],
                    in1=stats[:, 0, off:off + n], op0=ALU.mult, op1=ALU.mult),
                    [("xT", c, ti), ("colv",), ("st", 0, ti)], [("xT", c, ti)])
            nblk = max(1, n // 128)
            for tbi in range(nblk):
                m_ = min(128, n)
                tok0 = off + tbi * 128
                oi = P.rotate("ost", 4)
                so = mb_f32[:, oi, :]
                sok = ("mbstg", oi)
                for h in range(2):
                    b = P.bank()
                    pe([tp(banks[b][0:m_, cc * 128:(cc + 1) * 128], xT[:, h * 4 + cc, tok0:tok0 + m_], ident[:, :]) for cc in range(4)],
                       [("xT", h * 4 + cc, ti) for cc in range(4)] + [("ident",)], [BK(b)])
                    act(lambda e, b=b, h=h, so=so, m_=m_: e.activation(out=so[0:m_, h * 512:(h + 1) * 512], in_=banks[b][0:m_, :], func=AF.Copy),
                        [BK(b)], [sok] + (MB_ALL if (h == 0 and tbi == 0 and ti == 0) else []))
                if t["kind"] == "p":
                    dst = yp_d.ap()[t["seq0"] + tbi * 128:t["seq0"] + (tbi + 1) * 128, :]
                else:
                    dst = ys_d.ap()[:, :]
                out_evs.append(P.dma("sp", "o_mb%d" % oi, dst, so[0:m_, :], reads=[sok]))

    class _Stop(Exception):
        pass

    def phase_mark():
        return None

    issue_weights(NRING)
    try:
        for gi, tiles in enumerate(groups):
            load_group(gi, tiles)
            rms_stats(tiles)
            phase_mark()
            for l in range(depth):
                run_pass(gi, l, tiles)
            final_out(gi, tiles)
    except _Stop:
        pass
    P.wait_all("sp", out_evs)
    P.emit()
    es.close()
    return nc


_NC_CACHE = {}


def kernel(x_prompt, x_sample, state_conv_a, state_conv_c, norm_g, w_in, b_gate,
           w_conv_a, b_conv_a, ln_a_g, ln_a_b, w_proj_a, ln_b_g, ln_b_b, w_s, b_s,
           w_proj_b, w_conv_c, w_proj_c, w_out, final_g):
    f = lambda a: np.ascontiguousarray(np.asarray(a, dtype=np.float32))
    x_prompt, x_sample, state_conv_a, state_conv_c = f(x_prompt), f(x_sample), f(state_conv_a), f(state_conv_c)
    w_in, w_proj_a, w_proj_b, w_proj_c, w_out = f(w_in), f(w_proj_a), f(w_proj_b), f(w_proj_c), f(w_out)
    vecs = np.zeros((DEPTH, NR, D), dtype=np.float32)
    vecs[:, R_NORMG] = f(norm_g)
    vecs[:, R_BG:R_BG + 3] = f(b_gate).reshape(DEPTH, 3, D)
    vecs[:, R_WCA:R_WCA + KA] = f(w_conv_a)
    vecs[:, R_BCA] = f(b_conv_a)
    vecs[:, R_LNAG] = f(ln_a_g)
    vecs[:, R_LNAB] = f(ln_a_b)
    vecs[:, R_LNBG] = f(ln_b_g)
    vecs[:, R_LNBB] = f(ln_b_b)
    vecs[:, R_WCC:R_WCC + KC] = f(w_conv_c)
    vecs[:, R_FING] = f(final_g)[None, :]
    w_s_ = f(w_s)
    wpad = np.zeros((DEPTH, 32, D), dtype=np.float32)
    wpad[:, :KA] = f(w_conv_a)
    wst = np.ascontiguousarray(wpad.reshape(DEPTH, 8, 4, NCH, 4, 32).transpose(0, 2, 5, 3, 1, 4).reshape(DEPTH, 128, 256))
    b_s_ = f(b_s).reshape(DEPTH, NCH * 128)

    if "nc" not in _NC_CACHE:
        _NC_CACHE["nc"] = build()
    nc = _NC_CACHE["nc"]
    in_maps = []
    for b in range(8):
        in_maps.append({
            "xp": x_prompt[b],
            "xs": np.ascontiguousarray(x_sample[b * NS:(b + 1) * NS].reshape(NSTOK, D)),
            "sca": np.ascontiguousarray(state_conv_a[:, b * NS:(b + 1) * NS]),
            "scc": np.ascontiguousarray(state_conv_c[:, b * NS:(b + 1) * NS]),
            "w_in": w_in, "w_proj_a": w_proj_a, "w_proj_b": w_proj_b, "w_proj_c": w_proj_c, "w_out": w_out,
            "vecs": vecs, "w_s": w_s_, "b_s": b_s_, "wst": wst,
        })
    res = run_bass_kernel_spmd(nc, in_maps, core_ids=list(range(8)))
    R = res.results
    y_prompt = np.stack([R[b]["yp"] for b in range(8)], axis=0)
    y_sample = np.concatenate([R[b]["ys"].reshape(NS, ST, D) for b in range(8)], axis=0)
    nap = np.stack([R[b]["nap"] for b in range(8)], axis=1)
    nas = np.concatenate([R[b]["nas"] for b in range(8)], axis=1)
    ncp = np.stack([R[b]["ncp"] for b in range(8)], axis=1)
    ncs = np.concatenate([R[b]["ncs"].reshape(DEPTH, NS, KC - 1, D) for b in range(8)], axis=1)
    nvs = np.concatenate([R[b]["nvs"].reshape(DEPTH, NS, ST, D) for b in range(8)], axis=1)
    return (y_prompt.astype(np.float32), y_sample.astype(np.float32), nap.astype(np.float32), nas.astype(np.float32),
            ncp.astype(np.float32), ncs.astype(np.float32), nvs.astype(np.float32))
```
